# Optimizing a Trainium2 kernel written in Bass

```python
import math
import jax, jax.numpy as jnp
from jax import lax
import numpy as np

D_MODEL = 1024
BATCH = 8
SEQ = 2048
DEPTH = 4
DEC_BATCH = 32
DEC_SEQ = 2048
PAST_LEN = 128

N_MIXERS = 3
DIFF_HEAD_DIM = 64
DIFF_HEADS = D_MODEL // (2 * DIFF_HEAD_DIM)
ROPE_THETA = 10000.0
Q_BLOCK = 128
CONV_WIDTH = 31
CONV_PAD = (CONV_WIDTH - 1) // 2
CHUNK = 128
SG_GROUPS = 8
SG_GROUP_DIM = D_MODEL // SG_GROUPS
D_FF = ((8 * D_MODEL // 3 + 255) // 256) * 256
NORM_EPS = 1e-6
LN_EPS = 1e-5

kernel_name = "hybrid_diffattn_conformer_sgmlp_encoder"


def rmsnorm(x, g, eps=NORM_EPS):
    xf = x.astype(jnp.float32)
    y = xf * lax.rsqrt(jnp.mean(xf * xf, axis=-1, keepdims=True) + eps)
    return (y * g.astype(jnp.float32)).astype(x.dtype)


def layernorm(x, g, b, eps=LN_EPS):
    xf = x.astype(jnp.float32)
    mu = jnp.mean(xf, axis=-1, keepdims=True)
    var = jnp.mean(jnp.square(xf - mu), axis=-1, keepdims=True)
    y = (xf - mu) * lax.rsqrt(var + eps)
    return (y * g.astype(jnp.float32) + b.astype(jnp.float32)).astype(x.dtype)


def rope_tables(seq_len, dim):
    pos = jnp.arange(seq_len, dtype=jnp.float32)
    inv_freq = 1.0 / (ROPE_THETA ** (jnp.arange(0, dim, 2, dtype=jnp.float32) / dim))
    ang = pos[:, None] * inv_freq[None, :]
    return jnp.cos(ang), jnp.sin(ang)


def apply_rope(x, cos, sin):
    x1, x2 = jnp.split(x.astype(jnp.float32), 2, axis=-1)
    c = cos[None, :, None, None, :]
    s = sin[None, :, None, None, :]
    return jnp.concatenate([x1 * c - x2 * s, x1 * s + x2 * c], axis=-1).astype(x.dtype)


def lambda_init_fn(layer_idx):
    return 0.8 - 0.6 * math.exp(-0.3 * layer_idx)


def diff_attention(h, w_qkv, lam_q1, lam_k1, lam_q2, lam_k2, g_subln, w_o, lambda_init):
    b, s, _ = h.shape
    q, k, v = jnp.split(h @ w_qkv, 3, axis=-1)
    q = q.reshape(b, s, DIFF_HEADS, 2, DIFF_HEAD_DIM)
    k = k.reshape(b, s, DIFF_HEADS, 2, DIFF_HEAD_DIM)
    v = v.reshape(b, s, DIFF_HEADS, 2 * DIFF_HEAD_DIM)
    cos, sin = rope_tables(s, DIFF_HEAD_DIM)
    q = apply_rope(q, cos, sin) * (DIFF_HEAD_DIM ** -0.5)
    k = apply_rope(k, cos, sin)
    lam = (jnp.exp(jnp.sum(lam_q1.astype(jnp.float32) * lam_k1.astype(jnp.float32)))
           - jnp.exp(jnp.sum(lam_q2.astype(jnp.float32) * lam_k2.astype(jnp.float32)))
           + lambda_init)
    nb = s // Q_BLOCK
    qb = q.reshape(b, nb, Q_BLOCK, DIFF_HEADS, 2, DIFF_HEAD_DIM).transpose(1, 0, 2, 3, 4, 5)

    def block(q_blk):
        sc = jnp.einsum('bqhcd,bkhcd->bhcqk', q_blk, k).astype(jnp.float32)
        p = jax.nn.softmax(sc, axis=-1)
        a = p[:, :, 0] - lam * p[:, :, 1]
        return jnp.einsum('bhqk,bkhe->bqhe', a.astype(v.dtype), v)

    o = lax.map(block, qb)
    o = o.transpose(1, 0, 2, 3, 4).reshape(b, s, DIFF_HEADS, 2 * DIFF_HEAD_DIM)
    o = rmsnorm(o, g_subln, eps=LN_EPS) * (1.0 - lambda_init)
    return o.reshape(b, s, D_MODEL) @ w_o


def conformer_conv(h, w_pw1, b_pw1, w_dw, b_dw, g_cln, b_cln, w_pw2, b_pw2):
    a, g = jnp.split(h @ w_pw1 + b_pw1, 2, axis=-1)
    z = a * jax.nn.sigmoid(g)
    z = lax.conv_general_dilated(
        z, w_dw[:, None, :], window_strides=(1,), padding=[(CONV_PAD, CONV_PAD)],
        dimension_numbers=('NWC', 'WIO', 'NWC'), feature_group_count=D_MODEL) + b_dw
    z = jax.nn.silu(layernorm(z, g_cln, b_cln))
    return z @ w_pw2 + b_pw2


def spatial_gating(h, w_uv, b_uv, g_sln, b_sln, w_s, b_s, w_o, b_o):
    b, s, _ = h.shape
    u, v = jnp.split(jax.nn.gelu(h @ w_uv + b_uv), 2, axis=-1)
    v = layernorm(v, g_sln, b_sln)
    vc = v.reshape(b, s // CHUNK, CHUNK, SG_GROUPS, SG_GROUP_DIM)
    mixed = jnp.einsum('gpq,bcqgd->bcpgd', w_s, vc) + b_s.T[:, :, None]
    return (u * mixed.reshape(b, s, D_MODEL)) @ w_o + b_o


def swiglu(h, w_gate_up, w_down):
    g, u = jnp.split(h @ w_gate_up, 2, axis=-1)
    return (jax.nn.silu(g) * u) @ w_down


def setup_inputs(seed: int = 0) -> dict:
    key = jax.random.key(seed)
    keys = iter(jax.random.split(key, 512))

    def nrm(shape, scale):
        return scale * jax.random.normal(next(keys), shape, jnp.float32)

    def gain(shape):
        return 1.0 + nrm(shape, 0.02)

    D = D_MODEL
    p = {}
    p["x_prompt"] = nrm((BATCH, SEQ, D), 1.0)
    p["x_sample"] = nrm((DEC_BATCH, DEC_SEQ, D), 1.0)
    for i in range(DEPTH):
        n = f"l{i}_"
        p[n + "g_mix_pre"] = gain((D,))
        p[n + "g_mix_post"] = gain((D,))
        kind = i % N_MIXERS
        if kind == 0:
            p[n + "w_qkv"] = nrm((D, 3 * D), D ** -0.5)
            p[n + "lam_q1"] = nrm((DIFF_HEAD_DIM,), 0.1)
            p[n + "lam_k1"] = nrm((DIFF_HEAD_DIM,), 0.1)
            p[n + "lam_q2"] = nrm((DIFF_HEAD_DIM,), 0.1)
            p[n + "lam_k2"] = nrm((DIFF_HEAD_DIM,), 0.1)
            p[n + "g_subln"] = gain((2 * DIFF_HEAD_DIM,))
            p[n + "w_o"] = nrm((D, D), D ** -0.5)
        elif kind == 1:
            p[n + "w_pw1"] = nrm((D, 2 * D), D ** -0.5)
            p[n + "b_pw1"] = nrm((2 * D,), 0.02)
            p[n + "w_dw"] = nrm((CONV_WIDTH, D), CONV_WIDTH ** -0.5)
            p[n + "b_dw"] = nrm((D,), 0.02)
            p[n + "g_cln"] = gain((D,))
            p[n + "b_cln"] = nrm((D,), 0.02)
            p[n + "w_pw2"] = nrm((D, D), D ** -0.5)
            p[n + "b_pw2"] = nrm((D,), 0.02)
        else:
            p[n + "w_uv"] = nrm((D, 2 * D), D ** -0.5)
            p[n + "b_uv"] = nrm((2 * D,), 0.02)
            p[n + "g_sln"] = gain((D,))
            p[n + "b_sln"] = nrm((D,), 0.02)
            p[n + "w_s"] = nrm((SG_GROUPS, CHUNK, CHUNK), CHUNK ** -0.5)
            p[n + "b_s"] = gain((SG_GROUPS, CHUNK))
            p[n + "w_o"] = nrm((D, D), D ** -0.5)
            p[n + "b_o"] = nrm((D,), 0.02)
        p[n + "g_ffn_pre"] = gain((D,))
        p[n + "g_ffn_post"] = gain((D,))
        p[n + "w_gate_up"] = nrm((D, 2 * D_FF), D ** -0.5)
        p[n + "w_down"] = nrm((D_FF, D), D_FF ** -0.5)
    return p


def reference(x_prompt, x_sample,
              l0_g_mix_pre, l0_g_mix_post, l0_w_qkv, l0_lam_q1, l0_lam_k1, l0_lam_q2, l0_lam_k2, l0_g_subln, l0_w_o,
              l0_g_ffn_pre, l0_g_ffn_post, l0_w_gate_up, l0_w_down,
              l1_g_mix_pre, l1_g_mix_post, l1_w_pw1, l1_b_pw1, l1_w_dw, l1_b_dw, l1_g_cln, l1_b_cln, l1_w_pw2, l1_b_pw2,
              l1_g_ffn_pre, l1_g_ffn_post, l1_w_gate_up, l1_w_down,
              l2_g_mix_pre, l2_g_mix_post, l2_w_uv, l2_b_uv, l2_g_sln, l2_b_sln, l2_w_s, l2_b_s, l2_w_o, l2_b_o,
              l2_g_ffn_pre, l2_g_ffn_post, l2_w_gate_up, l2_w_down,
              l3_g_mix_pre, l3_g_mix_post, l3_w_qkv, l3_lam_q1, l3_lam_k1, l3_lam_q2, l3_lam_k2, l3_g_subln, l3_w_o,
              l3_g_ffn_pre, l3_g_ffn_post, l3_w_gate_up, l3_w_down):
    mixer_params = [
        (l0_w_qkv, l0_lam_q1, l0_lam_k1, l0_lam_q2, l0_lam_k2, l0_g_subln, l0_w_o),
        (l1_w_pw1, l1_b_pw1, l1_w_dw, l1_b_dw, l1_g_cln, l1_b_cln, l1_w_pw2, l1_b_pw2),
        (l2_w_uv, l2_b_uv, l2_g_sln, l2_b_sln, l2_w_s, l2_b_s, l2_w_o, l2_b_o),
        (l3_w_qkv, l3_lam_q1, l3_lam_k1, l3_lam_q2, l3_lam_k2, l3_g_subln, l3_w_o),
    ]
    norm_params = [
        (l0_g_mix_pre, l0_g_mix_post, l0_g_ffn_pre, l0_g_ffn_post),
        (l1_g_mix_pre, l1_g_mix_post, l1_g_ffn_pre, l1_g_ffn_post),
        (l2_g_mix_pre, l2_g_mix_post, l2_g_ffn_pre, l2_g_ffn_post),
        (l3_g_mix_pre, l3_g_mix_post, l3_g_ffn_pre, l3_g_ffn_post),
    ]
    ffn_params = [
        (l0_w_gate_up, l0_w_down),
        (l1_w_gate_up, l1_w_down),
        (l2_w_gate_up, l2_w_down),
        (l3_w_gate_up, l3_w_down),
    ]

    def trunk(x):
        for i in range(DEPTH):
            g_mix_pre, g_mix_post, g_ffn_pre, g_ffn_post = norm_params[i]
            h = rmsnorm(x, g_mix_pre)
            kind = i % N_MIXERS
            if kind == 0:
                m = diff_attention(h, *mixer_params[i], lambda_init=lambda_init_fn(i))
            elif kind == 1:
                m = conformer_conv(h, *mixer_params[i])
            else:
                m = spatial_gating(h, *mixer_params[i])
            x = x + rmsnorm(m, g_mix_post)
            f = swiglu(rmsnorm(x, g_ffn_pre), *ffn_params[i])
            x = x + rmsnorm(f, g_ffn_post)
        return x

    y_prompt = trunk(x_prompt)
    y_sample = trunk(x_sample)
    return (y_prompt, y_sample)
```

```python
import math
import os
from contextlib import ExitStack
import numpy as np
import concourse.bass as bass
import concourse.mybir as mybir
from concourse.bass_utils import run_bass_kernel_spmd

F32 = mybir.dt.float32
BF16 = mybir.dt.bfloat16
AF = mybir.ActivationFunctionType
ALU = mybir.AluOpType
AX = mybir.AxisListType

D = 1024
S = 2048
KC = 8
DFF = 2816
FC = 22
NTB = 4
DEPTH = 4
NCORES = 8
SEQ_PER_CORE = 5
GR = 256
EPOCH = 12000
SBUF_BASE = 16640
SBUF_LIMIT = 229312


class Op:
    __slots__ = ("eng", "fn", "deps", "is_dma", "dsem", "token", "idx")


class Prog:
    ENGS = ("pe", "act", "dve", "pool", "sp")

    def __init__(self, nc, es):
        self.nc = nc
        self.es = es
        self.ops = []
        self.last_w = {}
        self.readers = {}
        self.dma_sems = {}
        self.nsem = 0
        self.dry = False

    def _sem(self, name):
        self.nsem += 1
        return self.es.enter_context(self.nc.semaphore(name))

    def add(self, eng, fn, reads=(), writes=(), dma=None):
        if self.dry:
            return None
        op = Op()
        op.eng = eng
        op.fn = fn
        op.is_dma = dma is not None
        op.dsem = dma
        op.token = None
        op.idx = len(self.ops)
        deps = set()
        ops = self.ops
        for r in reads:
            w = self.last_w.get(r)
            if w is not None:
                wo = ops[w]
                if wo.is_dma or wo.eng != eng or eng not in ("pe",):
                    deps.add(w)
        for r in writes:
            w = self.last_w.get(r)
            if w is not None:
                wo = ops[w]
                if wo.is_dma or wo.eng != eng or eng not in ("pe",):
                    deps.add(w)
            rd = self.readers.get(r)
            if rd:
                for k, ri in rd.items():
                    ro = ops[ri]
                    if ro.is_dma or ro.eng != eng:
                        deps.add(ri)
        op.deps = deps
        for r in writes:
            self.last_w[r] = op.idx
            self.readers[r] = {}
        for r in reads:
            d = self.readers.get(r)
            if d is None:
                d = self.readers[r] = {}
            if op.is_dma:
                d[("dma", op.idx)] = op.idx
            else:
                d[eng] = op.idx
        self.ops.append(op)
        return op

    def barrier(self):
        if self.dry:
            return
        keys = set(self.last_w.keys()) | set(self.readers.keys())
        self.add("dve", (lambda v: v.engine_nop()), reads=(), writes=list(keys))

    def emit(self):
        nc = self.nc
        ops = self.ops
        needed = set()
        for op in ops:
            needed |= op.deps
        esem = {}
        cnt = {e: 0 for e in self.ENGS}
        for op in ops:
            if op.is_dma:
                ent = self.dma_sems.get(op.dsem)
                if ent is None:
                    ent = self.dma_sems[op.dsem] = [self._sem("d_" + op.dsem), 0]
                ent[1] += 16
                op.token = (ent[0], ent[1], ("d", op.dsem))
            elif op.idx in needed:
                e = op.eng
                ep = cnt[e] // EPOCH
                val = cnt[e] % EPOCH + 1
                cnt[e] += 1
                key = (e, ep)
                if key not in esem:
                    esem[key] = self._sem("s_%s_%d" % (e, ep))
                op.token = (esem[key], val, key)
        for op in ops:
            if op.token is not None and not op.is_dma:
                pass
        block = self.es.enter_context(nc.Block())
        per_eng = {e: [o for o in ops if o.eng == e] for e in self.ENGS}

        def run(engname, eng):
            waited = {}
            for op in per_eng[engname]:
                ws = {}
                for d in op.deps:
                    sem, val, key = ops[d].token
                    if waited.get(key, 0) >= val:
                        continue
                    if ws.get(key, (None, 0))[1] < val:
                        ws[key] = (sem, val)
                for key, (sem, val) in ws.items():
                    eng.wait_ge(sem, val)
                    waited[key] = val
                    if key[0] != "d":
                        for k2 in list(esem.keys()):
                            if k2[0] == key[0] and k2[1] < key[1]:
                                waited[k2] = EPOCH * 4
                ins = op.fn(eng)
                if op.token is not None:
                    if op.is_dma:
                        ins.then_inc(op.token[0], 16)
                    else:
                        ins.then_inc(op.token[0], 1)

        @block.tensor
        def _(t):
            run("pe", t)

        @block.scalar
        def _(a):
            run("act", a)

        @block.vector
        def _(v):
            run("dve", v)

        @block.gpsimd
        def _(g):
            run("pool", g)

        @block.sync
        def _(s):
            run("sp", s)


class SB:
    def __init__(self, nc, name, shape, dtype, off):
        self.t = nc.alloc_sbuf_tensor_at(name, [128] + list(shape), dtype, offset=off)
        self.off = off
        self.esz = 2 if dtype == BF16 else 4
        n = 1
        for s in shape:
            n *= s
        self.nbytes = n * self.esz
        assert off + self.nbytes <= SBUF_LIMIT, (name, off, self.nbytes)
        self.shape = shape

    def r(self, lo=0, hi=None):
        if hi is None:
            hi = self.nbytes // self.esz
        a = (self.off + lo * self.esz) // GR
        b = (self.off + hi * self.esz - 1) // GR
        return list(range(a, b + 1))

    def end(self):
        return self.off + self.nbytes


def PSR(*banks):
    return [("ps", b) for b in banks]


def fm(v):
    v = np.asarray(v, np.float32)
    return np.ascontiguousarray(v.reshape(-1, 128).T)


def pack_params(inp):
    cols = []
    index = {}

    def put(name, arr):
        arr = np.asarray(arr, np.float32)
        assert arr.shape[0] == 128
        index[name] = (sum(c.shape[1] for c in cols), arr.shape[1])
        cols.append(arr)

    for l in range(DEPTH):
        n = "l%d_" % l
        for g in ("g_mix_pre", "g_mix_post", "g_ffn_pre", "g_ffn_post"):
            put(n + g, fm(inp[n + g]))
        kind = l % 3
        if kind == 0:
            for g in ("lam_q1", "lam_k1", "lam_q2", "lam_k2"):
                put(n + g, np.broadcast_to(np.asarray(inp[n + g], np.float32)[None, :], (128, 64)))
            put(n + "g_subln", fm(inp[n + "g_subln"]))
        elif kind == 1:
            put(n + "b_pw1", fm(inp[n + "b_pw1"]))
            wdw = np.asarray(inp[n + "w_dw"], np.float32)
            put(n + "w_dw", np.ascontiguousarray(wdw.reshape(31, 8, 128).transpose(2, 1, 0)).reshape(128, 8 * 31))
            for g in ("b_dw", "g_cln", "b_cln", "b_pw2"):
                put(n + g, fm(inp[n + g]))
        else:
            put(n + "b_uv", fm(inp[n + "b_uv"]))
            for g in ("g_sln", "b_sln", "b_o"):
                put(n + g, fm(inp[n + g]))
    arr = np.ascontiguousarray(np.concatenate(cols, axis=1))
    return arr, index


def rope_tables():
    pos = np.arange(S, dtype=np.float32)
    inv = (1.0 / (10000.0 ** (np.arange(0, 64, 2, dtype=np.float32) / 64))).astype(np.float32)
    ang = pos[None, :] * inv[:, None]
    c = np.cos(ang).astype(np.float32)
    s = np.sin(ang).astype(np.float32)
    cosT = np.zeros((128, S), np.float32)
    sinT = np.zeros((128, S), np.float32)
    for comp in range(2):
        for half in range(2):
            base = comp * 64 + half * 32
            cosT[base:base + 32] = c
            sinT[base:base + 32] = -s if half == 0 else s
    return cosT, sinT


def const_inputs():
    perm = np.zeros((128, 128), np.float32)
    for p in range(128):
        perm[p ^ 32, p] = 1.0
    cosT, sinT = rope_tables()
    return {
        "c_ident": np.eye(128, dtype=np.float32),
        "c_perm": perm,
        "c_cos": cosT,
        "c_sin": sinT,
    }


_PARAM_INDEX = None


def param_index():
    global _PARAM_INDEX
    if _PARAM_INDEX is None:
        fake = {}
        for l in range(DEPTH):
            n = "l%d_" % l
            for g in ("g_mix_pre", "g_mix_post", "g_ffn_pre", "g_ffn_post"):
                fake[n + g] = np.zeros(D, np.float32)
            kind = l % 3
            if kind == 0:
                for g in ("lam_q1", "lam_k1", "lam_q2", "lam_k2"):
                    fake[n + g] = np.zeros(64, np.float32)
                fake[n + "g_subln"] = np.zeros(128, np.float32)
            elif kind == 1:
                fake[n + "b_pw1"] = np.zeros(2 * D, np.float32)
                fake[n + "w_dw"] = np.zeros((31, D), np.float32)
                for g in ("b_dw", "g_cln", "b_cln", "b_pw2"):
                    fake[n + g] = np.zeros(D, np.float32)
            else:
                fake[n + "b_uv"] = np.zeros(2 * D, np.float32)
                for g in ("g_sln", "b_sln", "b_o"):
                    fake[n + g] = np.zeros(D, np.float32)
        arr, idx = pack_params(fake)
        _PARAM_INDEX = (arr.shape[1], idx)
    return _PARAM_INDEX


class Builder:
    def __init__(self, nseq=SEQ_PER_CORE, layers=(0, 1, 2, 3), do_ffn=True, do_mix=True):
        self.nseq = nseq
        self.layers = layers
        self.do_ffn = do_ffn
        self.do_mix = do_mix
        self.nc = bass.Bass("TRN2", target_bir_lowering=False)
        self.es = ExitStack()
        self.P = Prog(self.nc, self.es)
        self.npar, self.pidx = param_index()
        self.plan_mode = False
        self.plan = []
        self.ws_issued = 0

    def declare_dram(self):
        nc = self.nc
        dr = {}
        dr["x"] = nc.dram_tensor("x", [self.nseq, S, D], F32, kind="ExternalInput").ap()
        dr["y"] = nc.dram_tensor("y", [self.nseq, S, D], F32, kind="ExternalOutput").ap()
        dr["params"] = nc.dram_tensor("params", [128, self.npar], F32, kind="ExternalInput").ap()
        dr["c_ident"] = nc.dram_tensor("c_ident", [128, 128], F32, kind="ExternalInput").ap()
        dr["c_perm"] = nc.dram_tensor("c_perm", [128, 128], F32, kind="ExternalInput").ap()
        dr["c_cos"] = nc.dram_tensor("c_cos", [128, S], F32, kind="ExternalInput").ap()
        dr["c_sin"] = nc.dram_tensor("c_sin", [128, S], F32, kind="ExternalInput").ap()
        for l in range(DEPTH):
            n = "l%d_" % l
            kind = l % 3
            if kind == 0:
                dr[n + "w_qkv"] = nc.dram_tensor(n + "w_qkv", [D, 3 * D], F32, kind="ExternalInput").ap()
                dr[n + "w_o"] = nc.dram_tensor(n + "w_o", [D, D], F32, kind="ExternalInput").ap()
            elif kind == 1:
                dr[n + "w_pw1"] = nc.dram_tensor(n + "w_pw1", [D, 2 * D], F32, kind="ExternalInput").ap()
                dr[n + "w_pw2"] = nc.dram_tensor(n + "w_pw2", [D, D], F32, kind="ExternalInput").ap()
            else:
                dr[n + "w_uv"] = nc.dram_tensor(n + "w_uv", [D, 2 * D], F32, kind="ExternalInput").ap()
                dr[n + "w_s"] = nc.dram_tensor(n + "w_s", [8, 128, 128], F32, kind="ExternalInput").ap()
                dr[n + "b_s"] = nc.dram_tensor(n + "b_s", [8, 128], F32, kind="ExternalInput").ap()
                dr[n + "w_o"] = nc.dram_tensor(n + "w_o", [D, D], F32, kind="ExternalInput").ap()
            dr[n + "w_gate_up"] = nc.dram_tensor(n + "w_gate_up", [D, 2 * DFF], F32, kind="ExternalInput").ap()
            dr[n + "w_down"] = nc.dram_tensor(n + "w_down", [DFF, D], F32, kind="ExternalInput").ap()
        self.dr = dr

    def alloc_persistent(self):
        nc = self.nc
        off = SBUF_BASE

        def A(name, shape, dt):
            nonlocal off
            b = SB(nc, name, shape, dt, off)
            off = (b.end() + GR - 1) // GR * GR
            return b

        self.XT = A("XT", [KC, S], F32)
        self.PAR = A("PAR", [self.npar], F32)
        self.identf = A("identf", [128], F32)
        self.identb = A("identb", [128], BF16)
        self.permb = A("permb", [128], BF16)
        self.onesb = A("onesb", [128], BF16)
        self.stat = A("stat", [64], F32)
        self.epst = A("epst", [16], F32)
        self.WS = [A("ws%d" % i, [4096], BF16) for i in range(2)]
        self.big0 = off
        self.ps = nc.alloc_psum_tensor("ps", [128, 4096], F32)
        self.ws_count = 0

    def bank(self, b, n=512, o=0):
        return self.ps[:, b * 512 + o: b * 512 + o + n]

    def par(self, name, c0=0, n=1):
        o, w = self.pidx[name]
        return self.PAR.t[:, o + c0: o + c0 + n]

    def par_r(self):
        return self.PAR.r()

    def wload(self, pieces):
        idx = self.ws_count
        self.ws_count += 1
        slot = self.WS[idx % len(self.WS)]
        if self.plan_mode:
            self.plan.append(pieces)
            return slot
        while self.ws_issued <= min(idx + 1, len(self.plan) - 1):
            j = self.ws_issued
            sl = self.WS[j % len(self.WS)]
            for ent in self.plan[j]:
                dst_fn, src_fn = ent[0], ent[1]
                wr = sl.r(*ent[2]) if len(ent) > 2 else sl.r()
                dst = dst_fn(sl.t)
                src = src_fn(self.dr)
                self.P.add("pool", (lambda g, d=dst, s_=src: g.dma_start(out=d, in_=s_)), reads=(), writes=wr, dma="ws%d" % (j % len(self.WS)))
            self.ws_issued += 1
        return slot

    def init_consts(self):
        P, dr = self.P, self.dr
        P.add("sp", lambda s: s.dma_start(out=self.PAR.t[:], in_=dr["params"]), writes=self.PAR.r(), dma="c0")
        P.add("sp", lambda s: s.dma_start(out=self.identf.t[:], in_=dr["c_ident"]), writes=self.identf.r(), dma="c1")
        P.add("pool", lambda g: g.dma_start(out=self.identb.t[:], in_=dr["c_ident"]), writes=self.identb.r(), dma="c2")
        P.add("pool", lambda g: g.dma_start(out=self.permb.t[:], in_=dr["c_perm"]), writes=self.permb.r(), dma="c3")
        P.add("pool", lambda g: g.memset(self.onesb.t[:], 1.0), writes=self.onesb.r())
        P.add("pool", lambda g: g.memset(self.epst.t[:, 0:1], 1e-6), writes=self.epst.r())
        P.add("pool", lambda g: g.memset(self.epst.t[:, 1:2], 1e-5), reads=self.epst.r(), writes=self.epst.r())

    def load_seq(self, s, stage):
        P = self.P
        XT = self.XT
        for t in range(16):
            st = t % 2
            src = self.dr["x"][s, t * 128:(t + 1) * 128, :]
            dstv = stage.t[:, st, :]
            P.add("sp", (lambda e, d=dstv, s_=src: e.dma_start(out=d, in_=s_)), writes=stage.r(st * 1024, (st + 1) * 1024), dma="ld%d" % st)
            for hb in range(2):
                bk = 6 + hb

                def tr(pe, st=st, hb=hb, bk=bk):
                    ins = None
                    for kk in range(4):
                        k = hb * 4 + kk
                        ins = pe.transpose(self.bank(bk, 128, kk * 128), stage.t[:, st, k * 128:(k + 1) * 128], self.identf.t[:])
                    return ins
                P.add("pe", tr, reads=stage.r(st * 1024 + hb * 512, st * 1024 + hb * 512 + 512) + self.identf.r(), writes=PSR(bk))
                dst = XT.t[:, hb * 4:(hb + 1) * 4, t * 128:(t + 1) * 128]
                src_ps = self.bank(bk).rearrange("p (k c) -> p k c", k=4)
                wr = []
                for kk in range(4):
                    k = hb * 4 + kk
                    wr += XT.r(k * S + t * 128, k * S + (t + 1) * 128)
                eng = "dve"
                if eng == "act":
                    P.add("act", (lambda a, d=dst, s_=src_ps: a.activation(out=d, in_=s_, func=AF.Copy)), reads=PSR(bk), writes=wr)
                else:
                    P.add("dve", (lambda v, d=dst, s_=src_ps: v.tensor_copy(out=d, in_=s_)), reads=PSR(bk), writes=wr)

    def store_seq(self, s, stage):
        P = self.P
        XT = self.XT
        for t in range(16):
            st = t % 2
            for hb in range(2):
                bk = 6 + hb

                def tr(pe, t=t, hb=hb, bk=bk):
                    ins = None
                    for kk in range(4):
                        k = hb * 4 + kk
                        ins = pe.transpose(self.bank(bk, 128, kk * 128), XT.t[:, k, t * 128:(t + 1) * 128], self.identf.t[:])
                    return ins
                rd = []
                for kk in range(4):
                    k = hb * 4 + kk
                    rd += XT.r(k * S + t * 128, k * S + (t + 1) * 128)
                P.add("pe", tr, reads=rd + self.identf.r(), writes=PSR(bk))
                dst = stage.t[:, st, hb * 512:(hb + 1) * 512]
                src_ps = self.bank(bk)
                wr = stage.r(st * 1024 + hb * 512, st * 1024 + hb * 512 + 512)
                if False:
                    P.add("act", (lambda a, d=dst, s_=src_ps: a.activation(out=d, in_=s_, func=AF.Copy)), reads=PSR(bk), writes=wr)
                else:
                    P.add("dve", (lambda v, d=dst, s_=src_ps: v.tensor_copy(out=d, in_=s_)), reads=PSR(bk), writes=wr)
            dst = self.dr["y"][s, t * 128:(t + 1) * 128, :]
            srcv = stage.t[:, st, :]
            P.add("sp", (lambda e, d=dst, s_=srcv: e.dma_start(out=d, in_=s_)), reads=stage.r(st * 1024, (st + 1) * 1024), dma="st%d" % st)

    def stats_begin(self):
        pass

    def rstd_from_sums(self, sum_bank, n, tmp, rstd, width, eps, o=0):
        P = self.P
        tb, to = tmp
        rb, ro = rstd
        P.add("act", (lambda a: a.activation(out=tb.t[:, to:to + n], in_=self.bank(sum_bank, n, o), func=AF.Sqrt, scale=1.0 / width, bias=self.epsb(eps))),
              reads=PSR(sum_bank) + self.epst.r(), writes=tb.r(to, to + n))
        P.add("dve", (lambda v: v.reciprocal(out=rb.t[:, ro:ro + n], in_=tb.t[:, to:to + n])),
              reads=tb.r(to, to + n), writes=rb.r(ro, ro + n))

    def epsb(self, eps):
        return self.epst.t[:, 0:1] if eps == 1e-6 else self.epst.t[:, 1:2]

    def prenorm_block(self, tb, gname, xnT, xn_off, sq, tmp, rstd, sbank):
        P = self.P
        XT = self.XT
        c0 = tb * 512
        for k in range(KC):
            sl = k % 2
            P.add("act", (lambda a, k=k, sl=sl: a.activation(out=sq.t[:, sl, :], in_=XT.t[:, k, c0:c0 + 512], func=AF.Square)),
                  reads=XT.r(k * S + c0, k * S + c0 + 512), writes=sq.r(sl * 512, sl * 512 + 512))
            P.add("pe", (lambda pe, k=k, sl=sl: pe.matmul(self.bank(sbank), lhsT=self.onesb.t[:], rhs=sq.t[:, sl, :], start=(k == 0), stop=(k == KC - 1))),
                  reads=sq.r(sl * 512, sl * 512 + 512) + self.onesb.r(), writes=PSR(sbank))
        self.rstd_from_sums(sbank, 512, (tmp, 0), (rstd, 0), float(D), 1e-6)
        xsh = xnT.shape[1]
        for k in range(KC):
            P.add("dve", (lambda v, k=k: v.scalar_tensor_tensor(out=xnT.t[:, k, xn_off:xn_off + 512], in0=XT.t[:, k, c0:c0 + 512], scalar=self.par(gname, k),
                                                                 in1=rstd.t[:, 0:512], op0=ALU.mult, op1=ALU.mult)),
                  reads=XT.r(k * S + c0, k * S + c0 + 512) + rstd.r(0, 512) + self.par_r(), writes=xnT.r(k * xsh + xn_off, k * xsh + xn_off + 512))

    def postnorm_block(self, tb, gname, fT, f_off, rstd, tscr):
        P = self.P
        XT = self.XT
        c0 = tb * 512
        fsh = fT.shape[1]
        for c in range(KC):
            sl = c % 2
            P.add("dve", (lambda v, c=c, sl=sl: v.scalar_tensor_tensor(out=tscr.t[:, sl, :], in0=fT.t[:, c, f_off:f_off + 512], scalar=self.par(gname, c),
                                                                        in1=rstd.t[:, 0:512], op0=ALU.mult, op1=ALU.mult)),
                  reads=fT.r(c * fsh + f_off, c * fsh + f_off + 512) + rstd.r(0, 512) + self.par_r(), writes=tscr.r(sl * 512, sl * 512 + 512))
            P.add("pool", (lambda g, c=c, sl=sl: g.tensor_tensor(out=XT.t[:, c, c0:c0 + 512], in0=XT.t[:, c, c0:c0 + 512], in1=tscr.t[:, sl, :], op=ALU.add)),
                  reads=tscr.r(sl * 512, sl * 512 + 512) + XT.r(c * S + c0, c * S + c0 + 512), writes=XT.r(c * S + c0, c * S + c0 + 512))

    def ffn(self, l):
        nc, P, dr = self.nc, self.P, self.dr
        n = "l%d_" % l
        off = self.big0

        def A(name, shape, dt):
            nonlocal off
            b = SB(nc, name + "_f%d_%d" % (l, self.uid()), shape, dt, off)
            off = (b.end() + GR - 1) // GR * GR
            return b

        xnT = A("xnT", [KC, 1024], BF16)
        hT = A("hT", [FC, 1024], BF16)
        fT = A("fT", [KC, 1024], F32)
        sq = A("sq", [2, 512], BF16)
        sg = A("sg", [2, 512], F32)
        tmp = A("tmp", [512], F32)
        rstd = A("rstd", [512], F32)
        rstd2 = [A("rstd2a", [512], F32), A("rstd2b", [512], F32)]
        tscr = A("tscr", [2, 512], F32)
        for half in range(2):
            for b in range(2):
                self.prenorm_block(half * 2 + b, n + "g_ffn_pre", xnT, b * 512, sq, tmp, rstd, 6 + b)
            cnt = 0
            for i in range(FC // 2):
                slot = self.wload([
                    (lambda t: t[:, 0:2048].rearrange("p (k c) -> p k c", k=8), lambda dr, i=i: dr[n + "w_gate_up"][:, i * 256:(i + 1) * 256].rearrange("(k p) c -> p k c", p=128)),
                    (lambda t: t[:, 2048:4096].rearrange("p (k c) -> p k c", k=8), lambda dr, i=i: dr[n + "w_gate_up"][:, DFF + i * 256:DFF + (i + 1) * 256].rearrange("(k p) c -> p k c", p=128)),
                ])
                for sub in range(2):
                    j = i * 2 + sub
                    st = cnt % 2
                    cnt += 1
                    gb = [st * 2 + 0, st * 2 + 1]
                    ub = [4 + (st * 2 + 0) % 2, 0]
                    ub = [4, 5]

                    def mm(pe, slot=slot, sub=sub, bb=gb, gi=0):
                        ins = None
                        wv = slot.t[:, :].rearrange("p (g k c) -> p g k c", g=2, k=8)
                        for k in range(KC):
                            for b in range(2):
                                ins = pe.matmul(self.bank(bb[b]), lhsT=wv[:, gi, k, sub * 128:(sub + 1) * 128], rhs=xnT.t[:, k, b * 512:(b + 1) * 512], start=(k == 0), stop=(k == KC - 1))
                        return ins
                    P.add("pe", mm, reads=slot.r() + xnT.r(), writes=PSR(gb[0], gb[1]))
                    P.add("pe", (lambda pe, slot=slot, sub=sub, ub=ub, mm=mm: mm(pe, slot, sub, ub, 1)), reads=slot.r() + xnT.r(), writes=PSR(ub[0], ub[1]))
                    for b in range(2):
                        P.add("act", (lambda a, b=b, gb=gb: a.activation(out=sg.t[:, b, :], in_=self.bank(gb[b]), func=AF.Silu)),
                              reads=PSR(gb[b]), writes=sg.r(b * 512, b * 512 + 512))
                        P.add("dve", (lambda v, b=b, ub=ub, j=j: v.tensor_tensor(out=hT.t[:, j, b * 512:(b + 1) * 512], in0=sg.t[:, b, :], in1=self.bank(ub[b]), op=ALU.mult)),
                              reads=PSR(ub[b]) + sg.r(b * 512, b * 512 + 512), writes=hT.r(j * 1024 + b * 512, j * 1024 + b * 512 + 512))
            for c in range(KC):
                slot = self.wload([
                    (lambda t: t[:, 0:FC * 128].rearrange("p (k c) -> p k c", k=FC), lambda dr, c=c: dr[n + "w_down"][:, c * 128:(c + 1) * 128].rearrange("(k p) c -> p k c", p=128)),
                ])
                fb = [(c % 2) * 2 + 0, (c % 2) * 2 + 1]

                def mm(pe, slot=slot, fb=fb):
                    ins = None
                    wv = slot.t[:, 0:FC * 128].rearrange("p (k c) -> p k c", k=FC)
                    for kf in range(FC):
                        for b in range(2):
                            ins = pe.matmul(self.bank(fb[b]), lhsT=wv[:, kf, :], rhs=hT.t[:, kf, b * 512:(b + 1) * 512], start=(kf == 0), stop=(kf == FC - 1))
                    return ins
                P.add("pe", mm, reads=slot.r() + hT.r(), writes=PSR(*fb))
                for b in range(2):
                    P.add("act", (lambda a, b=b, fb=fb, c=c: a.activation(out=fT.t[:, c, b * 512:(b + 1) * 512], in_=self.bank(fb[b]), func=AF.Copy)),
                          reads=PSR(fb[b]), writes=fT.r(c * 1024 + b * 512, c * 1024 + b * 512 + 512))
                    P.add("act", (lambda a, b=b, fb=fb: a.activation(out=sq.t[:, b, :], in_=self.bank(fb[b]), func=AF.Square)),
                          reads=PSR(fb[b]), writes=sq.r(b * 512, b * 512 + 512))
                    P.add("pe", (lambda pe, b=b, c=c: pe.matmul(self.bank(6 + b), lhsT=self.onesb.t[:], rhs=sq.t[:, b, :], start=(c == 0), stop=(c == KC - 1))),
                          reads=sq.r(b * 512, b * 512 + 512) + self.onesb.r(), writes=PSR(6 + b))
            for b in range(2):
                self.rstd_from_sums(6 + b, 512, (tmp, 0), (rstd2[b], 0), float(D), 1e-6)
                self.postnorm_block(half * 2 + b, n + "g_ffn_post", fT, b * 512, rstd2[b], tscr)

    _uid = 0

    def uid(self):
        Builder._uid += 1
        return Builder._uid


class SBview:
    def __init__(self, sb, b):
        self.sb = sb
        self.b = b
        self.t = sb.t[:, b, :]

    def r(self, lo=0, hi=512):
        return self.sb.r(self.b * 512 + lo, self.b * 512 + hi)


def _record(B, nseq, layers, do_ffn, do_mix):
    B.declare_dram()
    B.alloc_persistent()
    B.init_consts()
    stage = SB(B.nc, "stage%d" % B.uid(), [2, 1024], F32, B.big0)
    stage_l = SB(B.nc, "stagel%d" % B.uid(), [2, 1024], F32, B.big0 + 8192)
    for s in range(nseq):
        B.load_seq(s, stage_l)
        for l in layers:
            if do_mix:
                B.mixer(l)
            if do_ffn:
                B.ffn(l)
            B.P.barrier()
        B.store_seq(s, stage)
        B.P.barrier()


def build_program(nseq=SEQ_PER_CORE, layers=(0, 1, 2, 3), do_ffn=True, do_mix=True):
    B0 = Builder(nseq, layers, do_ffn, do_mix)
    B0.plan_mode = True
    B0.P.dry = True
    _record(B0, nseq, layers, do_ffn, do_mix)
    B = Builder(nseq, layers, do_ffn, do_mix)
    B.plan = B0.plan
    _record(B, nseq, layers, do_ffn, do_mix)
    B.P.emit()
    return B


def make_in_maps(inputs, nseq=SEQ_PER_CORE, ncores=NCORES):
    xp = np.asarray(inputs["x_prompt"], np.float32)
    xs = np.asarray(inputs["x_sample"], np.float32)
    params, _ = pack_params(inputs)
    consts = const_inputs()
    maps = []
    for c in range(ncores):
        xc = np.concatenate([xp[c:c + 1], xs[4 * c:4 * c + 4]], axis=0)[:nseq]
        m = {"x": np.ascontiguousarray(xc), "params": params}
        m.update(consts)
        for l in range(DEPTH):
            n = "l%d_" % l
            kind = l % 3
            names = ["w_gate_up", "w_down"]
            if kind == 0:
                names += ["w_qkv", "w_o"]
            elif kind == 1:
                names += ["w_pw1", "w_pw2"]
            else:
                names += ["w_uv", "w_s", "b_s", "w_o"]
            for nm in names:
                m[n + nm] = np.ascontiguousarray(np.asarray(inputs[n + nm], np.float32))
        maps.append(m)
    return maps


_CACHE = {}


def kernel(**inputs):
    if "B" not in _CACHE:
        _CACHE["B"] = build_program()
    B = _CACHE["B"]
    maps = make_in_maps(inputs)
    res = run_bass_kernel_spmd(B.nc, maps, core_ids=list(range(NCORES)))
    ys = [np.asarray(r["y"], np.float32) for r in res.results]
    y_prompt = np.stack([ys[c][0] for c in range(NCORES)], axis=0)
    y_sample = np.concatenate([ys[c][1:5] for c in range(NCORES)], axis=0)
    return (y_prompt, y_sample)


def _alloc(self, tag):
    nc = self.nc
    state = {"off": self.big0}

    def A(name, shape, dt):
        b = SB(nc, "%s_%s_%d" % (name, tag, self.uid()), shape, dt, state["off"])
        state["off"] = (b.end() + GR - 1) // GR * GR
        return b
    A.state = state
    return A


def _proj_feature_major(self, wname, ncols_total, col0, nchunks, inT, in_sh, tok0, nblk, banks, evac):
    P = self.P
    bi = 0
    for pi in range((nchunks + 3) // 4):
        nsub = min(4, nchunks - pi * 4)
        c0 = col0 + pi * 512
        slot = self.wload([
            (lambda t, nsub=nsub: t[:, 0:8 * nsub * 128].rearrange("p (k c) -> p k c", k=8),
             lambda dr, c0=c0, nsub=nsub: dr[wname][:, c0:c0 + nsub * 128].rearrange("(k p) c -> p k c", p=128)),
        ])
        for sub in range(nsub):
            oc = pi * 4 + sub
            for b in range(nblk):
                bk = banks[bi % len(banks)]
                bi += 1

                def mm(pe, slot=slot, sub=sub, b=b, bk=bk, nsub=nsub):
                    ins = None
                    wv = slot.t[:, 0:8 * nsub * 128].rearrange("p (k c) -> p k c", k=8)
                    for k in range(KC):
                        ins = pe.matmul(self.bank(bk), lhsT=wv[:, k, sub * 128:(sub + 1) * 128], rhs=inT.t[:, k, tok0 + b * 512: tok0 + (b + 1) * 512], start=(k == 0), stop=(k == KC - 1))
                    return ins
                rd = []
                for k in range(KC):
                    rd += inT.r(k * in_sh + tok0 + b * 512, k * in_sh + tok0 + (b + 1) * 512)
                P.add("pe", mm, reads=slot.r() + rd, writes=PSR(bk))
                evac(oc, b, bk)


def _out_proj_postnorm(self, wname, bias_name, gpost, inT, in_sh, half, mT, sq, tmp, rstd2, tscr):
    P = self.P

    def evac(oc, b, bk):
        if bias_name is not None:
            P.add("act", (lambda a: a.activation(out=mT.t[:, oc, b * 512:(b + 1) * 512], in_=self.bank(bk), func=AF.Identity, bias=self.par(bias_name, oc))),
                  reads=PSR(bk) + self.par_r(), writes=mT.r(oc * 1024 + b * 512, oc * 1024 + (b + 1) * 512))
        else:
            P.add("act", (lambda a: a.activation(out=mT.t[:, oc, b * 512:(b + 1) * 512], in_=self.bank(bk), func=AF.Copy)),
                  reads=PSR(bk), writes=mT.r(oc * 1024 + b * 512, oc * 1024 + (b + 1) * 512))
        P.add("act", (lambda a: a.activation(out=sq.t[:, b, :], in_=mT.t[:, oc, b * 512:(b + 1) * 512], func=AF.Square)),
              reads=mT.r(oc * 1024 + b * 512, oc * 1024 + (b + 1) * 512), writes=sq.r(b * 512, b * 512 + 512))
        P.add("pe", (lambda pe: pe.matmul(self.bank(6 + b), lhsT=self.onesb.t[:], rhs=sq.t[:, b, :], start=(oc == 0), stop=(oc == KC - 1))),
              reads=sq.r(b * 512, b * 512 + 512) + self.onesb.r(), writes=PSR(6 + b))
    self.proj_fm(wname, D, 0, KC, inT, in_sh, half * 1024, 2, [0, 1, 2, 3], evac)
    for b in range(2):
        self.rstd_from_sums(6 + b, 512, (tmp, 0), (rstd2[b], 0), float(D), 1e-6)
        self.postnorm_block(half * 2 + b, gpost, mT, b * 512, rstd2[b], tscr)


def _ln_stats(self, s1bank, s2bank, m, msq, tmp, rstd, eps):
    P = self.P
    P.add("dve", (lambda v: v.tensor_scalar(out=m.t[:, 0:512], in0=self.bank(s1bank), scalar1=1.0 / D, scalar2=None, op0=ALU.mult)),
          reads=PSR(s1bank), writes=m.r())
    P.add("pool", (lambda g: g.tensor_tensor(out=msq.t[:, 0:512], in0=m.t[:, 0:512], in1=m.t[:, 0:512], op=ALU.mult)),
          reads=m.r(), writes=msq.r())
    P.add("dve", (lambda v: v.scalar_tensor_tensor(out=tmp.t[:, 0:512], in0=self.bank(s2bank), scalar=1.0 / D, in1=msq.t[:, 0:512], op0=ALU.mult, op1=ALU.subtract)),
          reads=PSR(s2bank) + msq.r(), writes=tmp.r())
    P.add("act", (lambda a: a.activation(out=tmp.t[:, 0:512], in_=tmp.t[:, 0:512], func=AF.Sqrt, bias=self.epsb(eps))),
          reads=tmp.r() + self.epst.r(), writes=tmp.r())
    P.add("dve", (lambda v: v.reciprocal(out=rstd.t[:, 0:512], in_=tmp.t[:, 0:512])),
          reads=tmp.r(), writes=rstd.r())


GELU_C = 0.7978845608028654


def _gelu_evac(self, bk, bias_ap, out_ap, out_r, scr):
    P = self.P
    P.add("act", (lambda a: a.activation(out=out_ap, in_=self.bank(bk), func=AF.Gelu_apprx_tanh, bias=bias_ap)),
          reads=PSR(bk) + self.par_r(), writes=out_r)


def _mixer(self, l):
    kind = l % 3
    if kind == 0:
        self.attn(l)
    elif kind == 1:
        self.conv(l)
    else:
        self.sgate(l)


def _sgate(self, l):
    P, dr = self.P, self.dr
    n = "l%d_" % l
    H = 1024
    A = self.alloc("sg%d" % l)
    xnT = A("xnT", [KC, H], BF16)
    uT = A("uT", [KC, H], BF16)
    vT = A("vT", [KC, H], BF16)
    mT = A("mT", [KC, H], F32)
    sq = A("sq", [2, 512], BF16)
    tmp = A("tmp", [512], F32)
    rstd = A("rstd", [512], F32)
    rstd2 = [A("rstd2a", [512], F32), A("rstd2b", [512], F32)]
    tscr = A("tscr", [2, 512], F32)
    gscr = [A("gscr0", [3, 512], F32)]
    mm_ = A("m", [512], F32)
    msq = A("msq", [512], F32)
    bs = A("bs", [8, 128], F32)
    wsT = A("wsT", [8, 128], BF16)
    vtok = A("vtok", [2, 8, 128], BF16)
    wsP = SB(self.nc, "wsP_sg%d_%d" % (l, self.uid()), [8, 128], BF16, vtok.off)
    mixs = SB(self.nc, "mixs_sg%d_%d" % (l, self.uid()), [2, 512], F32, gscr[0].off)

    P.add("sp", (lambda e: e.dma_start(out=bs.t[:].rearrange("p g q -> p (g q)"), in_=dr[n + "b_s"].rearrange("g q -> (g q)").partition_broadcast(128))),
          writes=bs.r(), dma="sgc0")
    P.add("pool", (lambda g: g.dma_start(out=wsP.t[:], in_=dr[n + "w_s"].rearrange("g p q -> p g q"))), writes=wsP.r(), dma="sgc1")

    def trw(pe):
        ins = None
        for g in range(8):
            ins = pe.transpose(self.bank(4, 512).bitcast(BF16)[:, g * 128:(g + 1) * 128], wsP.t[:, g, :], self.identb.t[:])
        return ins
    P.add("pe", trw, reads=wsP.r() + self.identb.r(), writes=PSR(4))
    P.add("dve", (lambda v: v.tensor_copy(out=wsT.t[:].rearrange("p g q -> p (g q)"), in_=self.bank(4, 512).bitcast(BF16))), reads=PSR(4), writes=wsT.r())

    for half in range(2):
        for b in range(2):
            self.prenorm_block(half * 2 + b, n + "g_mix_pre", xnT, b * 512, sq, tmp, rstd, 6 + b)

        def evac_uv(oc, b, bk):
            dst = uT if oc < 8 else vT
            c = oc % 8
            self.gelu_evac(bk, self.par(n + "b_uv", oc), dst.t[:, c, b * 512:(b + 1) * 512], dst.r(c * H + b * 512, c * H + (b + 1) * 512), gscr[0])
        self.proj_fm(n + "w_uv", 2 * D, 0, 16, xnT, H, 0, 2, [0, 1, 2, 3], evac_uv)

        for b in range(2):
            c0 = b * 512
            s1, s2 = 4 + b, 6 + b
            for c in range(KC):
                sl = c % 2
                P.add("act", (lambda a, c=c, sl=sl, c0=c0: a.activation(out=sq.t[:, sl, :], in_=vT.t[:, c, c0:c0 + 512], func=AF.Square)),
                      reads=vT.r(c * H + c0, c * H + c0 + 512), writes=sq.r(sl * 512, sl * 512 + 512))
                P.add("pe", (lambda pe, c=c, sl=sl, s2=s2: pe.matmul(self.bank(s2), lhsT=self.onesb.t[:], rhs=sq.t[:, sl, :], start=(c == 0), stop=(c == KC - 1))),
                      reads=sq.r(sl * 512, sl * 512 + 512) + self.onesb.r(), writes=PSR(s2))
                P.add("pe", (lambda pe, c=c, s1=s1, c0=c0: pe.matmul(self.bank(s1), lhsT=self.onesb.t[:], rhs=vT.t[:, c, c0:c0 + 512], start=(c == 0), stop=(c == KC - 1))),
                      reads=vT.r(c * H + c0, c * H + c0 + 512) + self.onesb.r(), writes=PSR(s1))
            self.ln_stats(s1, s2, mm_, msq, tmp, rstd, 1e-5)
            for c in range(KC):
                sl = c % 2
                rr = vT.r(c * H + c0, c * H + c0 + 512)
                P.add("dve", (lambda v, c=c, sl=sl, c0=c0: v.tensor_tensor(out=tscr.t[:, sl, :], in0=vT.t[:, c, c0:c0 + 512], in1=mm_.t[:, 0:512], op=ALU.subtract)),
                      reads=rr + mm_.r(), writes=tscr.r(sl * 512, sl * 512 + 512))
                P.add("pool", (lambda g, sl=sl: g.tensor_tensor(out=tscr.t[:, sl, :], in0=tscr.t[:, sl, :], in1=rstd.t[:, 0:512], op=ALU.mult)),
                      reads=tscr.r(sl * 512, sl * 512 + 512) + rstd.r(), writes=tscr.r(sl * 512, sl * 512 + 512))
                P.add("dve", (lambda v, c=c, sl=sl, c0=c0: v.tensor_scalar(out=vT.t[:, c, c0:c0 + 512], in0=tscr.t[:, sl, :], scalar1=self.par(n + "g_sln", c), scalar2=self.par(n + "b_sln", c), op0=ALU.mult, op1=ALU.add)),
                      reads=tscr.r(sl * 512, sl * 512 + 512) + self.par_r(), writes=rr)

        for ch in range(8):
            t0 = ch * 128
            vs = ch % 2
            tbk = 4 + ch % 2

            def trv(pe, t0=t0, tbk=tbk):
                ins = None
                for g in range(8):
                    ins = pe.transpose(self.bank(tbk, 512).bitcast(BF16)[:, g * 128:(g + 1) * 128], vT.t[:, g, t0:t0 + 128], self.identb.t[:])
                return ins
            rd = []
            for g in range(8):
                rd += vT.r(g * H + t0, g * H + t0 + 128)
            P.add("pe", trv, reads=rd + self.identb.r(), writes=PSR(tbk))
            P.add("dve", (lambda v, vs=vs, tbk=tbk: v.tensor_copy(out=vtok.t[:, vs].rearrange("p g q -> p (g q)"), in_=self.bank(tbk, 512).bitcast(BF16))),
                  reads=PSR(tbk), writes=vtok.r(vs * 1024, vs * 1024 + 1024))
            for gh in range(2):
                obk = (ch % 2) * 2 + gh

                def mmx(pe, vs=vs, gh=gh, obk=obk):
                    ins = None
                    for gg in range(4):
                        g = gh * 4 + gg
                        ins = pe.matmul(self.bank(obk, 128, gg * 128), lhsT=vtok.t[:, vs, g, :], rhs=wsT.t[:, g, :], start=True, stop=True)
                    return ins
                P.add("pe", mmx, reads=vtok.r(vs * 1024, vs * 1024 + 1024) + wsT.r(), writes=PSR(obk))
                P.add("dve", (lambda v, gh=gh, obk=obk: v.tensor_tensor(out=mixs.t[:, gh, :], in0=self.bank(obk), in1=bs.t[:, gh * 4:(gh + 1) * 4, :].rearrange("p g q -> p (g q)"), op=ALU.add)),
                      reads=PSR(obk) + bs.r(), writes=mixs.r(gh * 512, gh * 512 + 512))
                ur = []
                for gg in range(4):
                    g = gh * 4 + gg
                    ur += uT.r(g * H + t0, g * H + t0 + 128)
                P.add("pool", (lambda g_, gh=gh, t0=t0: g_.tensor_tensor(out=uT.t[:, gh * 4:(gh + 1) * 4, t0:t0 + 128], in0=uT.t[:, gh * 4:(gh + 1) * 4, t0:t0 + 128],
                                                                        in1=mixs.t[:, gh, :].rearrange("p (g q) -> p g q", g=4), op=ALU.mult)),
                      reads=mixs.r(gh * 512, gh * 512 + 512) + ur, writes=ur)

        self.out_proj_postnorm(n + "w_o", n + "b_o", n + "g_mix_post", _Shift(uT, half * 1024), H, half, mT, sq, tmp, rstd2, tscr)


ZW = S + 32


def _conv(self, l):
    P, dr = self.P, self.dr
    n = "l%d_" % l
    A = self.alloc("cv%d" % l)
    xnT = A("xnT", [KC, S], BF16)
    zpad = A("zpad", [KC, ZW], BF16)
    Dc = A("Dc", [2, 31, 128], BF16)
    sT = A("sT", [KC, 1024], BF16)
    sq = A("sq", [2, 512], BF16)
    ybf = A("ybf", [2, 512], BF16)
    tmp = A("tmp", [512], F32)
    rstd = A("rstd", [512], F32)
    rstd2 = [A("rstd2a", [512], F32), A("rstd2b", [512], F32)]
    tscr = A("tscr", [2, 512], F32)
    mm_ = A("m", [512], F32)
    msq = A("msq", [512], F32)
    sig = SB(self.nc, "sig_cv%d_%d" % (l, self.uid()), [2, 512], F32, tscr.off)
    yT = SB(self.nc, "yT_cv%d_%d" % (l, self.uid()), [KC, 1024], F32, xnT.off)
    mT = yT

    for c in range(KC):
        P.add("pool", (lambda g, c=c: g.memset(zpad.t[:, c, 0:16], 0.0)), writes=zpad.r(c * ZW, c * ZW + 16))
        P.add("pool", (lambda g, c=c: g.memset(zpad.t[:, c, 16 + S:ZW], 0.0)), writes=zpad.r(c * ZW + 16 + S, (c + 1) * ZW))

    for tb in range(NTB):
        self.prenorm_block(tb, n + "g_mix_pre", xnT, tb * 512, sq, tmp, rstd, 6 + tb % 2)

    for pi in range(4):
        slot = self.wload([
            (lambda t: t[:, 0:2048].rearrange("p (k c) -> p k c", k=8), lambda dr, pi=pi: dr[n + "w_pw1"][:, pi * 256:(pi + 1) * 256].rearrange("(k p) c -> p k c", p=128)),
            (lambda t: t[:, 2048:4096].rearrange("p (k c) -> p k c", k=8), lambda dr, pi=pi: dr[n + "w_pw1"][:, D + pi * 256:D + (pi + 1) * 256].rearrange("(k p) c -> p k c", p=128)),
        ])
        for sub in range(2):
            c = pi * 2 + sub
            for tb in range(NTB):
                st = tb % 2
                ab, gb = st * 2, st * 2 + 1

                def mm(pe, slot=slot, sub=sub, tb=tb, bk=ab, gi=0):
                    ins = None
                    wv = slot.t[:, :].rearrange("p (g k c) -> p g k c", g=2, k=8)
                    for k in range(KC):
                        ins = pe.matmul(self.bank(bk), lhsT=wv[:, gi, k, sub * 128:(sub + 1) * 128], rhs=xnT.t[:, k, tb * 512:(tb + 1) * 512], start=(k == 0), stop=(k == KC - 1))
                    return ins
                rd = []
                for k in range(KC):
                    rd += xnT.r(k * S + tb * 512, k * S + (tb + 1) * 512)
                P.add("pe", mm, reads=slot.r() + rd, writes=PSR(ab))
                P.add("pe", (lambda pe, slot=slot, sub=sub, tb=tb, gb=gb, mm=mm: mm(pe, slot, sub, tb, gb, 1)), reads=slot.r() + rd, writes=PSR(gb))
                P.add("act", (lambda a, st=st, gb=gb, c=c: a.activation(out=sig.t[:, st, :], in_=self.bank(gb), func=AF.Sigmoid, bias=self.par(n + "b_pw1", 8 + c))),
                      reads=PSR(gb) + self.par_r(), writes=sig.r(st * 512, st * 512 + 512))
                zr = zpad.r(c * ZW + 16 + tb * 512, c * ZW + 16 + (tb + 1) * 512)
                P.add("dve", (lambda v, st=st, ab=ab, c=c, tb=tb: v.scalar_tensor_tensor(out=zpad.t[:, c, 16 + tb * 512:16 + (tb + 1) * 512], in0=self.bank(ab), scalar=self.par(n + "b_pw1", c),
                                                                                     in1=sig.t[:, st, :], op0=ALU.add, op1=ALU.mult)),
                      reads=PSR(ab) + sig.r(st * 512, st * 512 + 512) + self.par_r(), writes=zr)

    for half in range(2):
        for c in range(KC):
            ds = c % 2

            def mkd(v, c=c, ds=ds):
                ins = None
                for j in range(31):
                    ins = v.tensor_scalar(out=Dc.t[:, ds, j, :], in0=self.identb.t[:], scalar1=self.par(n + "w_dw", c * 31 + j), scalar2=None, op0=ALU.mult)
                return ins
            P.add("dve", mkd, reads=self.identb.r() + self.par_r(), writes=Dc.r(ds * 31 * 128, (ds + 1) * 31 * 128))
            for b in range(2):
                tb = half * 2 + b
                bk = (c * 2 + b) % 4

                def mmc(pe, c=c, ds=ds, tb=tb, bk=bk):
                    ins = None
                    for j in range(31):
                        ins = pe.matmul(self.bank(bk), lhsT=Dc.t[:, ds, j, :], rhs=zpad.t[:, c, tb * 512 + j + 1: tb * 512 + j + 1 + 512], start=(j == 0), stop=(j == 30))
                    return ins
                P.add("pe", mmc, reads=Dc.r(ds * 31 * 128, (ds + 1) * 31 * 128) + zpad.r(c * ZW + tb * 512, c * ZW + tb * 512 + 512 + 32), writes=PSR(bk))
                yr = yT.r(c * 1024 + b * 512, c * 1024 + (b + 1) * 512)
                P.add("act", (lambda a, c=c, b=b, bk=bk: a.activation(out=yT.t[:, c, b * 512:(b + 1) * 512], in_=self.bank(bk), func=AF.Identity, bias=self.par(n + "b_dw", c))),
                      reads=PSR(bk) + self.par_r(), writes=yr)
                P.add("act", (lambda a, c=c, b=b: a.activation(out=sq.t[:, b, :], in_=yT.t[:, c, b * 512:(b + 1) * 512], func=AF.Square)),
                      reads=yr, writes=sq.r(b * 512, b * 512 + 512))
                P.add("act", (lambda a, c=c, b=b: a.activation(out=ybf.t[:, b, :], in_=yT.t[:, c, b * 512:(b + 1) * 512], func=AF.Copy)),
                      reads=yr, writes=ybf.r(b * 512, b * 512 + 512))
                P.add("pe", (lambda pe, c=c, b=b: pe.matmul(self.bank(6 + b), lhsT=self.onesb.t[:], rhs=sq.t[:, b, :], start=(c == 0), stop=(c == KC - 1))),
                      reads=sq.r(b * 512, b * 512 + 512) + self.onesb.r(), writes=PSR(6 + b))
                P.add("pe", (lambda pe, c=c, b=b: pe.matmul(self.bank(4 + b), lhsT=self.onesb.t[:], rhs=ybf.t[:, b, :], start=(c == 0), stop=(c == KC - 1))),
                      reads=ybf.r(b * 512, b * 512 + 512) + self.onesb.r(), writes=PSR(4 + b))
        for b in range(2):
            self.ln_stats(4 + b, 6 + b, mm_, msq, tmp, rstd, 1e-5)
            for c in range(KC):
                sl = c % 2
                yr = yT.r(c * 1024 + b * 512, c * 1024 + (b + 1) * 512)
                P.add("dve", (lambda v, c=c, sl=sl, b=b: v.tensor_tensor(out=tscr.t[:, sl, :], in0=yT.t[:, c, b * 512:(b + 1) * 512], in1=mm_.t[:, 0:512], op=ALU.subtract)),
                      reads=yr + mm_.r(), writes=tscr.r(sl * 512, sl * 512 + 512))
                P.add("pool", (lambda g, sl=sl: g.tensor_tensor(out=tscr.t[:, sl, :], in0=tscr.t[:, sl, :], in1=rstd.t[:, 0:512], op=ALU.mult)),
                      reads=tscr.r(sl * 512, sl * 512 + 512) + rstd.r(), writes=tscr.r(sl * 512, sl * 512 + 512))
                P.add("act", (lambda a, c=c, sl=sl, b=b: a.activation(out=sT.t[:, c, b * 512:(b + 1) * 512], in_=tscr.t[:, sl, :], func=AF.Silu, scale=self.par(n + "g_cln", c), bias=self.par(n + "b_cln", c))),
                      reads=tscr.r(sl * 512, sl * 512 + 512) + self.par_r(), writes=sT.r(c * 1024 + b * 512, c * 1024 + (b + 1) * 512))
        self.out_proj_postnorm(n + "w_pw2", n + "b_pw2", n + "g_mix_post", _Shift(sT, half * 1024), 1024, half, mT, sq, tmp, rstd2, tscr)


class _Shift:
    def __init__(self, sb, tok0):
        self.sb = sb
        self.tok0 = tok0
        self.t = _ShiftT(sb.t, tok0)

    def r(self, lo, hi):
        return self.sb.r(lo - self.tok0, hi - self.tok0)


class _ShiftT:
    def __init__(self, t, tok0):
        self._t = t
        self.tok0 = tok0

    def __getitem__(self, key):
        p, k, sl = key
        return self._t[p, k, sl.start - self.tok0: sl.stop - self.tok0]


Builder.alloc = _alloc
Builder.proj_fm = _proj_feature_major
Builder.out_proj_postnorm = _out_proj_postnorm
Builder.ln_stats = _ln_stats
Builder.gelu_evac = _gelu_evac
Builder.mixer = _mixer
Builder.sgate = _sgate
Builder.conv = _conv


def _attn(self, l):
    P, dr = self.P, self.dr
    nc = self.nc
    n = "l%d_" % l
    lam_init = 0.8 - 0.6 * math.exp(-0.3 * l)
    A = self.alloc("at%d" % l)
    xnT = A("xnT", [KC, S], BF16)
    oT = A("oT", [KC, S], BF16)
    qT = A("qT", [S], BF16)
    kT = A("kT", [S], BF16)
    V = A("V", [16, 128], BF16)
    cosT = A("cosT", [S], BF16)
    sinT = A("sinT", [S], BF16)
    u1 = A.state["off"]
    sq = A("sq", [2, 512], BF16)
    tmp = A("tmp", [512], F32)
    rstd = A("rstd", [512], F32)
    rstd2 = [A("rstd2a", [512], F32), A("rstd2b", [512], F32)]
    A.state["off"] = u1
    E = [A("E%d" % i, [1024], BF16) for i in range(3)]
    raw = [A("raw%d" % i, [512], BF16) for i in range(2)]
    t1s = [A("t1a", [512], F32), A("t1b", [512], F32)]
    t2s = [A("t2a", [512], F32), A("t2b", [512], F32)]
    sl1, sl2 = t1s, t2s
    u2 = A.state["off"]
    tscr = A("tscr", [2, 512], F32)
    A.state["off"] = u2
    r1 = A("r1", [512], F32)
    r2 = A("r2", [512], F32)
    a1 = A("a1", [512], F32)
    a2 = A("a2", [512], F32)
    oh = A("oh", [S], F32)
    osq = A("osq", [S], BF16)
    sm = A("sm", [64], F32)
    gsub = A("gsub", [4], F32)
    mT = SB(nc, "mT_at%d_%d" % (l, self.uid()), [KC, 1024], F32, xnT.off)
    smr = sm.r()
    scale = 64 ** -0.5
    DBG = 9
    _Padd = P.add
    if DBG <= -3:
        P.add = lambda *a, **k: None

    P.add("pool", (lambda g: g.dma_start(out=cosT.t[:], in_=dr["c_cos"])), writes=cosT.r(), dma="rp0")
    P.add("pool", (lambda g: g.dma_start(out=sinT.t[:], in_=dr["c_sin"])), writes=sinT.r(), dma="rp1")
    P.add("dve", (lambda v: v.scalar_tensor_tensor(out=a1.t[:, 0:64], in0=self.par(n + "lam_q1", 0, 64), scalar=1.0, in1=self.par(n + "lam_k1", 0, 64), op0=ALU.mult, op1=ALU.mult, accum_out=sm.t[:, 0:1])),
          reads=self.par_r(), writes=smr + a1.r())
    P.add("dve", (lambda v: v.scalar_tensor_tensor(out=a1.t[:, 0:64], in0=self.par(n + "lam_q2", 0, 64), scalar=1.0, in1=self.par(n + "lam_k2", 0, 64), op0=ALU.mult, op1=ALU.mult, accum_out=sm.t[:, 1:2])),
          reads=self.par_r() + smr, writes=smr + a1.r())
    P.add("act", (lambda a: a.activation(out=sm.t[:, 2:4], in_=sm.t[:, 0:2], func=AF.Exp)), reads=smr, writes=smr)
    P.add("dve", (lambda v: v.scalar_tensor_tensor(out=sm.t[:, 4:5], in0=sm.t[:, 2:3], scalar=lam_init, in1=sm.t[:, 3:4], op0=ALU.add, op1=ALU.subtract)), reads=smr, writes=smr)
    P.add("dve", (lambda v: v.tensor_scalar(out=sm.t[:, 5:6], in0=sm.t[:, 4:5], scalar1=-1.0, scalar2=None, op0=ALU.mult)), reads=smr, writes=smr)
    P.add("dve", (lambda v: v.tensor_scalar(out=gsub.t[:, 0:1], in0=self.par(n + "g_subln", 0, 1), scalar1=1.0 - lam_init, scalar2=None, op0=ALU.mult)),
          reads=self.par_r(), writes=gsub.r())

    P.add = _Padd
    for tb in range(NTB):
        self.prenorm_block(tb, n + "g_mix_pre", xnT, tb * 512, sq, tmp, rstd, 6 + tb % 2)

    estep = 0
    pending = [None]
    for h in range(8):
        slot = self.wload([
            (lambda t, w=w: t[:, 0:3072].rearrange("p (k w c) -> p k w c", k=8, w=3)[:, :, w, :],
             lambda dr, w=w, h=h: dr[n + "w_qkv"][:, w * D + h * 128: w * D + (h + 1) * 128].rearrange("(k p) c -> p k c", p=128))
            for w in range(3)
        ])
        wv = slot.t[:, 0:3072].rearrange("p (k w c) -> p k w c", k=8, w=3)
        def rec_v(tg):
            vb = 6 + tg % 2

            def mmv(pe, tg=tg, vb=vb, wv=wv):
                ins = None
                for tt in range(4):
                    tk = (tg * 4 + tt) * 128
                    for k in range(KC):
                        ins = pe.matmul(self.bank(vb, 128, tt * 128), lhsT=xnT.t[:, k, tk:tk + 128], rhs=wv[:, k, 2, :], start=(k == 0), stop=(k == KC - 1))
                return ins
            xr = []
            for k in range(KC):
                xr += xnT.r(k * S + tg * 512, k * S + tg * 512 + 512)
            P.add("pe", mmv, reads=slot.r() + xr, writes=PSR(vb))
            P.add("dve", (lambda v, tg=tg, vb=vb: v.tensor_copy(out=V.t[:, tg * 4:(tg + 1) * 4, :].rearrange("p t e -> p (t e)"), in_=self.bank(vb))),
                  reads=PSR(vb), writes=V.r(tg * 512, tg * 512 + 512))

        blocks = [(w, dst, tb) for w, dst in ((0, qT), (1, kT)) for tb in range(NTB)] if DBG >= -1 else []

        def rec_mm(i):
            w, dst, tb = blocks[i]
            c0 = tb * 512
            pb = i % 4
            xr = []
            for k in range(KC):
                xr += xnT.r(k * S + c0, k * S + c0 + 512)

            def mm(pe, w=w, c0=c0, pb=pb, wv=wv):
                ins = None
                for k in range(KC):
                    ins = pe.matmul(self.bank(pb), lhsT=wv[:, k, w, :], rhs=xnT.t[:, k, c0:c0 + 512], start=(k == 0), stop=(k == KC - 1))
                return ins
            P.add("pe", mm, reads=slot.r() + xr, writes=PSR(pb))

        def rec_evac(i):
            w, dst, tb = blocks[i]
            c0 = tb * 512
            pb = i % 4
            rb = raw[i % 2]
            t1b = t1s[i % 2]
            P.add("dve", (lambda v: v.tensor_tensor(out=t1b.t[:, 0:512], in0=self.bank(pb), in1=cosT.t[:, c0:c0 + 512], op=ALU.mult)),
                  reads=PSR(pb) + cosT.r(c0, c0 + 512), writes=t1b.r())
            P.add("act", (lambda a: a.activation(out=rb.t[:, 0:512], in_=self.bank(pb), func=AF.Copy)), reads=PSR(pb) + t1b.r(), writes=rb.r())
            qb_ = 4 + i % 2
            P.add("pe", (lambda pe: pe.matmul(self.bank(qb_), lhsT=self.permb.t[:], rhs=rb.t[:, 0:512], start=True, stop=True)),
                  reads=rb.r() + self.permb.r(), writes=PSR(qb_))

        def rec_rope(i):
            w, dst, tb = blocks[i]
            c0 = tb * 512
            qb_ = 4 + i % 2
            t1b = t1s[i % 2]
            t2b = t2s[i % 2]
            P.add("dve", (lambda v: v.tensor_tensor(out=t2b.t[:, 0:512], in0=self.bank(qb_), in1=sinT.t[:, c0:c0 + 512], op=ALU.mult)),
                  reads=PSR(qb_) + sinT.r(c0, c0 + 512), writes=t2b.r())
            P.add("pool", (lambda g: g.tensor_tensor(out=dst.t[:, c0:c0 + 512], in0=t1b.t[:, 0:512], in1=t2b.t[:, 0:512], op=ALU.add)),
                  reads=t1b.r() + t2b.r(), writes=dst.r(c0, c0 + 512))

        nb = len(blocks)
        for i in range(nb + 2):
            if i < nb:
                rec_mm(i)

            if 0 <= i - 1 < nb:
                rec_evac(i - 1)
            if 0 <= i - 2 < nb:
                rec_rope(i - 2)
        for tg in range(4):
            rec_v(tg)
        steps = [(qt, kc) for qt in range(NTB if DBG >= 1 else 0) for kc in range(16)]

        def rec_scores(i):
            qt, kc = steps[i]
            st = (estep0 + i) % 2
            k0, q0 = kc * 128, qt * 512

            def sc(pe, st=st, k0=k0, q0=q0):
                ins = None
                for c in range(2):
                    ins = pe.matmul(self.bank(st * 2 + c), lhsT=kT.t[c * 64:(c + 1) * 64, k0:k0 + 128], rhs=qT.t[c * 64:(c + 1) * 64, q0:q0 + 512], start=True, stop=True)
                return ins
            P.add("pe", sc, reads=kT.r(k0, k0 + 128) + qT.r(q0, q0 + 512), writes=PSR(st * 2, st * 2 + 1))

        estep0 = estep
        if steps:
            rec_scores(0)
        for i, (qt, kc) in enumerate(steps):
            q0 = qt * 512
            st = (estep0 + i) % 2
            e = E[(estep0 + i) % 3]
            P.add("act", (lambda a, st=st, e=e: a.activation(out=e.t[:, 0:1024], in_=self.ps[:, st * 1024:(st + 1) * 1024], func=AF.Exp, scale=scale)),
                  reads=PSR(st * 2, st * 2 + 1), writes=e.r())
            if i + 1 < len(steps):
                rec_scores(i + 1)

            def av(pe, e=e, kc=kc):
                ins = None
                for c in range(2):
                    ins = pe.matmul(self.bank(4 + c), lhsT=V.t[:, kc, :], rhs=e.t[:, c * 512:(c + 1) * 512], start=(kc == 0), stop=(kc == 15))
                for c in range(2):
                    ins = pe.matmul(self.bank(6 + c), lhsT=self.onesb.t[:], rhs=e.t[:, c * 512:(c + 1) * 512], start=(kc == 0), stop=(kc == 15))
                return ins
            P.add("pe", av, reads=e.r() + V.r(kc * 128, kc * 128 + 128) + self.onesb.r(), writes=PSR(4, 5, 6, 7))
            if kc != 15:
                continue
            P.add("dve", (lambda v: v.tensor_copy(out=r1.t[:, 0:512], in_=self.bank(6))), reads=PSR(6), writes=r1.r())
            P.add("dve", (lambda v: v.tensor_copy(out=r2.t[:, 0:512], in_=self.bank(7))), reads=PSR(7), writes=r2.r())
            P.add("dve", (lambda v: v.tensor_copy(out=a1.t[:, 0:512], in_=self.bank(4))), reads=PSR(4), writes=a1.r())
            P.add("dve", (lambda v: v.tensor_copy(out=a2.t[:, 0:512], in_=self.bank(5))), reads=PSR(5), writes=a2.r())
            P.add("dve", (lambda v: v.reciprocal(out=r1.t[:, 0:512], in_=r1.t[:, 0:512])), reads=r1.r(), writes=r1.r())
            P.add("dve", (lambda v: v.reciprocal(out=r2.t[:, 0:512], in_=r2.t[:, 0:512])), reads=r2.r(), writes=r2.r())
            P.add("pool", (lambda v: v.tensor_tensor(out=a1.t[:, 0:512], in0=a1.t[:, 0:512], in1=r1.t[:, 0:512], op=ALU.mult)), reads=a1.r() + r1.r(), writes=a1.r())
            P.add("pool", (lambda v: v.tensor_tensor(out=a2.t[:, 0:512], in0=a2.t[:, 0:512], in1=r2.t[:, 0:512], op=ALU.mult)), reads=a2.r() + r2.r(), writes=a2.r())
            P.add("dve", (lambda g, q0=q0: g.scalar_tensor_tensor(out=oh.t[:, q0:q0 + 512], in0=a2.t[:, 0:512], scalar=sm.t[:, 5:6], in1=a1.t[:, 0:512], op0=ALU.mult, op1=ALU.add)),
                  reads=a1.r() + a2.r() + smr, writes=oh.r(q0, q0 + 512))
            P.add("pool", (lambda g, q0=q0: g.tensor_tensor(out=osq.t[:, q0:q0 + 512], in0=oh.t[:, q0:q0 + 512], in1=oh.t[:, q0:q0 + 512], op=ALU.mult)),
                  reads=oh.r(q0, q0 + 512), writes=osq.r(q0, q0 + 512))
        estep += len(steps)
        def subln(h=h):
            for qt in range(NTB):
                q0 = qt * 512
                sb_ = qt % 4
                P.add("pe", (lambda pe, q0=q0, sb_=sb_: pe.matmul(self.bank(sb_), lhsT=self.onesb.t[:], rhs=osq.t[:, q0:q0 + 512], start=True, stop=True)),
                      reads=osq.r(q0, q0 + 512) + self.onesb.r(), writes=PSR(sb_))
                rsb = sl1[qt % 2]
                rrb = sl2[qt % 2]
                self.rstd_from_sums(sb_, 512, (rsb, 0), (rrb, 0), 128.0, 1e-5)
                P.add("dve", (lambda v, q0=q0, rrb=rrb, h=h: v.scalar_tensor_tensor(out=oT.t[:, h, q0:q0 + 512], in0=oh.t[:, q0:q0 + 512], scalar=gsub.t[:, 0:1], in1=rrb.t[:, 0:512], op0=ALU.mult, op1=ALU.mult)),
                      reads=oh.r(q0, q0 + 512) + rrb.r() + gsub.r(), writes=oT.r(h * S + q0, h * S + q0 + 512))
        subln()

    for half in range(2):
        self.out_proj_postnorm(n + "w_o", None, n + "g_mix_post", oT, S, half, mT, sq, tmp, rstd2, tscr)


Builder.attn = _attn
```

```python
import math
import os
from contextlib import ExitStack
import numpy as np
import concourse.bass as bass
import concourse.mybir as mybir
from concourse.bass_utils import run_bass_kernel_spmd

F32 = mybir.dt.float32
BF16 = mybir.dt.bfloat16
AF = mybir.ActivationFunctionType
ALU = mybir.AluOpType
AX = mybir.AxisListType

D = 1024
S = 2048
KC = 8
DFF = 2816
FC = 22
NTB = 4
DEPTH = 4
NCORES = 8
SEQ_PER_CORE = 5
GR = 256
EPOCH = 12000
SBUF_BASE = 16640
SBUF_LIMIT = 229312


class Op:
    __slots__ = ("eng", "fn", "deps", "is_dma", "dsem", "token", "idx")


class Prog:
    ENGS = ("pe", "act", "dve", "pool", "sp")

    def __init__(self, nc, es):
        self.nc = nc
        self.es = es
        self.ops = []
        self.last_w = {}
        self.readers = {}
        self.dma_sems = {}
        self.nsem = 0
        self.dry = False

    def _sem(self, name):
        self.nsem += 1
        return self.es.enter_context(self.nc.semaphore(name))

    def add(self, eng, fn, reads=(), writes=(), dma=None):
        if self.dry:
            return None
        op = Op()
        op.eng = eng
        op.fn = fn
        op.is_dma = dma is not None
        op.dsem = dma
        op.token = None
        op.idx = len(self.ops)
        deps = set()
        ops = self.ops
        for r in reads:
            w = self.last_w.get(r)
            if w is not None:
                wo = ops[w]
                if wo.is_dma or wo.eng != eng or eng not in ("pe",):
                    deps.add(w)
        for r in writes:
            w = self.last_w.get(r)
            if w is not None:
                wo = ops[w]
                if wo.is_dma or wo.eng != eng or eng not in ("pe",):
                    deps.add(w)
            rd = self.readers.get(r)
            if rd:
                for k, ri in rd.items():
                    ro = ops[ri]
                    if ro.is_dma or ro.eng != eng:
                        deps.add(ri)
        op.deps = deps
        for r in writes:
            self.last_w[r] = op.idx
            self.readers[r] = {}
        for r in reads:
            d = self.readers.get(r)
            if d is None:
                d = self.readers[r] = {}
            if op.is_dma:
                d[("dma", op.idx)] = op.idx
            else:
                d[eng] = op.idx
        self.ops.append(op)
        return op

    def barrier(self):
        if self.dry:
            return
        keys = set(self.last_w.keys()) | set(self.readers.keys())
        self.add("dve", (lambda v: v.engine_nop()), reads=(), writes=list(keys))

    def emit(self):
        nc = self.nc
        ops = self.ops
        needed = set()
        for op in ops:
            needed |= op.deps
        esem = {}
        cnt = {e: 0 for e in self.ENGS}
        for op in ops:
            if op.is_dma:
                ent = self.dma_sems.get(op.dsem)
                if ent is None:
                    ent = self.dma_sems[op.dsem] = [self._sem("d_" + op.dsem), 0]
                ent[1] += 16
                op.token = (ent[0], ent[1], ("d", op.dsem))
            elif op.idx in needed:
                e = op.eng
                ep = cnt[e] // EPOCH
                val = cnt[e] % EPOCH + 1
                cnt[e] += 1
                key = (e, ep)
                if key not in esem:
                    esem[key] = self._sem("s_%s_%d" % (e, ep))
                op.token = (esem[key], val, key)
        for op in ops:
            if op.token is not None and not op.is_dma:
                pass
        block = self.es.enter_context(nc.Block())
        per_eng = {e: [o for o in ops if o.eng == e] for e in self.ENGS}

        def run(engname, eng):
            waited = {}
            for op in per_eng[engname]:
                ws = {}
                for d in op.deps:
                    sem, val, key = ops[d].token
                    if waited.get(key, 0) >= val:
                        continue
                    if ws.get(key, (None, 0))[1] < val:
                        ws[key] = (sem, val)
                for key, (sem, val) in ws.items():
                    eng.wait_ge(sem, val)
                    waited[key] = val
                    if key[0] != "d":
                        for k2 in list(esem.keys()):
                            if k2[0] == key[0] and k2[1] < key[1]:
                                waited[k2] = EPOCH * 4
                ins = op.fn(eng)
                if op.token is not None:
                    if op.is_dma:
                        ins.then_inc(op.token[0], 16)
                    else:
                        ins.then_inc(op.token[0], 1)

        @block.tensor
        def _(t):
            run("pe", t)

        @block.scalar
        def _(a):
            run("act", a)

        @block.vector
        def _(v):
            run("dve", v)

        @block.gpsimd
        def _(g):
            run("pool", g)

        @block.sync
        def _(s):
            run("sp", s)


class SB:
    def __init__(self, nc, name, shape, dtype, off):
        self.t = nc.alloc_sbuf_tensor_at(name, [128] + list(shape), dtype, offset=off)
        self.off = off
        self.esz = 2 if dtype == BF16 else 4
        n = 1
        for s in shape:
            n *= s
        self.nbytes = n * self.esz
        assert off + self.nbytes <= SBUF_LIMIT, (name, off, self.nbytes)
        self.shape = shape

    def r(self, lo=0, hi=None):
        if hi is None:
            hi = self.nbytes // self.esz
        a = (self.off + lo * self.esz) // GR
        b = (self.off + hi * self.esz - 1) // GR
        return list(range(a, b + 1))

    def end(self):
        return self.off + self.nbytes


def PSR(*banks):
    return [("ps", b) for b in banks]


def fm(v):
    v = np.asarray(v, np.float32)
    return np.ascontiguousarray(v.reshape(-1, 128).T)


def pack_params(inp):
    cols = []
    index = {}

    def put(name, arr):
        arr = np.asarray(arr, np.float32)
        assert arr.shape[0] == 128
        index[name] = (sum(c.shape[1] for c in cols), arr.shape[1])
        cols.append(arr)

    for l in range(DEPTH):
        n = "l%d_" % l
        for g in ("g_mix_pre", "g_mix_post", "g_ffn_pre", "g_ffn_post"):
            put(n + g, fm(inp[n + g]))
        kind = l % 3
        if kind == 0:
            for g in ("lam_q1", "lam_k1", "lam_q2", "lam_k2"):
                put(n + g, np.broadcast_to(np.asarray(inp[n + g], np.float32)[None, :], (128, 64)))
            put(n + "g_subln", fm(inp[n + "g_subln"]))
        elif kind == 1:
            put(n + "b_pw1", fm(inp[n + "b_pw1"]))
            wdw = np.asarray(inp[n + "w_dw"], np.float32)
            put(n + "w_dw", np.ascontiguousarray(wdw.reshape(31, 8, 128).transpose(2, 1, 0)).reshape(128, 8 * 31))
            for g in ("b_dw", "g_cln", "b_cln", "b_pw2"):
                put(n + g, fm(inp[n + g]))
        else:
            put(n + "b_uv", fm(inp[n + "b_uv"]))
            for g in ("g_sln", "b_sln", "b_o"):
                put(n + g, fm(inp[n + g]))
    arr = np.ascontiguousarray(np.concatenate(cols, axis=1))
    return arr, index


def rope_tables():
    pos = np.arange(S, dtype=np.float32)
    inv = (1.0 / (10000.0 ** (np.arange(0, 64, 2, dtype=np.float32) / 64))).astype(np.float32)
    ang = pos[None, :] * inv[:, None]
    c = np.cos(ang).astype(np.float32)
    s = np.sin(ang).astype(np.float32)
    cosT = np.zeros((128, S), np.float32)
    sinT = np.zeros((128, S), np.float32)
    for comp in range(2):
        for half in range(2):
            base = comp * 64 + half * 32
            cosT[base:base + 32] = c
            sinT[base:base + 32] = -s if half == 0 else s
    return cosT, sinT


def const_inputs():
    perm = np.zeros((128, 128), np.float32)
    for p in range(128):
        perm[p ^ 32, p] = 1.0
    cosT, sinT = rope_tables()
    return {
        "c_ident": np.eye(128, dtype=np.float32),
        "c_perm": perm,
        "c_cos": cosT,
        "c_sin": sinT,
    }


_PARAM_INDEX = None


def param_index():
    global _PARAM_INDEX
    if _PARAM_INDEX is None:
        fake = {}
        for l in range(DEPTH):
            n = "l%d_" % l
            for g in ("g_mix_pre", "g_mix_post", "g_ffn_pre", "g_ffn_post"):
                fake[n + g] = np.zeros(D, np.float32)
            kind = l % 3
            if kind == 0:
                for g in ("lam_q1", "lam_k1", "lam_q2", "lam_k2"):
                    fake[n + g] = np.zeros(64, np.float32)
                fake[n + "g_subln"] = np.zeros(128, np.float32)
            elif kind == 1:
                fake[n + "b_pw1"] = np.zeros(2 * D, np.float32)
                fake[n + "w_dw"] = np.zeros((31, D), np.float32)
                for g in ("b_dw", "g_cln", "b_cln", "b_pw2"):
                    fake[n + g] = np.zeros(D, np.float32)
            else:
                fake[n + "b_uv"] = np.zeros(2 * D, np.float32)
                for g in ("g_sln", "b_sln", "b_o"):
                    fake[n + g] = np.zeros(D, np.float32)
        arr, idx = pack_params(fake)
        _PARAM_INDEX = (arr.shape[1], idx)
    return _PARAM_INDEX


class Builder:
    def __init__(self, nseq=SEQ_PER_CORE, layers=(0, 1, 2, 3), do_ffn=True, do_mix=True):
        self.nseq = nseq
        self.layers = layers
        self.do_ffn = do_ffn
        self.do_mix = do_mix
        self.nc = bass.Bass("TRN2", target_bir_lowering=False)
        self.es = ExitStack()
        self.P = Prog(self.nc, self.es)
        self.npar, self.pidx = param_index()
        self.plan_mode = False
        self.plan = []
        self.ws_issued = 0

    def declare_dram(self):
        nc = self.nc
        dr = {}
        dr["x"] = nc.dram_tensor("x", [self.nseq, S, D], F32, kind="ExternalInput").ap()
        dr["y"] = nc.dram_tensor("y", [self.nseq, S, D], F32, kind="ExternalOutput").ap()
        dr["params"] = nc.dram_tensor("params", [128, self.npar], F32, kind="ExternalInput").ap()
        dr["c_ident"] = nc.dram_tensor("c_ident", [128, 128], F32, kind="ExternalInput").ap()
        dr["c_perm"] = nc.dram_tensor("c_perm", [128, 128], F32, kind="ExternalInput").ap()
        dr["c_cos"] = nc.dram_tensor("c_cos", [128, S], F32, kind="ExternalInput").ap()
        dr["c_sin"] = nc.dram_tensor("c_sin", [128, S], F32, kind="ExternalInput").ap()
        dr["bscr"] = nc.dram_tensor("bscr", [4, 512], F32).ap()
        for l in range(DEPTH):
            n = "l%d_" % l
            kind = l % 3
            if kind == 0:
                dr[n + "w_qkv"] = nc.dram_tensor(n + "w_qkv", [D, 3 * D], F32, kind="ExternalInput").ap()
                dr[n + "w_o"] = nc.dram_tensor(n + "w_o", [D, D], F32, kind="ExternalInput").ap()
            elif kind == 1:
                dr[n + "w_pw1"] = nc.dram_tensor(n + "w_pw1", [D, 2 * D], F32, kind="ExternalInput").ap()
                dr[n + "w_pw2"] = nc.dram_tensor(n + "w_pw2", [D, D], F32, kind="ExternalInput").ap()
            else:
                dr[n + "w_uv"] = nc.dram_tensor(n + "w_uv", [D, 2 * D], F32, kind="ExternalInput").ap()
                dr[n + "w_s"] = nc.dram_tensor(n + "w_s", [8, 128, 128], F32, kind="ExternalInput").ap()
                dr[n + "b_s"] = nc.dram_tensor(n + "b_s", [8, 128], F32, kind="ExternalInput").ap()
                dr[n + "w_o"] = nc.dram_tensor(n + "w_o", [D, D], F32, kind="ExternalInput").ap()
            dr[n + "w_gate_up"] = nc.dram_tensor(n + "w_gate_up", [D, 2 * DFF], F32, kind="ExternalInput").ap()
            dr[n + "w_down"] = nc.dram_tensor(n + "w_down", [DFF, D], F32, kind="ExternalInput").ap()
        self.dr = dr

    def alloc_persistent(self):
        nc = self.nc
        off = SBUF_BASE

        def A(name, shape, dt):
            nonlocal off
            b = SB(nc, name, shape, dt, off)
            off = (b.end() + GR - 1) // GR * GR
            return b

        self.XT = A("XT", [KC, S], F32)
        self.PAR = A("PAR", [self.npar], F32)
        self.identf = A("identf", [128], F32)
        self.identb = A("identb", [128], BF16)
        self.permb = A("permb", [128], BF16)
        self.onesb = A("onesb", [128], BF16)
        self.stat = A("stat", [64], F32)
        self.epst = A("epst", [16], F32)
        self.WS = [A("ws%d" % i, [4096], BF16) for i in range(2)]
        self.big0 = off
        self.ps = nc.alloc_psum_tensor("ps", [128, 4096], F32)
        self.ws_count = 0

    def bank(self, b, n=512, o=0):
        return self.ps[:, b * 512 + o: b * 512 + o + n]

    def par(self, name, c0=0, n=1):
        o, w = self.pidx[name]
        return self.PAR.t[:, o + c0: o + c0 + n]

    def par_r(self):
        return self.PAR.r()

    def wload(self, pieces):
        idx = self.ws_count
        self.ws_count += 1
        slot = self.WS[idx % len(self.WS)]
        if self.plan_mode:
            self.plan.append(pieces)
            return slot
        while self.ws_issued <= min(idx + 1, len(self.plan) - 1):
            j = self.ws_issued
            sl = self.WS[j % len(self.WS)]
            for ent in self.plan[j]:
                dst_fn, src_fn = ent[0], ent[1]
                wr = sl.r(*ent[2]) if len(ent) > 2 else sl.r()
                dst = dst_fn(sl.t)
                src = src_fn(self.dr)
                self.P.add("pool", (lambda g, d=dst, s_=src: g.dma_start(out=d, in_=s_)), reads=(), writes=wr, dma="ws%d" % (j % len(self.WS)))
            self.ws_issued += 1
        return slot

    def init_consts(self):
        P, dr = self.P, self.dr
        P.add("sp", lambda s: s.dma_start(out=self.PAR.t[:], in_=dr["params"]), writes=self.PAR.r(), dma="c0")
        P.add("sp", lambda s: s.dma_start(out=self.identf.t[:], in_=dr["c_ident"]), writes=self.identf.r(), dma="c1")
        P.add("pool", lambda g: g.dma_start(out=self.identb.t[:], in_=dr["c_ident"]), writes=self.identb.r(), dma="c2")
        P.add("pool", lambda g: g.dma_start(out=self.permb.t[:], in_=dr["c_perm"]), writes=self.permb.r(), dma="c3")
        P.add("pool", lambda g: g.memset(self.onesb.t[:], 1.0), writes=self.onesb.r())
        P.add("pool", lambda g: g.memset(self.epst.t[:, 0:1], 1e-6), writes=self.epst.r())
        P.add("pool", lambda g: g.memset(self.epst.t[:, 1:2], 1e-5), reads=self.epst.r(), writes=self.epst.r())

    def load_seq(self, s, stage):
        P = self.P
        XT = self.XT
        for t in range(16):
            st = t % 2
            src = self.dr["x"][s, t * 128:(t + 1) * 128, :]
            dstv = stage.t[:, st, :]
            P.add("sp", (lambda e, d=dstv, s_=src: e.dma_start(out=d, in_=s_)), writes=stage.r(st * 1024, (st + 1) * 1024), dma="ld%d" % st)
            for hb in range(2):
                bk = 6 + hb

                def tr(pe, st=st, hb=hb, bk=bk):
                    ins = None
                    for kk in range(4):
                        k = hb * 4 + kk
                        ins = pe.transpose(self.bank(bk, 128, kk * 128), stage.t[:, st, k * 128:(k + 1) * 128], self.identf.t[:])
                    return ins
                P.add("pe", tr, reads=stage.r(st * 1024 + hb * 512, st * 1024 + hb * 512 + 512) + self.identf.r(), writes=PSR(bk))
                dst = XT.t[:, hb * 4:(hb + 1) * 4, t * 128:(t + 1) * 128]
                src_ps = self.bank(bk).rearrange("p (k c) -> p k c", k=4)
                wr = []
                for kk in range(4):
                    k = hb * 4 + kk
                    wr += XT.r(k * S + t * 128, k * S + (t + 1) * 128)
                eng = "dve"
                if eng == "act":
                    P.add("act", (lambda a, d=dst, s_=src_ps: a.activation(out=d, in_=s_, func=AF.Copy)), reads=PSR(bk), writes=wr)
                else:
                    P.add("dve", (lambda v, d=dst, s_=src_ps: v.tensor_copy(out=d, in_=s_)), reads=PSR(bk), writes=wr)

    def store_seq(self, s, stage):
        P = self.P
        XT = self.XT
        for t in range(16):
            st = t % 2
            for hb in range(2):
                bk = 6 + hb

                def tr(pe, t=t, hb=hb, bk=bk):
                    ins = None
                    for kk in range(4):
                        k = hb * 4 + kk
                        ins = pe.transpose(self.bank(bk, 128, kk * 128), XT.t[:, k, t * 128:(t + 1) * 128], self.identf.t[:])
                    return ins
                rd = []
                for kk in range(4):
                    k = hb * 4 + kk
                    rd += XT.r(k * S + t * 128, k * S + (t + 1) * 128)
                P.add("pe", tr, reads=rd + self.identf.r(), writes=PSR(bk))
                dst = stage.t[:, st, hb * 512:(hb + 1) * 512]
                src_ps = self.bank(bk)
                wr = stage.r(st * 1024 + hb * 512, st * 1024 + hb * 512 + 512)
                if False:
                    P.add("act", (lambda a, d=dst, s_=src_ps: a.activation(out=d, in_=s_, func=AF.Copy)), reads=PSR(bk), writes=wr)
                else:
                    P.add("dve", (lambda v, d=dst, s_=src_ps: v.tensor_copy(out=d, in_=s_)), reads=PSR(bk), writes=wr)
            dst = self.dr["y"][s, t * 128:(t + 1) * 128, :]
            srcv = stage.t[:, st, :]
            P.add("sp", (lambda e, d=dst, s_=srcv: e.dma_start(out=d, in_=s_)), reads=stage.r(st * 1024, (st + 1) * 1024), dma="st%d" % st)

    def stats_begin(self):
        pass

    def rstd_from_sums(self, sum_bank, n, tmp, rstd, width, eps, o=0):
        P = self.P
        tb, to = tmp
        rb, ro = rstd
        P.add("act", (lambda a: a.activation(out=tb.t[:, to:to + n], in_=self.bank(sum_bank, n, o), func=AF.Sqrt, scale=1.0 / width, bias=self.epsb(eps))),
              reads=PSR(sum_bank) + self.epst.r(), writes=tb.r(to, to + n))
        P.add("dve", (lambda v: v.reciprocal(out=rb.t[:, ro:ro + n], in_=tb.t[:, to:to + n])),
              reads=tb.r(to, to + n), writes=rb.r(ro, ro + n))

    def epsb(self, eps):
        return self.epst.t[:, 0:1] if eps == 1e-6 else self.epst.t[:, 1:2]

    def prenorm_block(self, tb, gname, xnT, xn_off, sq, tmp, rstd, sbank):
        P = self.P
        XT = self.XT
        c0 = tb * 512
        for k in range(KC):
            sl = k % 2
            P.add("act", (lambda a, k=k, sl=sl: a.activation(out=sq.t[:, sl, :], in_=XT.t[:, k, c0:c0 + 512], func=AF.Square)),
                  reads=XT.r(k * S + c0, k * S + c0 + 512), writes=sq.r(sl * 512, sl * 512 + 512))
            P.add("pe", (lambda pe, k=k, sl=sl: pe.matmul(self.bank(sbank), lhsT=self.onesb.t[:], rhs=sq.t[:, sl, :], start=(k == 0), stop=(k == KC - 1))),
                  reads=sq.r(sl * 512, sl * 512 + 512) + self.onesb.r(), writes=PSR(sbank))
        self.rstd_from_sums(sbank, 512, (tmp, 0), (rstd, 0), float(D), 1e-6)
        xsh = xnT.shape[1]
        for k in range(KC):
            P.add("dve", (lambda v, k=k: v.scalar_tensor_tensor(out=xnT.t[:, k, xn_off:xn_off + 512], in0=XT.t[:, k, c0:c0 + 512], scalar=self.par(gname, k),
                                                                 in1=rstd.t[:, 0:512], op0=ALU.mult, op1=ALU.mult)),
                  reads=XT.r(k * S + c0, k * S + c0 + 512) + rstd.r(0, 512) + self.par_r(), writes=xnT.r(k * xsh + xn_off, k * xsh + xn_off + 512))

    def postnorm_block(self, tb, gname, fT, f_off, rstd, tscr):
        P = self.P
        XT = self.XT
        c0 = tb * 512
        fsh = fT.shape[1]
        for c in range(KC):
            sl = c % 2
            P.add("dve", (lambda v, c=c, sl=sl: v.scalar_tensor_tensor(out=tscr.t[:, sl, :], in0=fT.t[:, c, f_off:f_off + 512], scalar=self.par(gname, c),
                                                                        in1=rstd.t[:, 0:512], op0=ALU.mult, op1=ALU.mult)),
                  reads=fT.r(c * fsh + f_off, c * fsh + f_off + 512) + rstd.r(0, 512) + self.par_r(), writes=tscr.r(sl * 512, sl * 512 + 512))
            P.add("pool", (lambda g, c=c, sl=sl: g.tensor_tensor(out=XT.t[:, c, c0:c0 + 512], in0=XT.t[:, c, c0:c0 + 512], in1=tscr.t[:, sl, :], op=ALU.add)),
                  reads=tscr.r(sl * 512, sl * 512 + 512) + XT.r(c * S + c0, c * S + c0 + 512), writes=XT.r(c * S + c0, c * S + c0 + 512))

    def ffn(self, l):
        nc, P, dr = self.nc, self.P, self.dr
        n = "l%d_" % l
        off = self.big0

        def A(name, shape, dt):
            nonlocal off
            b = SB(nc, name + "_f%d_%d" % (l, self.uid()), shape, dt, off)
            off = (b.end() + GR - 1) // GR * GR
            return b

        xnT = A("xnT", [KC, 1024], BF16)
        hT = A("hT", [FC, 1024], BF16)
        fT = A("fT", [KC, 1024], F32)
        sq = A("sq", [2, 512], BF16)
        sg = A("sg", [2, 512], F32)
        tmp = A("tmp", [512], F32)
        rstd = A("rstd", [512], F32)
        rstd2 = [A("rstd2a", [512], F32), A("rstd2b", [512], F32)]
        tscr = A("tscr", [2, 512], F32)
        for half in range(2):
            for b in range(2):
                self.prenorm_block(half * 2 + b, n + "g_ffn_pre", xnT, b * 512, sq, tmp, rstd, 6 + b)
            cnt = 0
            for i in range(FC // 2):
                slot = self.wload([
                    (lambda t: t[:, 0:2048].rearrange("p (k c) -> p k c", k=8), lambda dr, i=i: dr[n + "w_gate_up"][:, i * 256:(i + 1) * 256].rearrange("(k p) c -> p k c", p=128)),
                    (lambda t: t[:, 2048:4096].rearrange("p (k c) -> p k c", k=8), lambda dr, i=i: dr[n + "w_gate_up"][:, DFF + i * 256:DFF + (i + 1) * 256].rearrange("(k p) c -> p k c", p=128)),
                ])
                for sub in range(2):
                    j = i * 2 + sub
                    st = cnt % 2
                    cnt += 1
                    gb = [st * 2 + 0, st * 2 + 1]
                    ub = [4 + (st * 2 + 0) % 2, 0]
                    ub = [4, 5]

                    def mm(pe, slot=slot, sub=sub, bb=gb, gi=0):
                        ins = None
                        wv = slot.t[:, :].rearrange("p (g k c) -> p g k c", g=2, k=8)
                        for k in range(KC):
                            for b in range(2):
                                ins = pe.matmul(self.bank(bb[b]), lhsT=wv[:, gi, k, sub * 128:(sub + 1) * 128], rhs=xnT.t[:, k, b * 512:(b + 1) * 512], start=(k == 0), stop=(k == KC - 1))
                        return ins
                    P.add("pe", mm, reads=slot.r() + xnT.r(), writes=PSR(gb[0], gb[1]))
                    P.add("pe", (lambda pe, slot=slot, sub=sub, ub=ub, mm=mm: mm(pe, slot, sub, ub, 1)), reads=slot.r() + xnT.r(), writes=PSR(ub[0], ub[1]))
                    for b in range(2):
                        P.add("act", (lambda a, b=b, gb=gb: a.activation(out=sg.t[:, b, :], in_=self.bank(gb[b]), func=AF.Silu)),
                              reads=PSR(gb[b]), writes=sg.r(b * 512, b * 512 + 512))
                        P.add("dve", (lambda v, b=b, ub=ub, j=j: v.tensor_tensor(out=hT.t[:, j, b * 512:(b + 1) * 512], in0=sg.t[:, b, :], in1=self.bank(ub[b]), op=ALU.mult)),
                              reads=PSR(ub[b]) + sg.r(b * 512, b * 512 + 512), writes=hT.r(j * 1024 + b * 512, j * 1024 + b * 512 + 512))
            for c in range(KC):
                slot = self.wload([
                    (lambda t: t[:, 0:FC * 128].rearrange("p (k c) -> p k c", k=FC), lambda dr, c=c: dr[n + "w_down"][:, c * 128:(c + 1) * 128].rearrange("(k p) c -> p k c", p=128)),
                ])
                fb = [(c % 2) * 2 + 0, (c % 2) * 2 + 1]

                def mm(pe, slot=slot, fb=fb):
                    ins = None
                    wv = slot.t[:, 0:FC * 128].rearrange("p (k c) -> p k c", k=FC)
                    for kf in range(FC):
                        for b in range(2):
                            ins = pe.matmul(self.bank(fb[b]), lhsT=wv[:, kf, :], rhs=hT.t[:, kf, b * 512:(b + 1) * 512], start=(kf == 0), stop=(kf == FC - 1))
                    return ins
                P.add("pe", mm, reads=slot.r() + hT.r(), writes=PSR(*fb))
                for b in range(2):
                    P.add("act", (lambda a, b=b, fb=fb, c=c: a.activation(out=fT.t[:, c, b * 512:(b + 1) * 512], in_=self.bank(fb[b]), func=AF.Copy)),
                          reads=PSR(fb[b]), writes=fT.r(c * 1024 + b * 512, c * 1024 + b * 512 + 512))
                    P.add("act", (lambda a, b=b, fb=fb: a.activation(out=sq.t[:, b, :], in_=self.bank(fb[b]), func=AF.Square)),
                          reads=PSR(fb[b]), writes=sq.r(b * 512, b * 512 + 512))
                    P.add("pe", (lambda pe, b=b, c=c: pe.matmul(self.bank(6 + b), lhsT=self.onesb.t[:], rhs=sq.t[:, b, :], start=(c == 0), stop=(c == KC - 1))),
                          reads=sq.r(b * 512, b * 512 + 512) + self.onesb.r(), writes=PSR(6 + b))
            for b in range(2):
                self.rstd_from_sums(6 + b, 512, (tmp, 0), (rstd2[b], 0), float(D), 1e-6)
                self.postnorm_block(half * 2 + b, n + "g_ffn_post", fT, b * 512, rstd2[b], tscr)

    _uid = 0

    def uid(self):
        Builder._uid += 1
        return Builder._uid


class SBview:
    def __init__(self, sb, b):
        self.sb = sb
        self.b = b
        self.t = sb.t[:, b, :]

    def r(self, lo=0, hi=512):
        return self.sb.r(self.b * 512 + lo, self.b * 512 + hi)


def _record(B, nseq, layers, do_ffn, do_mix):
    B.declare_dram()
    B.alloc_persistent()
    B.init_consts()
    stage = SB(B.nc, "stage%d" % B.uid(), [2, 1024], F32, B.big0)
    stage_l = SB(B.nc, "stagel%d" % B.uid(), [2, 1024], F32, B.big0 + 8192)
    for s in range(nseq):
        B.load_seq(s, stage_l)
        for l in layers:
            if do_mix:
                B.mixer(l)
            if do_ffn:
                B.ffn(l)
            B.P.barrier()
        B.store_seq(s, stage)
        B.P.barrier()


def build_program(nseq=SEQ_PER_CORE, layers=(0, 1, 2, 3), do_ffn=True, do_mix=True):
    B0 = Builder(nseq, layers, do_ffn, do_mix)
    B0.plan_mode = True
    B0.P.dry = True
    _record(B0, nseq, layers, do_ffn, do_mix)
    B = Builder(nseq, layers, do_ffn, do_mix)
    B.plan = B0.plan
    _record(B, nseq, layers, do_ffn, do_mix)
    B.P.emit()
    return B


def make_in_maps(inputs, nseq=SEQ_PER_CORE, ncores=NCORES):
    xp = np.asarray(inputs["x_prompt"], np.float32)
    xs = np.asarray(inputs["x_sample"], np.float32)
    params, _ = pack_params(inputs)
    consts = const_inputs()
    maps = []
    for c in range(ncores):
        xc = np.concatenate([xp[c:c + 1], xs[4 * c:4 * c + 4]], axis=0)[:nseq]
        m = {"x": np.ascontiguousarray(xc), "params": params}
        m.update(consts)
        for l in range(DEPTH):
            n = "l%d_" % l
            kind = l % 3
            names = ["w_gate_up", "w_down"]
            if kind == 0:
                names += ["w_qkv", "w_o"]
            elif kind == 1:
                names += ["w_pw1", "w_pw2"]
            else:
                names += ["w_uv", "w_s", "b_s", "w_o"]
            for nm in names:
                m[n + nm] = np.ascontiguousarray(np.asarray(inputs[n + nm], np.float32))
        maps.append(m)
    return maps


_CACHE = {}


def kernel(**inputs):
    if "B" not in _CACHE:
        _CACHE["B"] = build_program()
    B = _CACHE["B"]
    maps = make_in_maps(inputs)
    res = run_bass_kernel_spmd(B.nc, maps, core_ids=list(range(NCORES)))
    ys = [np.asarray(r["y"], np.float32) for r in res.results]
    y_prompt = np.stack([ys[c][0] for c in range(NCORES)], axis=0)
    y_sample = np.concatenate([ys[c][1:5] for c in range(NCORES)], axis=0)
    return (y_prompt, y_sample)


def _alloc(self, tag):
    nc = self.nc
    state = {"off": self.big0}

    def A(name, shape, dt):
        b = SB(nc, "%s_%s_%d" % (name, tag, self.uid()), shape, dt, state["off"])
        state["off"] = (b.end() + GR - 1) // GR * GR
        return b
    A.state = state
    return A


def _proj_feature_major(self, wname, ncols_total, col0, nchunks, inT, in_sh, tok0, nblk, banks, evac):
    P = self.P
    bi = 0
    for pi in range((nchunks + 3) // 4):
        nsub = min(4, nchunks - pi * 4)
        c0 = col0 + pi * 512
        slot = self.wload([
            (lambda t, nsub=nsub: t[:, 0:8 * nsub * 128].rearrange("p (k c) -> p k c", k=8),
             lambda dr, c0=c0, nsub=nsub: dr[wname][:, c0:c0 + nsub * 128].rearrange("(k p) c -> p k c", p=128)),
        ])
        for sub in range(nsub):
            oc = pi * 4 + sub
            for b in range(nblk):
                bk = banks[bi % len(banks)]
                bi += 1

                def mm(pe, slot=slot, sub=sub, b=b, bk=bk, nsub=nsub):
                    ins = None
                    wv = slot.t[:, 0:8 * nsub * 128].rearrange("p (k c) -> p k c", k=8)
                    for k in range(KC):
                        ins = pe.matmul(self.bank(bk), lhsT=wv[:, k, sub * 128:(sub + 1) * 128], rhs=inT.t[:, k, tok0 + b * 512: tok0 + (b + 1) * 512], start=(k == 0), stop=(k == KC - 1))
                    return ins
                rd = []
                for k in range(KC):
                    rd += inT.r(k * in_sh + tok0 + b * 512, k * in_sh + tok0 + (b + 1) * 512)
                P.add("pe", mm, reads=slot.r() + rd, writes=PSR(bk))
                evac(oc, b, bk)


def _out_proj_postnorm(self, wname, bias_name, gpost, inT, in_sh, half, mT, sq, tmp, rstd2, tscr):
    P = self.P

    def evac(oc, b, bk):
        if bias_name is not None:
            P.add("act", (lambda a: a.activation(out=mT.t[:, oc, b * 512:(b + 1) * 512], in_=self.bank(bk), func=AF.Identity, bias=self.par(bias_name, oc))),
                  reads=PSR(bk) + self.par_r(), writes=mT.r(oc * 1024 + b * 512, oc * 1024 + (b + 1) * 512))
        else:
            P.add("act", (lambda a: a.activation(out=mT.t[:, oc, b * 512:(b + 1) * 512], in_=self.bank(bk), func=AF.Copy)),
                  reads=PSR(bk), writes=mT.r(oc * 1024 + b * 512, oc * 1024 + (b + 1) * 512))
        P.add("act", (lambda a: a.activation(out=sq.t[:, b, :], in_=mT.t[:, oc, b * 512:(b + 1) * 512], func=AF.Square)),
              reads=mT.r(oc * 1024 + b * 512, oc * 1024 + (b + 1) * 512), writes=sq.r(b * 512, b * 512 + 512))
        P.add("pe", (lambda pe: pe.matmul(self.bank(6 + b), lhsT=self.onesb.t[:], rhs=sq.t[:, b, :], start=(oc == 0), stop=(oc == KC - 1))),
              reads=sq.r(b * 512, b * 512 + 512) + self.onesb.r(), writes=PSR(6 + b))
    self.proj_fm(wname, D, 0, KC, inT, in_sh, half * 1024, 2, [0, 1, 2, 3], evac)
    for b in range(2):
        self.rstd_from_sums(6 + b, 512, (tmp, 0), (rstd2[b], 0), float(D), 1e-6)
        self.postnorm_block(half * 2 + b, gpost, mT, b * 512, rstd2[b], tscr)


def _ln_stats(self, s1bank, s2bank, m, msq, tmp, rstd, eps):
    P = self.P
    P.add("dve", (lambda v: v.tensor_scalar(out=m.t[:, 0:512], in0=self.bank(s1bank), scalar1=1.0 / D, scalar2=None, op0=ALU.mult)),
          reads=PSR(s1bank), writes=m.r())
    P.add("pool", (lambda g: g.tensor_tensor(out=msq.t[:, 0:512], in0=m.t[:, 0:512], in1=m.t[:, 0:512], op=ALU.mult)),
          reads=m.r(), writes=msq.r())
    P.add("dve", (lambda v: v.scalar_tensor_tensor(out=tmp.t[:, 0:512], in0=self.bank(s2bank), scalar=1.0 / D, in1=msq.t[:, 0:512], op0=ALU.mult, op1=ALU.subtract)),
          reads=PSR(s2bank) + msq.r(), writes=tmp.r())
    P.add("act", (lambda a: a.activation(out=tmp.t[:, 0:512], in_=tmp.t[:, 0:512], func=AF.Sqrt, bias=self.epsb(eps))),
          reads=tmp.r() + self.epst.r(), writes=tmp.r())
    P.add("dve", (lambda v: v.reciprocal(out=rstd.t[:, 0:512], in_=tmp.t[:, 0:512])),
          reads=tmp.r(), writes=rstd.r())


GELU_C = 0.7978845608028654


def _gelu_evac(self, bk, bias_ap, out_ap, out_r, scr):
    P = self.P
    P.add("act", (lambda a: a.activation(out=out_ap, in_=self.bank(bk), func=AF.Gelu_apprx_tanh, bias=bias_ap)),
          reads=PSR(bk) + self.par_r(), writes=out_r)


def _mixer(self, l):
    kind = l % 3
    if kind == 0:
        self.attn(l)
    elif kind == 1:
        self.conv(l)
    else:
        self.sgate(l)


def _sgate(self, l):
    P, dr = self.P, self.dr
    n = "l%d_" % l
    H = 1024
    A = self.alloc("sg%d" % l)
    xnT = A("xnT", [KC, H], BF16)
    uT = A("uT", [KC, H], BF16)
    vT = A("vT", [KC, H], BF16)
    mT = A("mT", [KC, H], F32)
    sq = A("sq", [2, 512], BF16)
    tmp = A("tmp", [512], F32)
    rstd = A("rstd", [512], F32)
    rstd2 = [A("rstd2a", [512], F32), A("rstd2b", [512], F32)]
    tscr = A("tscr", [2, 512], F32)
    gscr = [A("gscr0", [3, 512], F32)]
    mm_ = A("m", [512], F32)
    msq = A("msq", [512], F32)
    bs = A("bs", [8, 128], F32)
    wsT = A("wsT", [8, 128], BF16)
    vtok = A("vtok", [2, 8, 128], BF16)
    wsP = SB(self.nc, "wsP_sg%d_%d" % (l, self.uid()), [8, 128], BF16, vtok.off)
    mixs = SB(self.nc, "mixs_sg%d_%d" % (l, self.uid()), [2, 512], F32, gscr[0].off)

    P.add("sp", (lambda e: e.dma_start(out=bs.t[:].rearrange("p g q -> p (g q)"), in_=dr[n + "b_s"].rearrange("g q -> (g q)").partition_broadcast(128))),
          writes=bs.r(), dma="sgc0")
    P.add("pool", (lambda g: g.dma_start(out=wsP.t[:], in_=dr[n + "w_s"].rearrange("g p q -> p g q"))), writes=wsP.r(), dma="sgc1")

    def trw(pe):
        ins = None
        for g in range(8):
            ins = pe.transpose(self.bank(4, 512).bitcast(BF16)[:, g * 128:(g + 1) * 128], wsP.t[:, g, :], self.identb.t[:])
        return ins
    P.add("pe", trw, reads=wsP.r() + self.identb.r(), writes=PSR(4))
    P.add("dve", (lambda v: v.tensor_copy(out=wsT.t[:].rearrange("p g q -> p (g q)"), in_=self.bank(4, 512).bitcast(BF16))), reads=PSR(4), writes=wsT.r())

    for half in range(2):
        for b in range(2):
            self.prenorm_block(half * 2 + b, n + "g_mix_pre", xnT, b * 512, sq, tmp, rstd, 6 + b)

        def evac_uv(oc, b, bk):
            dst = uT if oc < 8 else vT
            c = oc % 8
            self.gelu_evac(bk, self.par(n + "b_uv", oc), dst.t[:, c, b * 512:(b + 1) * 512], dst.r(c * H + b * 512, c * H + (b + 1) * 512), gscr[0])
        self.proj_fm(n + "w_uv", 2 * D, 0, 16, xnT, H, 0, 2, [0, 1, 2, 3], evac_uv)

        for b in range(2):
            c0 = b * 512
            s1, s2 = 4 + b, 6 + b
            for c in range(KC):
                sl = c % 2
                P.add("act", (lambda a, c=c, sl=sl, c0=c0: a.activation(out=sq.t[:, sl, :], in_=vT.t[:, c, c0:c0 + 512], func=AF.Square)),
                      reads=vT.r(c * H + c0, c * H + c0 + 512), writes=sq.r(sl * 512, sl * 512 + 512))
                P.add("pe", (lambda pe, c=c, sl=sl, s2=s2: pe.matmul(self.bank(s2), lhsT=self.onesb.t[:], rhs=sq.t[:, sl, :], start=(c == 0), stop=(c == KC - 1))),
                      reads=sq.r(sl * 512, sl * 512 + 512) + self.onesb.r(), writes=PSR(s2))
                P.add("pe", (lambda pe, c=c, s1=s1, c0=c0: pe.matmul(self.bank(s1), lhsT=self.onesb.t[:], rhs=vT.t[:, c, c0:c0 + 512], start=(c == 0), stop=(c == KC - 1))),
                      reads=vT.r(c * H + c0, c * H + c0 + 512) + self.onesb.r(), writes=PSR(s1))
            self.ln_stats(s1, s2, mm_, msq, tmp, rstd, 1e-5)
            for c in range(KC):
                sl = c % 2
                rr = vT.r(c * H + c0, c * H + c0 + 512)
                P.add("dve", (lambda v, c=c, sl=sl, c0=c0: v.tensor_tensor(out=tscr.t[:, sl, :], in0=vT.t[:, c, c0:c0 + 512], in1=mm_.t[:, 0:512], op=ALU.subtract)),
                      reads=rr + mm_.r(), writes=tscr.r(sl * 512, sl * 512 + 512))
                P.add("pool", (lambda g, sl=sl: g.tensor_tensor(out=tscr.t[:, sl, :], in0=tscr.t[:, sl, :], in1=rstd.t[:, 0:512], op=ALU.mult)),
                      reads=tscr.r(sl * 512, sl * 512 + 512) + rstd.r(), writes=tscr.r(sl * 512, sl * 512 + 512))
                P.add("dve", (lambda v, c=c, sl=sl, c0=c0: v.tensor_scalar(out=vT.t[:, c, c0:c0 + 512], in0=tscr.t[:, sl, :], scalar1=self.par(n + "g_sln", c), scalar2=self.par(n + "b_sln", c), op0=ALU.mult, op1=ALU.add)),
                      reads=tscr.r(sl * 512, sl * 512 + 512) + self.par_r(), writes=rr)

        for ch in range(8):
            t0 = ch * 128
            vs = ch % 2
            tbk = 4 + ch % 2

            def trv(pe, t0=t0, tbk=tbk):
                ins = None
                for g in range(8):
                    ins = pe.transpose(self.bank(tbk, 512).bitcast(BF16)[:, g * 128:(g + 1) * 128], vT.t[:, g, t0:t0 + 128], self.identb.t[:])
                return ins
            rd = []
            for g in range(8):
                rd += vT.r(g * H + t0, g * H + t0 + 128)
            P.add("pe", trv, reads=rd + self.identb.r(), writes=PSR(tbk))
            P.add("dve", (lambda v, vs=vs, tbk=tbk: v.tensor_copy(out=vtok.t[:, vs].rearrange("p g q -> p (g q)"), in_=self.bank(tbk, 512).bitcast(BF16))),
                  reads=PSR(tbk), writes=vtok.r(vs * 1024, vs * 1024 + 1024))
            for gh in range(2):
                obk = (ch % 2) * 2 + gh

                def mmx(pe, vs=vs, gh=gh, obk=obk):
                    ins = None
                    for gg in range(4):
                        g = gh * 4 + gg
                        ins = pe.matmul(self.bank(obk, 128, gg * 128), lhsT=vtok.t[:, vs, g, :], rhs=wsT.t[:, g, :], start=True, stop=True)
                    return ins
                P.add("pe", mmx, reads=vtok.r(vs * 1024, vs * 1024 + 1024) + wsT.r(), writes=PSR(obk))
                P.add("dve", (lambda v, gh=gh, obk=obk: v.tensor_tensor(out=mixs.t[:, gh, :], in0=self.bank(obk), in1=bs.t[:, gh * 4:(gh + 1) * 4, :].rearrange("p g q -> p (g q)"), op=ALU.add)),
                      reads=PSR(obk) + bs.r(), writes=mixs.r(gh * 512, gh * 512 + 512))
                ur = []
                for gg in range(4):
                    g = gh * 4 + gg
                    ur += uT.r(g * H + t0, g * H + t0 + 128)
                P.add("pool", (lambda g_, gh=gh, t0=t0: g_.tensor_tensor(out=uT.t[:, gh * 4:(gh + 1) * 4, t0:t0 + 128], in0=uT.t[:, gh * 4:(gh + 1) * 4, t0:t0 + 128],
                                                                        in1=mixs.t[:, gh, :].rearrange("p (g q) -> p g q", g=4), op=ALU.mult)),
                      reads=mixs.r(gh * 512, gh * 512 + 512) + ur, writes=ur)

        self.out_proj_postnorm(n + "w_o", n + "b_o", n + "g_mix_post", _Shift(uT, half * 1024), H, half, mT, sq, tmp, rstd2, tscr)


ZW = S + 32


def _conv(self, l):
    P, dr = self.P, self.dr
    n = "l%d_" % l
    A = self.alloc("cv%d" % l)
    xnT = A("xnT", [KC, S], BF16)
    zpad = A("zpad", [KC, ZW], BF16)
    Dc = A("Dc", [2, 31, 128], BF16)
    sT = A("sT", [KC, 1024], BF16)
    sq = A("sq", [2, 512], BF16)
    ybf = A("ybf", [2, 512], BF16)
    tmp = A("tmp", [512], F32)
    rstd = A("rstd", [512], F32)
    rstd2 = [A("rstd2a", [512], F32), A("rstd2b", [512], F32)]
    tscr = A("tscr", [2, 512], F32)
    mm_ = A("m", [512], F32)
    msq = A("msq", [512], F32)
    sig = SB(self.nc, "sig_cv%d_%d" % (l, self.uid()), [2, 512], F32, tscr.off)
    yT = SB(self.nc, "yT_cv%d_%d" % (l, self.uid()), [KC, 1024], F32, xnT.off)
    mT = yT

    for c in range(KC):
        P.add("pool", (lambda g, c=c: g.memset(zpad.t[:, c, 0:16], 0.0)), writes=zpad.r(c * ZW, c * ZW + 16))
        P.add("pool", (lambda g, c=c: g.memset(zpad.t[:, c, 16 + S:ZW], 0.0)), writes=zpad.r(c * ZW + 16 + S, (c + 1) * ZW))

    for tb in range(NTB):
        self.prenorm_block(tb, n + "g_mix_pre", xnT, tb * 512, sq, tmp, rstd, 6 + tb % 2)

    for pi in range(4):
        slot = self.wload([
            (lambda t: t[:, 0:2048].rearrange("p (k c) -> p k c", k=8), lambda dr, pi=pi: dr[n + "w_pw1"][:, pi * 256:(pi + 1) * 256].rearrange("(k p) c -> p k c", p=128)),
            (lambda t: t[:, 2048:4096].rearrange("p (k c) -> p k c", k=8), lambda dr, pi=pi: dr[n + "w_pw1"][:, D + pi * 256:D + (pi + 1) * 256].rearrange("(k p) c -> p k c", p=128)),
        ])
        for sub in range(2):
            c = pi * 2 + sub
            for tb in range(NTB):
                st = tb % 2
                ab, gb = st * 2, st * 2 + 1

                def mm(pe, slot=slot, sub=sub, tb=tb, bk=ab, gi=0):
                    ins = None
                    wv = slot.t[:, :].rearrange("p (g k c) -> p g k c", g=2, k=8)
                    for k in range(KC):
                        ins = pe.matmul(self.bank(bk), lhsT=wv[:, gi, k, sub * 128:(sub + 1) * 128], rhs=xnT.t[:, k, tb * 512:(tb + 1) * 512], start=(k == 0), stop=(k == KC - 1))
                    return ins
                rd = []
                for k in range(KC):
                    rd += xnT.r(k * S + tb * 512, k * S + (tb + 1) * 512)
                P.add("pe", mm, reads=slot.r() + rd, writes=PSR(ab))
                P.add("pe", (lambda pe, slot=slot, sub=sub, tb=tb, gb=gb, mm=mm: mm(pe, slot, sub, tb, gb, 1)), reads=slot.r() + rd, writes=PSR(gb))
                P.add("act", (lambda a, st=st, gb=gb, c=c: a.activation(out=sig.t[:, st, :], in_=self.bank(gb), func=AF.Sigmoid, bias=self.par(n + "b_pw1", 8 + c))),
                      reads=PSR(gb) + self.par_r(), writes=sig.r(st * 512, st * 512 + 512))
                zr = zpad.r(c * ZW + 16 + tb * 512, c * ZW + 16 + (tb + 1) * 512)
                P.add("dve", (lambda v, st=st, ab=ab, c=c, tb=tb: v.scalar_tensor_tensor(out=zpad.t[:, c, 16 + tb * 512:16 + (tb + 1) * 512], in0=self.bank(ab), scalar=self.par(n + "b_pw1", c),
                                                                                     in1=sig.t[:, st, :], op0=ALU.add, op1=ALU.mult)),
                      reads=PSR(ab) + sig.r(st * 512, st * 512 + 512) + self.par_r(), writes=zr)

    for half in range(2):
        for c in range(KC):
            ds = c % 2

            def mkd(v, c=c, ds=ds):
                ins = None
                for j in range(31):
                    ins = v.tensor_scalar(out=Dc.t[:, ds, j, :], in0=self.identb.t[:], scalar1=self.par(n + "w_dw", c * 31 + j), scalar2=None, op0=ALU.mult)
                return ins
            P.add("dve", mkd, reads=self.identb.r() + self.par_r(), writes=Dc.r(ds * 31 * 128, (ds + 1) * 31 * 128))
            for b in range(2):
                tb = half * 2 + b
                bk = (c * 2 + b) % 4

                def mmc(pe, c=c, ds=ds, tb=tb, bk=bk):
                    ins = None
                    for j in range(31):
                        ins = pe.matmul(self.bank(bk), lhsT=Dc.t[:, ds, j, :], rhs=zpad.t[:, c, tb * 512 + j + 1: tb * 512 + j + 1 + 512], start=(j == 0), stop=(j == 30))
                    return ins
                P.add("pe", mmc, reads=Dc.r(ds * 31 * 128, (ds + 1) * 31 * 128) + zpad.r(c * ZW + tb * 512, c * ZW + tb * 512 + 512 + 32), writes=PSR(bk))
                yr = yT.r(c * 1024 + b * 512, c * 1024 + (b + 1) * 512)
                P.add("act", (lambda a, c=c, b=b, bk=bk: a.activation(out=yT.t[:, c, b * 512:(b + 1) * 512], in_=self.bank(bk), func=AF.Identity, bias=self.par(n + "b_dw", c))),
                      reads=PSR(bk) + self.par_r(), writes=yr)
                P.add("act", (lambda a, c=c, b=b: a.activation(out=sq.t[:, b, :], in_=yT.t[:, c, b * 512:(b + 1) * 512], func=AF.Square)),
                      reads=yr, writes=sq.r(b * 512, b * 512 + 512))
                P.add("act", (lambda a, c=c, b=b: a.activation(out=ybf.t[:, b, :], in_=yT.t[:, c, b * 512:(b + 1) * 512], func=AF.Copy)),
                      reads=yr, writes=ybf.r(b * 512, b * 512 + 512))
                P.add("pe", (lambda pe, c=c, b=b: pe.matmul(self.bank(6 + b), lhsT=self.onesb.t[:], rhs=sq.t[:, b, :], start=(c == 0), stop=(c == KC - 1))),
                      reads=sq.r(b * 512, b * 512 + 512) + self.onesb.r(), writes=PSR(6 + b))
                P.add("pe", (lambda pe, c=c, b=b: pe.matmul(self.bank(4 + b), lhsT=self.onesb.t[:], rhs=ybf.t[:, b, :], start=(c == 0), stop=(c == KC - 1))),
                      reads=ybf.r(b * 512, b * 512 + 512) + self.onesb.r(), writes=PSR(4 + b))
        for b in range(2):
            self.ln_stats(4 + b, 6 + b, mm_, msq, tmp, rstd, 1e-5)
            for c in range(KC):
                sl = c % 2
                yr = yT.r(c * 1024 + b * 512, c * 1024 + (b + 1) * 512)
                P.add("dve", (lambda v, c=c, sl=sl, b=b: v.tensor_tensor(out=tscr.t[:, sl, :], in0=yT.t[:, c, b * 512:(b + 1) * 512], in1=mm_.t[:, 0:512], op=ALU.subtract)),
                      reads=yr + mm_.r(), writes=tscr.r(sl * 512, sl * 512 + 512))
                P.add("pool", (lambda g, sl=sl: g.tensor_tensor(out=tscr.t[:, sl, :], in0=tscr.t[:, sl, :], in1=rstd.t[:, 0:512], op=ALU.mult)),
                      reads=tscr.r(sl * 512, sl * 512 + 512) + rstd.r(), writes=tscr.r(sl * 512, sl * 512 + 512))
                P.add("act", (lambda a, c=c, sl=sl, b=b: a.activation(out=sT.t[:, c, b * 512:(b + 1) * 512], in_=tscr.t[:, sl, :], func=AF.Silu, scale=self.par(n + "g_cln", c), bias=self.par(n + "b_cln", c))),
                      reads=tscr.r(sl * 512, sl * 512 + 512) + self.par_r(), writes=sT.r(c * 1024 + b * 512, c * 1024 + (b + 1) * 512))
        self.out_proj_postnorm(n + "w_pw2", n + "b_pw2", n + "g_mix_post", _Shift(sT, half * 1024), 1024, half, mT, sq, tmp, rstd2, tscr)


class _Shift:
    def __init__(self, sb, tok0):
        self.sb = sb
        self.tok0 = tok0
        self.t = _ShiftT(sb.t, tok0)

    def r(self, lo, hi):
        return self.sb.r(lo - self.tok0, hi - self.tok0)


class _ShiftT:
    def __init__(self, t, tok0):
        self._t = t
        self.tok0 = tok0

    def __getitem__(self, key):
        p, k, sl = key
        return self._t[p, k, sl.start - self.tok0: sl.stop - self.tok0]


Builder.alloc = _alloc
Builder.proj_fm = _proj_feature_major
Builder.out_proj_postnorm = _out_proj_postnorm
Builder.ln_stats = _ln_stats
Builder.gelu_evac = _gelu_evac
Builder.mixer = _mixer
Builder.sgate = _sgate
Builder.conv = _conv


def _attn(self, l):
    P, dr = self.P, self.dr
    nc = self.nc
    n = "l%d_" % l
    lam_init = 0.8 - 0.6 * math.exp(-0.3 * l)
    A = self.alloc("at%d" % l)
    xnT = A("xnT", [KC, S], BF16)
    oT = A("oT", [KC, S], BF16)
    qT = A("qT", [S], BF16)
    kT = A("kT", [S], BF16)
    V = A("V", [16, 128], BF16)
    cosT = A("cosT", [S], BF16)
    sinT = A("sinT", [S], BF16)
    u1 = A.state["off"]
    sq = A("sq", [2, 512], BF16)
    tmp = A("tmp", [512], F32)
    rstd = A("rstd", [512], F32)
    rstd2 = [A("rstd2a", [512], F32), A("rstd2b", [512], F32)]
    A.state["off"] = u1
    E = [A("E%d" % i, [1024], BF16) for i in range(3)]
    raw = [A("raw%d" % i, [512], BF16) for i in range(2)]
    t1s = [A("t1a", [512], F32), A("t1b", [512], F32)]
    t2s = [A("t2a", [512], F32), A("t2b", [512], F32)]
    sl1, sl2 = t1s, t2s
    u2 = A.state["off"]
    tscr = A("tscr", [2, 512], F32)
    A.state["off"] = u2
    r1 = A("r1", [512], F32)
    r2 = A("r2", [512], F32)
    a1 = A("a1", [512], F32)
    a2 = A("a2", [512], F32)
    oh = A("oh", [S], F32)
    osq = A("osq", [S], BF16)
    ssb = SB(nc, "ssb_at%d_%d" % (l, self.uid()), [512], F32, t1s[0].off)
    sm = A("sm", [64], F32)
    gsub = A("gsub", [4], F32)
    mT = SB(nc, "mT_at%d_%d" % (l, self.uid()), [KC, 1024], F32, xnT.off)
    smr = sm.r()
    scale = 64 ** -0.5
    DBG = 9
    _Padd = P.add
    if DBG <= -3:
        P.add = lambda *a, **k: None

    P.add("pool", (lambda g: g.dma_start(out=cosT.t[:], in_=dr["c_cos"])), writes=cosT.r(), dma="rp0")
    P.add("pool", (lambda g: g.dma_start(out=sinT.t[:], in_=dr["c_sin"])), writes=sinT.r(), dma="rp1")
    P.add("dve", (lambda v: v.scalar_tensor_tensor(out=a1.t[:, 0:64], in0=self.par(n + "lam_q1", 0, 64), scalar=1.0, in1=self.par(n + "lam_k1", 0, 64), op0=ALU.mult, op1=ALU.mult, accum_out=sm.t[:, 0:1])),
          reads=self.par_r(), writes=smr + a1.r())
    P.add("dve", (lambda v: v.scalar_tensor_tensor(out=a1.t[:, 0:64], in0=self.par(n + "lam_q2", 0, 64), scalar=1.0, in1=self.par(n + "lam_k2", 0, 64), op0=ALU.mult, op1=ALU.mult, accum_out=sm.t[:, 1:2])),
          reads=self.par_r() + smr, writes=smr + a1.r())
    P.add("act", (lambda a: a.activation(out=sm.t[:, 2:4], in_=sm.t[:, 0:2], func=AF.Exp)), reads=smr, writes=smr)
    P.add("dve", (lambda v: v.scalar_tensor_tensor(out=sm.t[:, 4:5], in0=sm.t[:, 2:3], scalar=lam_init, in1=sm.t[:, 3:4], op0=ALU.add, op1=ALU.subtract)), reads=smr, writes=smr)
    P.add("dve", (lambda v: v.tensor_scalar(out=sm.t[:, 5:6], in0=sm.t[:, 4:5], scalar1=-1.0, scalar2=None, op0=ALU.mult)), reads=smr, writes=smr)
    P.add("dve", (lambda v: v.tensor_scalar(out=gsub.t[:, 0:1], in0=self.par(n + "g_subln", 0, 1), scalar1=1.0 - lam_init, scalar2=None, op0=ALU.mult)),
          reads=self.par_r(), writes=gsub.r())

    P.add = _Padd
    for tb in range(NTB):
        self.prenorm_block(tb, n + "g_mix_pre", xnT, tb * 512, sq, tmp, rstd, 6 + tb % 2)

    estep = 0
    pending = [None]
    for h in range(8):
        slot = self.wload([
            (lambda t, w=w: t[:, 0:3072].rearrange("p (k w c) -> p k w c", k=8, w=3)[:, :, w, :],
             lambda dr, w=w, h=h: dr[n + "w_qkv"][:, w * D + h * 128: w * D + (h + 1) * 128].rearrange("(k p) c -> p k c", p=128))
            for w in range(3)
        ])
        wv = slot.t[:, 0:3072].rearrange("p (k w c) -> p k w c", k=8, w=3)
        def rec_v(tg):
            vb = 6 + tg % 2

            def mmv(pe, tg=tg, vb=vb, wv=wv):
                ins = None
                for tt in range(4):
                    tk = (tg * 4 + tt) * 128
                    for k in range(KC):
                        ins = pe.matmul(self.bank(vb, 128, tt * 128), lhsT=xnT.t[:, k, tk:tk + 128], rhs=wv[:, k, 2, :], start=(k == 0), stop=(k == KC - 1))
                return ins
            xr = []
            for k in range(KC):
                xr += xnT.r(k * S + tg * 512, k * S + tg * 512 + 512)
            P.add("pe", mmv, reads=slot.r() + xr, writes=PSR(vb))
            P.add("dve", (lambda v, tg=tg, vb=vb: v.tensor_copy(out=V.t[:, tg * 4:(tg + 1) * 4, :].rearrange("p t e -> p (t e)"), in_=self.bank(vb))),
                  reads=PSR(vb), writes=V.r(tg * 512, tg * 512 + 512))

        blocks = [(w, dst, tb) for w, dst in ((0, qT), (1, kT)) for tb in range(NTB)] if DBG >= -1 else []

        def rec_mm(i):
            w, dst, tb = blocks[i]
            c0 = tb * 512
            pb = i % 4
            xr = []
            for k in range(KC):
                xr += xnT.r(k * S + c0, k * S + c0 + 512)

            def mm(pe, w=w, c0=c0, pb=pb, wv=wv):
                ins = None
                for k in range(KC):
                    ins = pe.matmul(self.bank(pb), lhsT=wv[:, k, w, :], rhs=xnT.t[:, k, c0:c0 + 512], start=(k == 0), stop=(k == KC - 1))
                return ins
            P.add("pe", mm, reads=slot.r() + xr, writes=PSR(pb))

        def rec_evac(i):
            w, dst, tb = blocks[i]
            c0 = tb * 512
            pb = i % 4
            rb = raw[i % 2]
            t1b = t1s[i % 2]
            P.add("dve", (lambda v: v.tensor_tensor(out=t1b.t[:, 0:512], in0=self.bank(pb), in1=cosT.t[:, c0:c0 + 512], op=ALU.mult)),
                  reads=PSR(pb) + cosT.r(c0, c0 + 512), writes=t1b.r())
            P.add("act", (lambda a: a.activation(out=rb.t[:, 0:512], in_=self.bank(pb), func=AF.Copy)), reads=PSR(pb) + t1b.r(), writes=rb.r())
            qb_ = 4 + i % 2
            P.add("pe", (lambda pe: pe.matmul(self.bank(qb_), lhsT=self.permb.t[:], rhs=rb.t[:, 0:512], start=True, stop=True)),
                  reads=rb.r() + self.permb.r(), writes=PSR(qb_))

        def rec_rope(i):
            w, dst, tb = blocks[i]
            c0 = tb * 512
            qb_ = 4 + i % 2
            t1b = t1s[i % 2]
            t2b = t2s[i % 2]
            P.add("dve", (lambda v: v.tensor_tensor(out=t2b.t[:, 0:512], in0=self.bank(qb_), in1=sinT.t[:, c0:c0 + 512], op=ALU.mult)),
                  reads=PSR(qb_) + sinT.r(c0, c0 + 512), writes=t2b.r())
            P.add("pool", (lambda g: g.tensor_tensor(out=dst.t[:, c0:c0 + 512], in0=t1b.t[:, 0:512], in1=t2b.t[:, 0:512], op=ALU.add)),
                  reads=t1b.r() + t2b.r(), writes=dst.r(c0, c0 + 512))

        nb = len(blocks)
        for i in range(nb + 2):
            if i < nb:
                rec_mm(i)

            if 0 <= i - 1 < nb:
                rec_evac(i - 1)
            if 0 <= i - 2 < nb:
                rec_rope(i - 2)
        for tg in range(4):
            rec_v(tg)
        steps = [(qt, kc) for qt in range(NTB if DBG >= 1 else 0) for kc in range(16)]

        def rec_scores(i):
            qt, kc = steps[i]
            st = (estep0 + i) % 2
            k0, q0 = kc * 128, qt * 512

            def sc(pe, st=st, k0=k0, q0=q0):
                ins = None
                for c in range(2):
                    ins = pe.matmul(self.bank(st * 2 + c), lhsT=kT.t[c * 64:(c + 1) * 64, k0:k0 + 128], rhs=qT.t[c * 64:(c + 1) * 64, q0:q0 + 512], start=True, stop=True)
                return ins
            P.add("pe", sc, reads=kT.r(k0, k0 + 128) + qT.r(q0, q0 + 512), writes=PSR(st * 2, st * 2 + 1))

        estep0 = estep
        if steps:
            rec_scores(0)
        for i, (qt, kc) in enumerate(steps):
            q0 = qt * 512
            st = (estep0 + i) % 2
            e = E[(estep0 + i) % 3]
            P.add("act", (lambda a, st=st, e=e: a.activation(out=e.t[:, 0:1024], in_=self.ps[:, st * 1024:(st + 1) * 1024], func=AF.Exp, scale=scale)),
                  reads=PSR(st * 2, st * 2 + 1), writes=e.r())
            if i + 1 < len(steps):
                rec_scores(i + 1)

            def av(pe, e=e, kc=kc):
                ins = None
                for c in range(2):
                    ins = pe.matmul(self.bank(4 + c), lhsT=V.t[:, kc, :], rhs=e.t[:, c * 512:(c + 1) * 512], start=(kc == 0), stop=(kc == 15))
                for c in range(2):
                    ins = pe.matmul(self.ps[c * 32:(c + 1) * 32, 6 * 512:7 * 512], lhsT=self.onesb.t[:, 0:32], rhs=e.t[:, c * 512:(c + 1) * 512], start=(kc == 0), stop=(kc == 15))
                return ins
            P.add("pe", av, reads=e.r() + V.r(kc * 128, kc * 128 + 128) + self.onesb.r(), writes=PSR(4, 5, 6))
            if kc != 15:
                continue
            P.add("dve", (lambda v: v.tensor_copy(out=ssb.t[0:64, 0:512], in_=self.ps[0:64, 6 * 512:7 * 512])), reads=PSR(6), writes=ssb.r())
            P.add("dve", (lambda v: v.tensor_copy(out=a1.t[:, 0:512], in_=self.bank(4))), reads=PSR(4), writes=a1.r())
            P.add("dve", (lambda v: v.tensor_copy(out=a2.t[:, 0:512], in_=self.bank(5))), reads=PSR(5), writes=a2.r())
            P.add("dve", (lambda v: v.reciprocal(out=ssb.t[0:64, 0:512], in_=ssb.t[0:64, 0:512])), reads=ssb.r(), writes=ssb.r())
            for c, rc in ((0, r1), (1, r2)):
                row = (qt % 2) * 2 + c
                P.add("sp", (lambda e_, c=c, row=row: e_.dma_start(out=dr["bscr"][row:row + 1, :], in_=ssb.t[c * 32:c * 32 + 1, 0:512])),
                      reads=ssb.r(), writes=[("dram", row)], dma="bw%d" % c)
                P.add("sp", (lambda e_, rc=rc, row=row: e_.dma_start(out=rc.t[:, 0:512], in_=dr["bscr"][row].partition_broadcast(128))),
                      reads=[("dram", row)], writes=rc.r(), dma="bc%d" % c)
            P.add("pool", (lambda v: v.tensor_tensor(out=a1.t[:, 0:512], in0=a1.t[:, 0:512], in1=r1.t[:, 0:512], op=ALU.mult)), reads=a1.r() + r1.r(), writes=a1.r())
            P.add("pool", (lambda v: v.tensor_tensor(out=a2.t[:, 0:512], in0=a2.t[:, 0:512], in1=r2.t[:, 0:512], op=ALU.mult)), reads=a2.r() + r2.r(), writes=a2.r())
            P.add("dve", (lambda g, q0=q0: g.scalar_tensor_tensor(out=oh.t[:, q0:q0 + 512], in0=a2.t[:, 0:512], scalar=sm.t[:, 5:6], in1=a1.t[:, 0:512], op0=ALU.mult, op1=ALU.add)),
                  reads=a1.r() + a2.r() + smr, writes=oh.r(q0, q0 + 512))
            P.add("pool", (lambda g, q0=q0: g.tensor_tensor(out=osq.t[:, q0:q0 + 512], in0=oh.t[:, q0:q0 + 512], in1=oh.t[:, q0:q0 + 512], op=ALU.mult)),
                  reads=oh.r(q0, q0 + 512), writes=osq.r(q0, q0 + 512))
        estep += len(steps)
        def subln(h=h):
            for qt in range(NTB):
                q0 = qt * 512
                sb_ = qt % 4
                P.add("pe", (lambda pe, q0=q0, sb_=sb_: pe.matmul(self.bank(sb_), lhsT=self.onesb.t[:], rhs=osq.t[:, q0:q0 + 512], start=True, stop=True)),
                      reads=osq.r(q0, q0 + 512) + self.onesb.r(), writes=PSR(sb_))
                rsb = sl1[qt % 2]
                rrb = sl2[qt % 2]
                self.rstd_from_sums(sb_, 512, (rsb, 0), (rrb, 0), 128.0, 1e-5)
                P.add("dve", (lambda v, q0=q0, rrb=rrb, h=h: v.scalar_tensor_tensor(out=oT.t[:, h, q0:q0 + 512], in0=oh.t[:, q0:q0 + 512], scalar=gsub.t[:, 0:1], in1=rrb.t[:, 0:512], op0=ALU.mult, op1=ALU.mult)),
                      reads=oh.r(q0, q0 + 512) + rrb.r() + gsub.r(), writes=oT.r(h * S + q0, h * S + q0 + 512))
        subln()

    for half in range(2):
        self.out_proj_postnorm(n + "w_o", None, n + "g_mix_post", oT, S, half, mT, sq, tmp, rstd2, tscr)


Builder.attn = _attn
```

```python
import math
import os
from contextlib import ExitStack
import numpy as np
import concourse.bass as bass
import concourse.mybir as mybir
from concourse.bass_utils import run_bass_kernel_spmd

F32 = mybir.dt.float32
BF16 = mybir.dt.bfloat16
AF = mybir.ActivationFunctionType
ALU = mybir.AluOpType
AX = mybir.AxisListType

D = 1024
S = 2048
KC = 8
DFF = 2816
FC = 22
NTB = 4
DEPTH = 4
NCORES = 8
SEQ_PER_CORE = 5
GR = 256
EPOCH = 12000
SBUF_BASE = 16640
SBUF_LIMIT = 229312


class Op:
    __slots__ = ("eng", "fn", "deps", "is_dma", "dsem", "token", "idx")


class Prog:
    ENGS = ("pe", "act", "dve", "pool", "sp")

    def __init__(self, nc, es):
        self.nc = nc
        self.es = es
        self.ops = []
        self.last_w = {}
        self.readers = {}
        self.dma_sems = {}
        self.nsem = 0
        self.dry = False

    def _sem(self, name):
        self.nsem += 1
        return self.es.enter_context(self.nc.semaphore(name))

    def add(self, eng, fn, reads=(), writes=(), dma=None):
        if self.dry:
            return None
        op = Op()
        op.eng = eng
        op.fn = fn
        op.is_dma = dma is not None
        op.dsem = dma
        op.token = None
        op.idx = len(self.ops)
        deps = set()
        ops = self.ops
        for r in reads:
            w = self.last_w.get(r)
            if w is not None:
                wo = ops[w]
                if wo.is_dma or wo.eng != eng or eng not in ("pe",):
                    deps.add(w)
        for r in writes:
            w = self.last_w.get(r)
            if w is not None:
                wo = ops[w]
                if wo.is_dma or wo.eng != eng or eng not in ("pe",):
                    deps.add(w)
            rd = self.readers.get(r)
            if rd:
                for k, ri in rd.items():
                    ro = ops[ri]
                    if ro.is_dma or ro.eng != eng or eng != "pe":
                        deps.add(ri)
        op.deps = deps
        for r in writes:
            self.last_w[r] = op.idx
            self.readers[r] = {}
        for r in reads:
            d = self.readers.get(r)
            if d is None:
                d = self.readers[r] = {}
            if op.is_dma:
                d[("dma", op.idx)] = op.idx
            else:
                d[eng] = op.idx
        self.ops.append(op)
        return op

    def barrier(self):
        if self.dry:
            return
        keys = set(self.last_w.keys()) | set(self.readers.keys())
        self.add("dve", (lambda v: v.engine_nop()), reads=(), writes=list(keys))

    def emit(self):
        nc = self.nc
        ops = self.ops
        needed = set()
        for op in ops:
            needed |= op.deps
        esem = {}
        cnt = {e: 0 for e in self.ENGS}
        for op in ops:
            if op.is_dma:
                ent = self.dma_sems.get(op.dsem)
                if ent is None:
                    ent = self.dma_sems[op.dsem] = [self._sem("d_" + op.dsem), 0]
                ent[1] += 16
                op.token = (ent[0], ent[1], ("d", op.dsem))
            elif op.idx in needed:
                e = op.eng
                ep = cnt[e] // EPOCH
                val = cnt[e] % EPOCH + 1
                cnt[e] += 1
                key = (e, ep)
                if key not in esem:
                    esem[key] = self._sem("s_%s_%d" % (e, ep))
                op.token = (esem[key], val, key)
        for op in ops:
            if op.token is not None and not op.is_dma:
                pass
        block = self.es.enter_context(nc.Block())
        per_eng = {e: [o for o in ops if o.eng == e] for e in self.ENGS}

        def run(engname, eng):
            waited = {}
            for op in per_eng[engname]:
                ws = {}
                for d in op.deps:
                    sem, val, key = ops[d].token
                    if waited.get(key, 0) >= val:
                        continue
                    if ws.get(key, (None, 0))[1] < val:
                        ws[key] = (sem, val)
                for key, (sem, val) in ws.items():
                    eng.wait_ge(sem, val)
                    waited[key] = val
                    if key[0] != "d":
                        for k2 in list(esem.keys()):
                            if k2[0] == key[0] and k2[1] < key[1]:
                                waited[k2] = EPOCH * 4
                ins = op.fn(eng)
                if op.token is not None:
                    if op.is_dma:
                        ins.then_inc(op.token[0], 16)
                    else:
                        ins.then_inc(op.token[0], 1)

        @block.tensor
        def _(t):
            run("pe", t)

        @block.scalar
        def _(a):
            run("act", a)

        @block.vector
        def _(v):
            run("dve", v)

        @block.gpsimd
        def _(g):
            run("pool", g)

        @block.sync
        def _(s):
            run("sp", s)


class SB:
    def __init__(self, nc, name, shape, dtype, off):
        self.t = nc.alloc_sbuf_tensor_at(name, [128] + list(shape), dtype, offset=off)
        self.off = off
        self.esz = 2 if dtype == BF16 else 4
        n = 1
        for s in shape:
            n *= s
        self.nbytes = n * self.esz
        assert off + self.nbytes <= SBUF_LIMIT, (name, off, self.nbytes)
        self.shape = shape

    def r(self, lo=0, hi=None):
        if hi is None:
            hi = self.nbytes // self.esz
        a = (self.off + lo * self.esz) // GR
        b = (self.off + hi * self.esz - 1) // GR
        return list(range(a, b + 1))

    def end(self):
        return self.off + self.nbytes


def PSR(*banks):
    return [("ps", b) for b in banks]


def fm(v):
    v = np.asarray(v, np.float32)
    return np.ascontiguousarray(v.reshape(-1, 128).T)


def pack_params(inp):
    cols = []
    index = {}

    def put(name, arr):
        arr = np.asarray(arr, np.float32)
        assert arr.shape[0] == 128
        index[name] = (sum(c.shape[1] for c in cols), arr.shape[1])
        cols.append(arr)

    for l in range(DEPTH):
        n = "l%d_" % l
        for g in ("g_mix_pre", "g_mix_post", "g_ffn_pre", "g_ffn_post"):
            put(n + g, fm(inp[n + g]))
        kind = l % 3
        if kind == 0:
            for g in ("lam_q1", "lam_k1", "lam_q2", "lam_k2"):
                put(n + g, np.broadcast_to(np.asarray(inp[n + g], np.float32)[None, :], (128, 64)))
            put(n + "g_subln", fm(inp[n + "g_subln"]))
        elif kind == 1:
            put(n + "b_pw1", fm(inp[n + "b_pw1"]))
            wdw = np.asarray(inp[n + "w_dw"], np.float32)
            put(n + "w_dw", np.ascontiguousarray(wdw.reshape(31, 8, 128).transpose(2, 1, 0)).reshape(128, 8 * 31))
            for g in ("b_dw", "g_cln", "b_cln", "b_pw2"):
                put(n + g, fm(inp[n + g]))
        else:
            put(n + "b_uv", fm(inp[n + "b_uv"]))
            for g in ("g_sln", "b_sln", "b_o"):
                put(n + g, fm(inp[n + g]))
    arr = np.ascontiguousarray(np.concatenate(cols, axis=1))
    return arr, index


def rope_tables():
    pos = np.arange(S, dtype=np.float32)
    inv = (1.0 / (10000.0 ** (np.arange(0, 64, 2, dtype=np.float32) / 64))).astype(np.float32)
    ang = pos[None, :] * inv[:, None]
    c = np.cos(ang).astype(np.float32)
    s = np.sin(ang).astype(np.float32)
    cosT = np.zeros((128, S), np.float32)
    sinT = np.zeros((128, S), np.float32)
    for comp in range(2):
        for half in range(2):
            base = comp * 64 + half * 32
            cosT[base:base + 32] = c
            sinT[base:base + 32] = -s if half == 0 else s
    return cosT, sinT


def const_inputs():
    perm = np.zeros((128, 128), np.float32)
    for p in range(128):
        perm[p ^ 32, p] = 1.0
    cosT, sinT = rope_tables()
    return {
        "c_ident": np.eye(128, dtype=np.float32),
        "c_perm": perm,
        "c_cos": cosT,
        "c_sin": sinT,
    }


_PARAM_INDEX = None


def param_index():
    global _PARAM_INDEX
    if _PARAM_INDEX is None:
        fake = {}
        for l in range(DEPTH):
            n = "l%d_" % l
            for g in ("g_mix_pre", "g_mix_post", "g_ffn_pre", "g_ffn_post"):
                fake[n + g] = np.zeros(D, np.float32)
            kind = l % 3
            if kind == 0:
                for g in ("lam_q1", "lam_k1", "lam_q2", "lam_k2"):
                    fake[n + g] = np.zeros(64, np.float32)
                fake[n + "g_subln"] = np.zeros(128, np.float32)
            elif kind == 1:
                fake[n + "b_pw1"] = np.zeros(2 * D, np.float32)
                fake[n + "w_dw"] = np.zeros((31, D), np.float32)
                for g in ("b_dw", "g_cln", "b_cln", "b_pw2"):
                    fake[n + g] = np.zeros(D, np.float32)
            else:
                fake[n + "b_uv"] = np.zeros(2 * D, np.float32)
                for g in ("g_sln", "b_sln", "b_o"):
                    fake[n + g] = np.zeros(D, np.float32)
        arr, idx = pack_params(fake)
        _PARAM_INDEX = (arr.shape[1], idx)
    return _PARAM_INDEX


class Builder:
    def __init__(self, nseq=SEQ_PER_CORE, layers=(0, 1, 2, 3), do_ffn=True, do_mix=True):
        self.nseq = nseq
        self.layers = layers
        self.do_ffn = do_ffn
        self.do_mix = do_mix
        self.nc = bass.Bass("TRN2", target_bir_lowering=False)
        self.es = ExitStack()
        self.P = Prog(self.nc, self.es)
        self.npar, self.pidx = param_index()
        self.plan_mode = False
        self.plan = []
        self.ws_issued = 0

    def declare_dram(self):
        nc = self.nc
        dr = {}
        dr["x"] = nc.dram_tensor("x", [self.nseq, S, D], F32, kind="ExternalInput").ap()
        dr["y"] = nc.dram_tensor("y", [self.nseq, S, D], F32, kind="ExternalOutput").ap()
        dr["params"] = nc.dram_tensor("params", [128, self.npar], F32, kind="ExternalInput").ap()
        dr["c_ident"] = nc.dram_tensor("c_ident", [128, 128], F32, kind="ExternalInput").ap()
        dr["c_perm"] = nc.dram_tensor("c_perm", [128, 128], F32, kind="ExternalInput").ap()
        dr["c_cos"] = nc.dram_tensor("c_cos", [128, S], F32, kind="ExternalInput").ap()
        dr["c_sin"] = nc.dram_tensor("c_sin", [128, S], F32, kind="ExternalInput").ap()
        dr["bscr"] = nc.dram_tensor("bscr", [4, 512], F32).ap()
        for l in range(DEPTH):
            n = "l%d_" % l
            kind = l % 3
            if kind == 0:
                dr[n + "w_qkv"] = nc.dram_tensor(n + "w_qkv", [D, 3 * D], F32, kind="ExternalInput").ap()
                dr[n + "w_o"] = nc.dram_tensor(n + "w_o", [D, D], F32, kind="ExternalInput").ap()
            elif kind == 1:
                dr[n + "w_pw1"] = nc.dram_tensor(n + "w_pw1", [D, 2 * D], F32, kind="ExternalInput").ap()
                dr[n + "w_pw2"] = nc.dram_tensor(n + "w_pw2", [D, D], F32, kind="ExternalInput").ap()
            else:
                dr[n + "w_uv"] = nc.dram_tensor(n + "w_uv", [D, 2 * D], F32, kind="ExternalInput").ap()
                dr[n + "w_s"] = nc.dram_tensor(n + "w_s", [8, 128, 128], F32, kind="ExternalInput").ap()
                dr[n + "b_s"] = nc.dram_tensor(n + "b_s", [8, 128], F32, kind="ExternalInput").ap()
                dr[n + "w_o"] = nc.dram_tensor(n + "w_o", [D, D], F32, kind="ExternalInput").ap()
            dr[n + "w_gate_up"] = nc.dram_tensor(n + "w_gate_up", [D, 2 * DFF], F32, kind="ExternalInput").ap()
            dr[n + "w_down"] = nc.dram_tensor(n + "w_down", [DFF, D], F32, kind="ExternalInput").ap()
        self.dr = dr

    def alloc_persistent(self):
        nc = self.nc
        off = SBUF_BASE

        def A(name, shape, dt):
            nonlocal off
            b = SB(nc, name, shape, dt, off)
            off = (b.end() + GR - 1) // GR * GR
            return b

        self.XT = A("XT", [KC, S], F32)
        self.PAR = A("PAR", [self.npar], F32)
        self.identf = A("identf", [128], F32)
        self.identb = A("identb", [128], BF16)
        self.permb = A("permb", [128], BF16)
        self.onesb = A("onesb", [128], BF16)
        self.stat = A("stat", [64], F32)
        self.epst = A("epst", [16], F32)
        self.WS = [A("ws%d" % i, [4096], BF16) for i in range(2)]
        self.big0 = off
        self.ps = nc.alloc_psum_tensor("ps", [128, 4096], F32)
        self.ws_count = 0

    def bank(self, b, n=512, o=0):
        return self.ps[:, b * 512 + o: b * 512 + o + n]

    def par(self, name, c0=0, n=1):
        o, w = self.pidx[name]
        return self.PAR.t[:, o + c0: o + c0 + n]

    def par_r(self):
        return self.PAR.r()

    def wload(self, pieces):
        idx = self.ws_count
        self.ws_count += 1
        slot = self.WS[idx % len(self.WS)]
        if self.plan_mode:
            self.plan.append(pieces)
            return slot
        while self.ws_issued <= min(idx + 1, len(self.plan) - 1):
            j = self.ws_issued
            sl = self.WS[j % len(self.WS)]
            for ent in self.plan[j]:
                dst_fn, src_fn = ent[0], ent[1]
                wr = sl.r(*ent[2]) if len(ent) > 2 else sl.r()
                dst = dst_fn(sl.t)
                src = src_fn(self.dr)
                self.P.add("pool", (lambda g, d=dst, s_=src: g.dma_start(out=d, in_=s_)), reads=(), writes=wr, dma="ws%d" % (j % len(self.WS)))
            self.ws_issued += 1
        return slot

    def init_consts(self):
        P, dr = self.P, self.dr
        P.add("sp", lambda s: s.dma_start(out=self.PAR.t[:], in_=dr["params"]), writes=self.PAR.r(), dma="c0")
        P.add("sp", lambda s: s.dma_start(out=self.identf.t[:], in_=dr["c_ident"]), writes=self.identf.r(), dma="c1")
        P.add("pool", lambda g: g.dma_start(out=self.identb.t[:], in_=dr["c_ident"]), writes=self.identb.r(), dma="c2")
        P.add("pool", lambda g: g.dma_start(out=self.permb.t[:], in_=dr["c_perm"]), writes=self.permb.r(), dma="c3")
        P.add("pool", lambda g: g.memset(self.onesb.t[:], 1.0), writes=self.onesb.r())
        P.add("pool", lambda g: g.memset(self.epst.t[:, 0:1], 1e-6), writes=self.epst.r())
        P.add("pool", lambda g: g.memset(self.epst.t[:, 1:2], 1e-5), reads=self.epst.r(), writes=self.epst.r())

    def load_seq(self, s, stage):
        P = self.P
        XT = self.XT
        for t in range(16):
            st = t % 2
            src = self.dr["x"][s, t * 128:(t + 1) * 128, :]
            dstv = stage.t[:, st, :]
            P.add("sp", (lambda e, d=dstv, s_=src: e.dma_start(out=d, in_=s_)), writes=stage.r(st * 1024, (st + 1) * 1024), dma="ld%d" % st)
            for hb in range(2):
                bk = 6 + hb

                def tr(pe, st=st, hb=hb, bk=bk):
                    ins = None
                    for kk in range(4):
                        k = hb * 4 + kk
                        ins = pe.transpose(self.bank(bk, 128, kk * 128), stage.t[:, st, k * 128:(k + 1) * 128], self.identf.t[:])
                    return ins
                P.add("pe", tr, reads=stage.r(st * 1024 + hb * 512, st * 1024 + hb * 512 + 512) + self.identf.r(), writes=PSR(bk))
                dst = XT.t[:, hb * 4:(hb + 1) * 4, t * 128:(t + 1) * 128]
                src_ps = self.bank(bk).rearrange("p (k c) -> p k c", k=4)
                wr = []
                for kk in range(4):
                    k = hb * 4 + kk
                    wr += XT.r(k * S + t * 128, k * S + (t + 1) * 128)
                eng = "dve"
                if eng == "act":
                    P.add("act", (lambda a, d=dst, s_=src_ps: a.activation(out=d, in_=s_, func=AF.Copy)), reads=PSR(bk), writes=wr)
                else:
                    P.add("dve", (lambda v, d=dst, s_=src_ps: v.tensor_copy(out=d, in_=s_)), reads=PSR(bk), writes=wr)

    def store_seq(self, s, stage):
        P = self.P
        XT = self.XT
        for t in range(16):
            st = t % 2
            for hb in range(2):
                bk = 6 + hb

                def tr(pe, t=t, hb=hb, bk=bk):
                    ins = None
                    for kk in range(4):
                        k = hb * 4 + kk
                        ins = pe.transpose(self.bank(bk, 128, kk * 128), XT.t[:, k, t * 128:(t + 1) * 128], self.identf.t[:])
                    return ins
                rd = []
                for kk in range(4):
                    k = hb * 4 + kk
                    rd += XT.r(k * S + t * 128, k * S + (t + 1) * 128)
                P.add("pe", tr, reads=rd + self.identf.r(), writes=PSR(bk))
                dst = stage.t[:, st, hb * 512:(hb + 1) * 512]
                src_ps = self.bank(bk)
                wr = stage.r(st * 1024 + hb * 512, st * 1024 + hb * 512 + 512)
                if False:
                    P.add("act", (lambda a, d=dst, s_=src_ps: a.activation(out=d, in_=s_, func=AF.Copy)), reads=PSR(bk), writes=wr)
                else:
                    P.add("dve", (lambda v, d=dst, s_=src_ps: v.tensor_copy(out=d, in_=s_)), reads=PSR(bk), writes=wr)
            dst = self.dr["y"][s, t * 128:(t + 1) * 128, :]
            srcv = stage.t[:, st, :]
            P.add("sp", (lambda e, d=dst, s_=srcv: e.dma_start(out=d, in_=s_)), reads=stage.r(st * 1024, (st + 1) * 1024), dma="st%d" % st)

    def stats_begin(self):
        pass

    def rstd_from_sums(self, sum_bank, n, tmp, rstd, width, eps, o=0):
        P = self.P
        tb, to = tmp
        rb, ro = rstd
        P.add("act", (lambda a: a.activation(out=tb.t[:, to:to + n], in_=self.bank(sum_bank, n, o), func=AF.Sqrt, scale=1.0 / width, bias=self.epsb(eps))),
              reads=PSR(sum_bank) + self.epst.r(), writes=tb.r(to, to + n))
        P.add("dve", (lambda v: v.reciprocal(out=rb.t[:, ro:ro + n], in_=tb.t[:, to:to + n])),
              reads=tb.r(to, to + n), writes=rb.r(ro, ro + n))

    def epsb(self, eps):
        return self.epst.t[:, 0:1] if eps == 1e-6 else self.epst.t[:, 1:2]

    def prenorm_block(self, tb, gname, xnT, xn_off, sq, tmp, rstd, sbank):
        P = self.P
        XT = self.XT
        c0 = tb * 512
        for k in range(KC):
            sl = k % 2
            P.add("act", (lambda a, k=k, sl=sl: a.activation(out=sq.t[:, sl, :], in_=XT.t[:, k, c0:c0 + 512], func=AF.Square)),
                  reads=XT.r(k * S + c0, k * S + c0 + 512), writes=sq.r(sl * 512, sl * 512 + 512))
            P.add("pe", (lambda pe, k=k, sl=sl: pe.matmul(self.bank(sbank), lhsT=self.onesb.t[:], rhs=sq.t[:, sl, :], start=(k == 0), stop=(k == KC - 1))),
                  reads=sq.r(sl * 512, sl * 512 + 512) + self.onesb.r(), writes=PSR(sbank))
        self.rstd_from_sums(sbank, 512, (tmp, 0), (rstd, 0), float(D), 1e-6)
        xsh = xnT.shape[1]
        for k in range(KC):
            P.add("dve", (lambda v, k=k: v.scalar_tensor_tensor(out=xnT.t[:, k, xn_off:xn_off + 512], in0=XT.t[:, k, c0:c0 + 512], scalar=self.par(gname, k),
                                                                 in1=rstd.t[:, 0:512], op0=ALU.mult, op1=ALU.mult)),
                  reads=XT.r(k * S + c0, k * S + c0 + 512) + rstd.r(0, 512) + self.par_r(), writes=xnT.r(k * xsh + xn_off, k * xsh + xn_off + 512))

    def postnorm_block(self, tb, gname, fT, f_off, rstd, tscr):
        P = self.P
        XT = self.XT
        c0 = tb * 512
        fsh = fT.shape[1]
        for c in range(KC):
            sl = c % 2
            P.add("dve", (lambda v, c=c, sl=sl: v.scalar_tensor_tensor(out=tscr.t[:, sl, :], in0=fT.t[:, c, f_off:f_off + 512], scalar=self.par(gname, c),
                                                                        in1=rstd.t[:, 0:512], op0=ALU.mult, op1=ALU.mult)),
                  reads=fT.r(c * fsh + f_off, c * fsh + f_off + 512) + rstd.r(0, 512) + self.par_r(), writes=tscr.r(sl * 512, sl * 512 + 512))
            P.add("pool", (lambda g, c=c, sl=sl: g.tensor_tensor(out=XT.t[:, c, c0:c0 + 512], in0=XT.t[:, c, c0:c0 + 512], in1=tscr.t[:, sl, :], op=ALU.add)),
                  reads=tscr.r(sl * 512, sl * 512 + 512) + XT.r(c * S + c0, c * S + c0 + 512), writes=XT.r(c * S + c0, c * S + c0 + 512))

    def ffn(self, l):
        nc, P, dr = self.nc, self.P, self.dr
        n = "l%d_" % l
        off = self.big0

        def A(name, shape, dt):
            nonlocal off
            b = SB(nc, name + "_f%d_%d" % (l, self.uid()), shape, dt, off)
            off = (b.end() + GR - 1) // GR * GR
            return b

        xnT = A("xnT", [KC, 1024], BF16)
        hT = A("hT", [FC, 1024], BF16)
        fT = A("fT", [KC, 1024], F32)
        sq = A("sq", [2, 512], BF16)
        sg = A("sg", [2, 512], F32)
        tmp = A("tmp", [512], F32)
        rstd = A("rstd", [512], F32)
        rstd2 = [A("rstd2a", [512], F32), A("rstd2b", [512], F32)]
        tscr = A("tscr", [2, 512], F32)
        for half in range(2):
            for b in range(2):
                self.prenorm_block(half * 2 + b, n + "g_ffn_pre", xnT, b * 512, sq, tmp, rstd, 6 + b)
            cnt = 0
            for i in range(FC // 2):
                slot = self.wload([
                    (lambda t: t[:, 0:2048].rearrange("p (k c) -> p k c", k=8), lambda dr, i=i: dr[n + "w_gate_up"][:, i * 256:(i + 1) * 256].rearrange("(k p) c -> p k c", p=128)),
                    (lambda t: t[:, 2048:4096].rearrange("p (k c) -> p k c", k=8), lambda dr, i=i: dr[n + "w_gate_up"][:, DFF + i * 256:DFF + (i + 1) * 256].rearrange("(k p) c -> p k c", p=128)),
                ])
                for sub in range(2):
                    j = i * 2 + sub
                    st = cnt % 2
                    cnt += 1
                    gb = [st * 2 + 0, st * 2 + 1]
                    ub = [4 + (st * 2 + 0) % 2, 0]
                    ub = [4, 5]

                    def mm(pe, slot=slot, sub=sub, bb=gb, gi=0):
                        ins = None
                        wv = slot.t[:, :].rearrange("p (g k c) -> p g k c", g=2, k=8)
                        for k in range(KC):
                            for b in range(2):
                                ins = pe.matmul(self.bank(bb[b]), lhsT=wv[:, gi, k, sub * 128:(sub + 1) * 128], rhs=xnT.t[:, k, b * 512:(b + 1) * 512], start=(k == 0), stop=(k == KC - 1))
                        return ins
                    P.add("pe", mm, reads=slot.r() + xnT.r(), writes=PSR(gb[0], gb[1]))
                    P.add("pe", (lambda pe, slot=slot, sub=sub, ub=ub, mm=mm: mm(pe, slot, sub, ub, 1)), reads=slot.r() + xnT.r(), writes=PSR(ub[0], ub[1]))
                    for b in range(2):
                        P.add("act", (lambda a, b=b, gb=gb: a.activation(out=sg.t[:, b, :], in_=self.bank(gb[b]), func=AF.Silu)),
                              reads=PSR(gb[b]), writes=sg.r(b * 512, b * 512 + 512))
                        P.add("dve", (lambda v, b=b, ub=ub, j=j: v.tensor_tensor(out=hT.t[:, j, b * 512:(b + 1) * 512], in0=sg.t[:, b, :], in1=self.bank(ub[b]), op=ALU.mult)),
                              reads=PSR(ub[b]) + sg.r(b * 512, b * 512 + 512), writes=hT.r(j * 1024 + b * 512, j * 1024 + b * 512 + 512))
            for c in range(KC):
                slot = self.wload([
                    (lambda t: t[:, 0:FC * 128].rearrange("p (k c) -> p k c", k=FC), lambda dr, c=c: dr[n + "w_down"][:, c * 128:(c + 1) * 128].rearrange("(k p) c -> p k c", p=128)),
                ])
                fb = [(c % 2) * 2 + 0, (c % 2) * 2 + 1]

                def mm(pe, slot=slot, fb=fb):
                    ins = None
                    wv = slot.t[:, 0:FC * 128].rearrange("p (k c) -> p k c", k=FC)
                    for kf in range(FC):
                        for b in range(2):
                            ins = pe.matmul(self.bank(fb[b]), lhsT=wv[:, kf, :], rhs=hT.t[:, kf, b * 512:(b + 1) * 512], start=(kf == 0), stop=(kf == FC - 1))
                    return ins
                P.add("pe", mm, reads=slot.r() + hT.r(), writes=PSR(*fb))
                for b in range(2):
                    P.add("act", (lambda a, b=b, fb=fb, c=c: a.activation(out=fT.t[:, c, b * 512:(b + 1) * 512], in_=self.bank(fb[b]), func=AF.Copy)),
                          reads=PSR(fb[b]), writes=fT.r(c * 1024 + b * 512, c * 1024 + b * 512 + 512))
                    P.add("act", (lambda a, b=b, fb=fb: a.activation(out=sq.t[:, b, :], in_=self.bank(fb[b]), func=AF.Square)),
                          reads=PSR(fb[b]), writes=sq.r(b * 512, b * 512 + 512))
                    P.add("pe", (lambda pe, b=b, c=c: pe.matmul(self.bank(6 + b), lhsT=self.onesb.t[:], rhs=sq.t[:, b, :], start=(c == 0), stop=(c == KC - 1))),
                          reads=sq.r(b * 512, b * 512 + 512) + self.onesb.r(), writes=PSR(6 + b))
            for b in range(2):
                self.rstd_from_sums(6 + b, 512, (tmp, 0), (rstd2[b], 0), float(D), 1e-6)
                self.postnorm_block(half * 2 + b, n + "g_ffn_post", fT, b * 512, rstd2[b], tscr)

    _uid = 0

    def uid(self):
        Builder._uid += 1
        return Builder._uid


class SBview:
    def __init__(self, sb, b):
        self.sb = sb
        self.b = b
        self.t = sb.t[:, b, :]

    def r(self, lo=0, hi=512):
        return self.sb.r(self.b * 512 + lo, self.b * 512 + hi)


def _record(B, nseq, layers, do_ffn, do_mix):
    B.declare_dram()
    B.alloc_persistent()
    B.init_consts()
    stage = SB(B.nc, "stage%d" % B.uid(), [2, 1024], F32, B.big0)
    stage_l = SB(B.nc, "stagel%d" % B.uid(), [2, 1024], F32, B.big0 + 8192)
    for s in range(nseq):
        B.load_seq(s, stage_l)
        for l in layers:
            if do_mix:
                B.mixer(l)
            if do_ffn:
                B.ffn(l)
            B.P.barrier()
        B.store_seq(s, stage)
        B.P.barrier()


def build_program(nseq=SEQ_PER_CORE, layers=(0, 1, 2, 3), do_ffn=True, do_mix=True):
    B0 = Builder(nseq, layers, do_ffn, do_mix)
    B0.plan_mode = True
    B0.P.dry = True
    _record(B0, nseq, layers, do_ffn, do_mix)
    B = Builder(nseq, layers, do_ffn, do_mix)
    B.plan = B0.plan
    _record(B, nseq, layers, do_ffn, do_mix)
    B.P.emit()
    return B


def make_in_maps(inputs, nseq=SEQ_PER_CORE, ncores=NCORES):
    xp = np.asarray(inputs["x_prompt"], np.float32)
    xs = np.asarray(inputs["x_sample"], np.float32)
    params, _ = pack_params(inputs)
    consts = const_inputs()
    maps = []
    for c in range(ncores):
        xc = np.concatenate([xp[c:c + 1], xs[4 * c:4 * c + 4]], axis=0)[:nseq]
        m = {"x": np.ascontiguousarray(xc), "params": params}
        m.update(consts)
        for l in range(DEPTH):
            n = "l%d_" % l
            kind = l % 3
            names = ["w_gate_up", "w_down"]
            if kind == 0:
                names += ["w_qkv", "w_o"]
            elif kind == 1:
                names += ["w_pw1", "w_pw2"]
            else:
                names += ["w_uv", "w_s", "b_s", "w_o"]
            for nm in names:
                m[n + nm] = np.ascontiguousarray(np.asarray(inputs[n + nm], np.float32))
        maps.append(m)
    return maps


_CACHE = {}


def kernel(**inputs):
    if "B" not in _CACHE:
        _CACHE["B"] = build_program()
    B = _CACHE["B"]
    maps = make_in_maps(inputs)
    res = run_bass_kernel_spmd(B.nc, maps, core_ids=list(range(NCORES)))
    ys = [np.asarray(r["y"], np.float32) for r in res.results]
    y_prompt = np.stack([ys[c][0] for c in range(NCORES)], axis=0)
    y_sample = np.concatenate([ys[c][1:5] for c in range(NCORES)], axis=0)
    return (y_prompt, y_sample)


def _alloc(self, tag):
    nc = self.nc
    state = {"off": self.big0}

    def A(name, shape, dt):
        b = SB(nc, "%s_%s_%d" % (name, tag, self.uid()), shape, dt, state["off"])
        state["off"] = (b.end() + GR - 1) // GR * GR
        return b
    A.state = state
    return A


def _proj_feature_major(self, wname, ncols_total, col0, nchunks, inT, in_sh, tok0, nblk, banks, evac):
    P = self.P
    bi = 0
    for pi in range((nchunks + 3) // 4):
        nsub = min(4, nchunks - pi * 4)
        c0 = col0 + pi * 512
        slot = self.wload([
            (lambda t, nsub=nsub: t[:, 0:8 * nsub * 128].rearrange("p (k c) -> p k c", k=8),
             lambda dr, c0=c0, nsub=nsub: dr[wname][:, c0:c0 + nsub * 128].rearrange("(k p) c -> p k c", p=128)),
        ])
        for sub in range(nsub):
            oc = pi * 4 + sub
            for b in range(nblk):
                bk = banks[bi % len(banks)]
                bi += 1

                def mm(pe, slot=slot, sub=sub, b=b, bk=bk, nsub=nsub):
                    ins = None
                    wv = slot.t[:, 0:8 * nsub * 128].rearrange("p (k c) -> p k c", k=8)
                    for k in range(KC):
                        ins = pe.matmul(self.bank(bk), lhsT=wv[:, k, sub * 128:(sub + 1) * 128], rhs=inT.t[:, k, tok0 + b * 512: tok0 + (b + 1) * 512], start=(k == 0), stop=(k == KC - 1))
                    return ins
                rd = []
                for k in range(KC):
                    rd += inT.r(k * in_sh + tok0 + b * 512, k * in_sh + tok0 + (b + 1) * 512)
                P.add("pe", mm, reads=slot.r() + rd, writes=PSR(bk))
                evac(oc, b, bk)


def _out_proj_postnorm(self, wname, bias_name, gpost, inT, in_sh, half, mT, sq, tmp, rstd2, tscr):
    P = self.P

    def evac(oc, b, bk):
        if bias_name is not None:
            P.add("act", (lambda a: a.activation(out=mT.t[:, oc, b * 512:(b + 1) * 512], in_=self.bank(bk), func=AF.Identity, bias=self.par(bias_name, oc))),
                  reads=PSR(bk) + self.par_r(), writes=mT.r(oc * 1024 + b * 512, oc * 1024 + (b + 1) * 512))
        else:
            P.add("act", (lambda a: a.activation(out=mT.t[:, oc, b * 512:(b + 1) * 512], in_=self.bank(bk), func=AF.Copy)),
                  reads=PSR(bk), writes=mT.r(oc * 1024 + b * 512, oc * 1024 + (b + 1) * 512))
        P.add("act", (lambda a: a.activation(out=sq.t[:, b, :], in_=mT.t[:, oc, b * 512:(b + 1) * 512], func=AF.Square)),
              reads=mT.r(oc * 1024 + b * 512, oc * 1024 + (b + 1) * 512), writes=sq.r(b * 512, b * 512 + 512))
        P.add("pe", (lambda pe: pe.matmul(self.bank(6 + b), lhsT=self.onesb.t[:], rhs=sq.t[:, b, :], start=(oc == 0), stop=(oc == KC - 1))),
              reads=sq.r(b * 512, b * 512 + 512) + self.onesb.r(), writes=PSR(6 + b))
    self.proj_fm(wname, D, 0, KC, inT, in_sh, half * 1024, 2, [0, 1, 2, 3], evac)
    for b in range(2):
        self.rstd_from_sums(6 + b, 512, (tmp, 0), (rstd2[b], 0), float(D), 1e-6)
        self.postnorm_block(half * 2 + b, gpost, mT, b * 512, rstd2[b], tscr)


def _ln_stats(self, s1bank, s2bank, m, msq, tmp, rstd, eps):
    P = self.P
    P.add("dve", (lambda v: v.tensor_scalar(out=m.t[:, 0:512], in0=self.bank(s1bank), scalar1=1.0 / D, scalar2=None, op0=ALU.mult)),
          reads=PSR(s1bank), writes=m.r())
    P.add("pool", (lambda g: g.tensor_tensor(out=msq.t[:, 0:512], in0=m.t[:, 0:512], in1=m.t[:, 0:512], op=ALU.mult)),
          reads=m.r(), writes=msq.r())
    P.add("dve", (lambda v: v.scalar_tensor_tensor(out=tmp.t[:, 0:512], in0=self.bank(s2bank), scalar=1.0 / D, in1=msq.t[:, 0:512], op0=ALU.mult, op1=ALU.subtract)),
          reads=PSR(s2bank) + msq.r(), writes=tmp.r())
    P.add("act", (lambda a: a.activation(out=tmp.t[:, 0:512], in_=tmp.t[:, 0:512], func=AF.Sqrt, bias=self.epsb(eps))),
          reads=tmp.r() + self.epst.r(), writes=tmp.r())
    P.add("dve", (lambda v: v.reciprocal(out=rstd.t[:, 0:512], in_=tmp.t[:, 0:512])),
          reads=tmp.r(), writes=rstd.r())


GELU_C = 0.7978845608028654


def _gelu_evac(self, bk, bias_ap, out_ap, out_r, scr):
    P = self.P
    P.add("act", (lambda a: a.activation(out=out_ap, in_=self.bank(bk), func=AF.Gelu_apprx_tanh, bias=bias_ap)),
          reads=PSR(bk) + self.par_r(), writes=out_r)


def _mixer(self, l):
    kind = l % 3
    if kind == 0:
        self.attn(l)
    elif kind == 1:
        self.conv(l)
    else:
        self.sgate(l)


def _sgate(self, l):
    P, dr = self.P, self.dr
    n = "l%d_" % l
    H = 1024
    A = self.alloc("sg%d" % l)
    xnT = A("xnT", [KC, H], BF16)
    uT = A("uT", [KC, H], BF16)
    vT = A("vT", [KC, H], BF16)
    mT = A("mT", [KC, H], F32)
    sq = A("sq", [2, 512], BF16)
    tmp = A("tmp", [512], F32)
    rstd = A("rstd", [512], F32)
    rstd2 = [A("rstd2a", [512], F32), A("rstd2b", [512], F32)]
    tscr = A("tscr", [2, 512], F32)
    gscr = [A("gscr0", [3, 512], F32)]
    mm_ = A("m", [512], F32)
    msq = A("msq", [512], F32)
    bs = A("bs", [8, 128], F32)
    wsT = A("wsT", [8, 128], BF16)
    vtok = A("vtok", [2, 8, 128], BF16)
    wsP = SB(self.nc, "wsP_sg%d_%d" % (l, self.uid()), [8, 128], BF16, vtok.off)
    mixs = SB(self.nc, "mixs_sg%d_%d" % (l, self.uid()), [2, 512], F32, gscr[0].off)

    P.add("sp", (lambda e: e.dma_start(out=bs.t[:].rearrange("p g q -> p (g q)"), in_=dr[n + "b_s"].rearrange("g q -> (g q)").partition_broadcast(128))),
          writes=bs.r(), dma="sgc0")
    P.add("pool", (lambda g: g.dma_start(out=wsP.t[:], in_=dr[n + "w_s"].rearrange("g p q -> p g q"))), writes=wsP.r(), dma="sgc1")

    def trw(pe):
        ins = None
        for g in range(8):
            ins = pe.transpose(self.bank(4, 512).bitcast(BF16)[:, g * 128:(g + 1) * 128], wsP.t[:, g, :], self.identb.t[:])
        return ins
    P.add("pe", trw, reads=wsP.r() + self.identb.r(), writes=PSR(4))
    P.add("dve", (lambda v: v.tensor_copy(out=wsT.t[:].rearrange("p g q -> p (g q)"), in_=self.bank(4, 512).bitcast(BF16))), reads=PSR(4), writes=wsT.r())

    for half in range(2):
        for b in range(2):
            self.prenorm_block(half * 2 + b, n + "g_mix_pre", xnT, b * 512, sq, tmp, rstd, 6 + b)

        def evac_uv(oc, b, bk):
            dst = uT if oc < 8 else vT
            c = oc % 8
            self.gelu_evac(bk, self.par(n + "b_uv", oc), dst.t[:, c, b * 512:(b + 1) * 512], dst.r(c * H + b * 512, c * H + (b + 1) * 512), gscr[0])
        self.proj_fm(n + "w_uv", 2 * D, 0, 16, xnT, H, 0, 2, [0, 1, 2, 3], evac_uv)

        for b in range(2):
            c0 = b * 512
            s1, s2 = 4 + b, 6 + b
            for c in range(KC):
                sl = c % 2
                P.add("act", (lambda a, c=c, sl=sl, c0=c0: a.activation(out=sq.t[:, sl, :], in_=vT.t[:, c, c0:c0 + 512], func=AF.Square)),
                      reads=vT.r(c * H + c0, c * H + c0 + 512), writes=sq.r(sl * 512, sl * 512 + 512))
                P.add("pe", (lambda pe, c=c, sl=sl, s2=s2: pe.matmul(self.bank(s2), lhsT=self.onesb.t[:], rhs=sq.t[:, sl, :], start=(c == 0), stop=(c == KC - 1))),
                      reads=sq.r(sl * 512, sl * 512 + 512) + self.onesb.r(), writes=PSR(s2))
                P.add("pe", (lambda pe, c=c, s1=s1, c0=c0: pe.matmul(self.bank(s1), lhsT=self.onesb.t[:], rhs=vT.t[:, c, c0:c0 + 512], start=(c == 0), stop=(c == KC - 1))),
                      reads=vT.r(c * H + c0, c * H + c0 + 512) + self.onesb.r(), writes=PSR(s1))
            self.ln_stats(s1, s2, mm_, msq, tmp, rstd, 1e-5)
            for c in range(KC):
                sl = c % 2
                rr = vT.r(c * H + c0, c * H + c0 + 512)
                P.add("dve", (lambda v, c=c, sl=sl, c0=c0: v.tensor_tensor(out=tscr.t[:, sl, :], in0=vT.t[:, c, c0:c0 + 512], in1=mm_.t[:, 0:512], op=ALU.subtract)),
                      reads=rr + mm_.r(), writes=tscr.r(sl * 512, sl * 512 + 512))
                P.add("pool", (lambda g, sl=sl: g.tensor_tensor(out=tscr.t[:, sl, :], in0=tscr.t[:, sl, :], in1=rstd.t[:, 0:512], op=ALU.mult)),
                      reads=tscr.r(sl * 512, sl * 512 + 512) + rstd.r(), writes=tscr.r(sl * 512, sl * 512 + 512))
                P.add("dve", (lambda v, c=c, sl=sl, c0=c0: v.tensor_scalar(out=vT.t[:, c, c0:c0 + 512], in0=tscr.t[:, sl, :], scalar1=self.par(n + "g_sln", c), scalar2=self.par(n + "b_sln", c), op0=ALU.mult, op1=ALU.add)),
                      reads=tscr.r(sl * 512, sl * 512 + 512) + self.par_r(), writes=rr)

        for ch in range(8):
            t0 = ch * 128
            vs = ch % 2
            tbk = 4 + ch % 2

            def trv(pe, t0=t0, tbk=tbk):
                ins = None
                for g in range(8):
                    ins = pe.transpose(self.bank(tbk, 512).bitcast(BF16)[:, g * 128:(g + 1) * 128], vT.t[:, g, t0:t0 + 128], self.identb.t[:])
                return ins
            rd = []
            for g in range(8):
                rd += vT.r(g * H + t0, g * H + t0 + 128)
            P.add("pe", trv, reads=rd + self.identb.r(), writes=PSR(tbk))
            P.add("dve", (lambda v, vs=vs, tbk=tbk: v.tensor_copy(out=vtok.t[:, vs].rearrange("p g q -> p (g q)"), in_=self.bank(tbk, 512).bitcast(BF16))),
                  reads=PSR(tbk), writes=vtok.r(vs * 1024, vs * 1024 + 1024))
            for gh in range(2):
                obk = (ch % 2) * 2 + gh

                def mmx(pe, vs=vs, gh=gh, obk=obk):
                    ins = None
                    for gg in range(4):
                        g = gh * 4 + gg
                        ins = pe.matmul(self.bank(obk, 128, gg * 128), lhsT=vtok.t[:, vs, g, :], rhs=wsT.t[:, g, :], start=True, stop=True)
                    return ins
                P.add("pe", mmx, reads=vtok.r(vs * 1024, vs * 1024 + 1024) + wsT.r(), writes=PSR(obk))
                P.add("dve", (lambda v, gh=gh, obk=obk: v.tensor_tensor(out=mixs.t[:, gh, :], in0=self.bank(obk), in1=bs.t[:, gh * 4:(gh + 1) * 4, :].rearrange("p g q -> p (g q)"), op=ALU.add)),
                      reads=PSR(obk) + bs.r(), writes=mixs.r(gh * 512, gh * 512 + 512))
                ur = []
                for gg in range(4):
                    g = gh * 4 + gg
                    ur += uT.r(g * H + t0, g * H + t0 + 128)
                P.add("pool", (lambda g_, gh=gh, t0=t0: g_.tensor_tensor(out=uT.t[:, gh * 4:(gh + 1) * 4, t0:t0 + 128], in0=uT.t[:, gh * 4:(gh + 1) * 4, t0:t0 + 128],
                                                                        in1=mixs.t[:, gh, :].rearrange("p (g q) -> p g q", g=4), op=ALU.mult)),
                      reads=mixs.r(gh * 512, gh * 512 + 512) + ur, writes=ur)

        self.out_proj_postnorm(n + "w_o", n + "b_o", n + "g_mix_post", _Shift(uT, half * 1024), H, half, mT, sq, tmp, rstd2, tscr)


ZW = S + 32


def _conv(self, l):
    P, dr = self.P, self.dr
    n = "l%d_" % l
    A = self.alloc("cv%d" % l)
    xnT = A("xnT", [KC, S], BF16)
    zpad = A("zpad", [KC, ZW], BF16)
    Dc = A("Dc", [2, 31, 128], BF16)
    sT = A("sT", [KC, 1024], BF16)
    sq = A("sq", [2, 512], BF16)
    ybf = A("ybf", [2, 512], BF16)
    tmp = A("tmp", [512], F32)
    rstd = A("rstd", [512], F32)
    rstd2 = [A("rstd2a", [512], F32), A("rstd2b", [512], F32)]
    tscr = A("tscr", [2, 512], F32)
    mm_ = A("m", [512], F32)
    msq = A("msq", [512], F32)
    sig = SB(self.nc, "sig_cv%d_%d" % (l, self.uid()), [2, 512], F32, tscr.off)
    yT = SB(self.nc, "yT_cv%d_%d" % (l, self.uid()), [KC, 1024], F32, xnT.off)
    mT = yT

    for c in range(KC):
        P.add("pool", (lambda g, c=c: g.memset(zpad.t[:, c, 0:16], 0.0)), writes=zpad.r(c * ZW, c * ZW + 16))
        P.add("pool", (lambda g, c=c: g.memset(zpad.t[:, c, 16 + S:ZW], 0.0)), writes=zpad.r(c * ZW + 16 + S, (c + 1) * ZW))

    for tb in range(NTB):
        self.prenorm_block(tb, n + "g_mix_pre", xnT, tb * 512, sq, tmp, rstd, 6 + tb % 2)

    for pi in range(4):
        slot = self.wload([
            (lambda t: t[:, 0:2048].rearrange("p (k c) -> p k c", k=8), lambda dr, pi=pi: dr[n + "w_pw1"][:, pi * 256:(pi + 1) * 256].rearrange("(k p) c -> p k c", p=128)),
            (lambda t: t[:, 2048:4096].rearrange("p (k c) -> p k c", k=8), lambda dr, pi=pi: dr[n + "w_pw1"][:, D + pi * 256:D + (pi + 1) * 256].rearrange("(k p) c -> p k c", p=128)),
        ])
        for sub in range(2):
            c = pi * 2 + sub
            for tb in range(NTB):
                st = tb % 2
                ab, gb = st * 2, st * 2 + 1

                def mm(pe, slot=slot, sub=sub, tb=tb, bk=ab, gi=0):
                    ins = None
                    wv = slot.t[:, :].rearrange("p (g k c) -> p g k c", g=2, k=8)
                    for k in range(KC):
                        ins = pe.matmul(self.bank(bk), lhsT=wv[:, gi, k, sub * 128:(sub + 1) * 128], rhs=xnT.t[:, k, tb * 512:(tb + 1) * 512], start=(k == 0), stop=(k == KC - 1))
                    return ins
                rd = []
                for k in range(KC):
                    rd += xnT.r(k * S + tb * 512, k * S + (tb + 1) * 512)
                P.add("pe", mm, reads=slot.r() + rd, writes=PSR(ab))
                P.add("pe", (lambda pe, slot=slot, sub=sub, tb=tb, gb=gb, mm=mm: mm(pe, slot, sub, tb, gb, 1)), reads=slot.r() + rd, writes=PSR(gb))
                P.add("act", (lambda a, st=st, gb=gb, c=c: a.activation(out=sig.t[:, st, :], in_=self.bank(gb), func=AF.Sigmoid, bias=self.par(n + "b_pw1", 8 + c))),
                      reads=PSR(gb) + self.par_r(), writes=sig.r(st * 512, st * 512 + 512))
                zr = zpad.r(c * ZW + 16 + tb * 512, c * ZW + 16 + (tb + 1) * 512)
                P.add("dve", (lambda v, st=st, ab=ab, c=c, tb=tb: v.scalar_tensor_tensor(out=zpad.t[:, c, 16 + tb * 512:16 + (tb + 1) * 512], in0=self.bank(ab), scalar=self.par(n + "b_pw1", c),
                                                                                     in1=sig.t[:, st, :], op0=ALU.add, op1=ALU.mult)),
                      reads=PSR(ab) + sig.r(st * 512, st * 512 + 512) + self.par_r(), writes=zr)

    for half in range(2):
        for c in range(KC):
            ds = c % 2

            def mkd(v, c=c, ds=ds):
                ins = None
                for j in range(31):
                    ins = v.tensor_scalar(out=Dc.t[:, ds, j, :], in0=self.identb.t[:], scalar1=self.par(n + "w_dw", c * 31 + j), scalar2=None, op0=ALU.mult)
                return ins
            P.add("dve", mkd, reads=self.identb.r() + self.par_r(), writes=Dc.r(ds * 31 * 128, (ds + 1) * 31 * 128))
            for b in range(2):
                tb = half * 2 + b
                bk = (c * 2 + b) % 4

                def mmc(pe, c=c, ds=ds, tb=tb, bk=bk):
                    ins = None
                    for j in range(31):
                        ins = pe.matmul(self.bank(bk), lhsT=Dc.t[:, ds, j, :], rhs=zpad.t[:, c, tb * 512 + j + 1: tb * 512 + j + 1 + 512], start=(j == 0), stop=(j == 30))
                    return ins
                P.add("pe", mmc, reads=Dc.r(ds * 31 * 128, (ds + 1) * 31 * 128) + zpad.r(c * ZW + tb * 512, c * ZW + tb * 512 + 512 + 32), writes=PSR(bk))
                yr = yT.r(c * 1024 + b * 512, c * 1024 + (b + 1) * 512)
                P.add("act", (lambda a, c=c, b=b, bk=bk: a.activation(out=yT.t[:, c, b * 512:(b + 1) * 512], in_=self.bank(bk), func=AF.Identity, bias=self.par(n + "b_dw", c))),
                      reads=PSR(bk) + self.par_r(), writes=yr)
                P.add("act", (lambda a, c=c, b=b: a.activation(out=sq.t[:, b, :], in_=yT.t[:, c, b * 512:(b + 1) * 512], func=AF.Square)),
                      reads=yr, writes=sq.r(b * 512, b * 512 + 512))
                P.add("act", (lambda a, c=c, b=b: a.activation(out=ybf.t[:, b, :], in_=yT.t[:, c, b * 512:(b + 1) * 512], func=AF.Copy)),
                      reads=yr, writes=ybf.r(b * 512, b * 512 + 512))
                P.add("pe", (lambda pe, c=c, b=b: pe.matmul(self.bank(6 + b), lhsT=self.onesb.t[:], rhs=sq.t[:, b, :], start=(c == 0), stop=(c == KC - 1))),
                      reads=sq.r(b * 512, b * 512 + 512) + self.onesb.r(), writes=PSR(6 + b))
                P.add("pe", (lambda pe, c=c, b=b: pe.matmul(self.bank(4 + b), lhsT=self.onesb.t[:], rhs=ybf.t[:, b, :], start=(c == 0), stop=(c == KC - 1))),
                      reads=ybf.r(b * 512, b * 512 + 512) + self.onesb.r(), writes=PSR(4 + b))
        for b in range(2):
            self.ln_stats(4 + b, 6 + b, mm_, msq, tmp, rstd, 1e-5)
            for c in range(KC):
                sl = c % 2
                yr = yT.r(c * 1024 + b * 512, c * 1024 + (b + 1) * 512)
                P.add("dve", (lambda v, c=c, sl=sl, b=b: v.tensor_tensor(out=tscr.t[:, sl, :], in0=yT.t[:, c, b * 512:(b + 1) * 512], in1=mm_.t[:, 0:512], op=ALU.subtract)),
                      reads=yr + mm_.r(), writes=tscr.r(sl * 512, sl * 512 + 512))
                P.add("pool", (lambda g, sl=sl: g.tensor_tensor(out=tscr.t[:, sl, :], in0=tscr.t[:, sl, :], in1=rstd.t[:, 0:512], op=ALU.mult)),
                      reads=tscr.r(sl * 512, sl * 512 + 512) + rstd.r(), writes=tscr.r(sl * 512, sl * 512 + 512))
                P.add("act", (lambda a, c=c, sl=sl, b=b: a.activation(out=sT.t[:, c, b * 512:(b + 1) * 512], in_=tscr.t[:, sl, :], func=AF.Silu, scale=self.par(n + "g_cln", c), bias=self.par(n + "b_cln", c))),
                      reads=tscr.r(sl * 512, sl * 512 + 512) + self.par_r(), writes=sT.r(c * 1024 + b * 512, c * 1024 + (b + 1) * 512))
        self.out_proj_postnorm(n + "w_pw2", n + "b_pw2", n + "g_mix_post", _Shift(sT, half * 1024), 1024, half, mT, sq, tmp, rstd2, tscr)


class _Shift:
    def __init__(self, sb, tok0):
        self.sb = sb
        self.tok0 = tok0
        self.t = _ShiftT(sb.t, tok0)

    def r(self, lo, hi):
        return self.sb.r(lo - self.tok0, hi - self.tok0)


class _ShiftT:
    def __init__(self, t, tok0):
        self._t = t
        self.tok0 = tok0

    def __getitem__(self, key):
        p, k, sl = key
        return self._t[p, k, sl.start - self.tok0: sl.stop - self.tok0]


Builder.alloc = _alloc
Builder.proj_fm = _proj_feature_major
Builder.out_proj_postnorm = _out_proj_postnorm
Builder.ln_stats = _ln_stats
Builder.gelu_evac = _gelu_evac
Builder.mixer = _mixer
Builder.sgate = _sgate
Builder.conv = _conv


def _attn(self, l):
    P, dr = self.P, self.dr
    nc = self.nc
    n = "l%d_" % l
    lam_init = 0.8 - 0.6 * math.exp(-0.3 * l)
    A = self.alloc("at%d" % l)
    xnT = A("xnT", [KC, S], BF16)
    oT = A("oT", [KC, S], BF16)
    qT = A("qT", [S], BF16)
    kT = A("kT", [S], BF16)
    V = A("V", [16, 128], BF16)
    cosT = A("cosT", [S], BF16)
    sinT = A("sinT", [S], BF16)
    u1 = A.state["off"]
    sq = A("sq", [2, 512], BF16)
    tmp = A("tmp", [512], F32)
    rstd = A("rstd", [512], F32)
    rstd2 = [A("rstd2a", [512], F32), A("rstd2b", [512], F32)]
    A.state["off"] = u1
    E = [A("E%d" % i, [1024], BF16) for i in range(3)]
    raw = [A("raw%d" % i, [512], BF16) for i in range(2)]
    t1s = [A("t1a", [512], F32), A("t1b", [512], F32)]
    t2s = [A("t2a", [512], F32), A("t2b", [512], F32)]
    sl1, sl2 = t1s, t2s
    u2 = A.state["off"]
    tscr = A("tscr", [2, 512], F32)
    A.state["off"] = u2
    r1 = A("r1", [512], F32)
    r2 = A("r2", [512], F32)
    a1 = A("a1", [512], F32)
    a2 = A("a2", [512], F32)
    oh = A("oh", [S], F32)
    osq = A("osq", [S], BF16)
    ssb = SB(nc, "ssb_at%d_%d" % (l, self.uid()), [512], F32, t1s[0].off)
    sm = A("sm", [64], F32)
    gsub = A("gsub", [4], F32)
    mT = SB(nc, "mT_at%d_%d" % (l, self.uid()), [KC, 1024], F32, xnT.off)
    smr = sm.r()
    scale = 64 ** -0.5
    DBG = 9
    _Padd = P.add
    if DBG <= -3:
        P.add = lambda *a, **k: None

    P.add("pool", (lambda g: g.dma_start(out=cosT.t[:], in_=dr["c_cos"])), writes=cosT.r(), dma="rp0")
    P.add("pool", (lambda g: g.dma_start(out=sinT.t[:], in_=dr["c_sin"])), writes=sinT.r(), dma="rp1")
    P.add("dve", (lambda v: v.scalar_tensor_tensor(out=a1.t[:, 0:64], in0=self.par(n + "lam_q1", 0, 64), scalar=1.0, in1=self.par(n + "lam_k1", 0, 64), op0=ALU.mult, op1=ALU.mult, accum_out=sm.t[:, 0:1])),
          reads=self.par_r(), writes=smr + a1.r())
    P.add("dve", (lambda v: v.scalar_tensor_tensor(out=a1.t[:, 0:64], in0=self.par(n + "lam_q2", 0, 64), scalar=1.0, in1=self.par(n + "lam_k2", 0, 64), op0=ALU.mult, op1=ALU.mult, accum_out=sm.t[:, 1:2])),
          reads=self.par_r() + smr, writes=smr + a1.r())
    P.add("act", (lambda a: a.activation(out=sm.t[:, 2:4], in_=sm.t[:, 0:2], func=AF.Exp)), reads=smr, writes=smr)
    P.add("dve", (lambda v: v.scalar_tensor_tensor(out=sm.t[:, 4:5], in0=sm.t[:, 2:3], scalar=lam_init, in1=sm.t[:, 3:4], op0=ALU.add, op1=ALU.subtract)), reads=smr, writes=smr)
    P.add("dve", (lambda v: v.tensor_scalar(out=sm.t[:, 5:6], in0=sm.t[:, 4:5], scalar1=-1.0, scalar2=None, op0=ALU.mult)), reads=smr, writes=smr)
    P.add("dve", (lambda v: v.tensor_scalar(out=gsub.t[:, 0:1], in0=self.par(n + "g_subln", 0, 1), scalar1=1.0 - lam_init, scalar2=None, op0=ALU.mult)),
          reads=self.par_r(), writes=gsub.r())

    P.add = _Padd
    for tb in range(NTB):
        self.prenorm_block(tb, n + "g_mix_pre", xnT, tb * 512, sq, tmp, rstd, 6 + tb % 2)

    estep = 0
    pending = [None]
    for h in range(8):
        slot = self.wload([
            (lambda t, w=w: t[:, 0:3072].rearrange("p (k w c) -> p k w c", k=8, w=3)[:, :, w, :],
             lambda dr, w=w, h=h: dr[n + "w_qkv"][:, w * D + h * 128: w * D + (h + 1) * 128].rearrange("(k p) c -> p k c", p=128))
            for w in range(3)
        ])
        wv = slot.t[:, 0:3072].rearrange("p (k w c) -> p k w c", k=8, w=3)
        def rec_v(tg):
            vb = 6 + tg % 2

            def mmv(pe, tg=tg, vb=vb, wv=wv):
                ins = None
                for tt in range(4):
                    tk = (tg * 4 + tt) * 128
                    for k in range(KC):
                        ins = pe.matmul(self.bank(vb, 128, tt * 128), lhsT=xnT.t[:, k, tk:tk + 128], rhs=wv[:, k, 2, :], start=(k == 0), stop=(k == KC - 1))
                return ins
            xr = []
            for k in range(KC):
                xr += xnT.r(k * S + tg * 512, k * S + tg * 512 + 512)
            P.add("pe", mmv, reads=slot.r() + xr, writes=PSR(vb))
            P.add("dve", (lambda v, tg=tg, vb=vb: v.tensor_copy(out=V.t[:, tg * 4:(tg + 1) * 4, :].rearrange("p t e -> p (t e)"), in_=self.bank(vb))),
                  reads=PSR(vb), writes=V.r(tg * 512, tg * 512 + 512))

        blocks = [(w, dst, tb) for w, dst in ((0, qT), (1, kT)) for tb in range(NTB)] if DBG >= -1 else []

        def rec_mm(i):
            w, dst, tb = blocks[i]
            c0 = tb * 512
            pb = i % 4
            xr = []
            for k in range(KC):
                xr += xnT.r(k * S + c0, k * S + c0 + 512)

            def mm(pe, w=w, c0=c0, pb=pb, wv=wv):
                ins = None
                for k in range(KC):
                    ins = pe.matmul(self.bank(pb), lhsT=wv[:, k, w, :], rhs=xnT.t[:, k, c0:c0 + 512], start=(k == 0), stop=(k == KC - 1))
                return ins
            P.add("pe", mm, reads=slot.r() + xr, writes=PSR(pb))

        def rec_evac(i):
            w, dst, tb = blocks[i]
            c0 = tb * 512
            pb = i % 4
            rb = raw[i % 2]
            t1b = t1s[i % 2]
            P.add("dve", (lambda v: v.tensor_tensor(out=t1b.t[:, 0:512], in0=self.bank(pb), in1=cosT.t[:, c0:c0 + 512], op=ALU.mult)),
                  reads=PSR(pb) + cosT.r(c0, c0 + 512), writes=t1b.r())
            P.add("act", (lambda a: a.activation(out=rb.t[:, 0:512], in_=self.bank(pb), func=AF.Copy)), reads=PSR(pb) + t1b.r(), writes=rb.r())
            qb_ = 4 + i % 2
            P.add("pe", (lambda pe: pe.matmul(self.bank(qb_), lhsT=self.permb.t[:], rhs=rb.t[:, 0:512], start=True, stop=True)),
                  reads=rb.r() + self.permb.r(), writes=PSR(qb_))

        def rec_rope(i):
            w, dst, tb = blocks[i]
            c0 = tb * 512
            qb_ = 4 + i % 2
            t1b = t1s[i % 2]
            t2b = t2s[i % 2]
            P.add("dve", (lambda v: v.tensor_tensor(out=t2b.t[:, 0:512], in0=self.bank(qb_), in1=sinT.t[:, c0:c0 + 512], op=ALU.mult)),
                  reads=PSR(qb_) + sinT.r(c0, c0 + 512), writes=t2b.r())
            P.add("pool", (lambda g: g.tensor_tensor(out=dst.t[:, c0:c0 + 512], in0=t1b.t[:, 0:512], in1=t2b.t[:, 0:512], op=ALU.add)),
                  reads=t1b.r() + t2b.r(), writes=dst.r(c0, c0 + 512))

        nb = len(blocks)
        for i in range(nb + 2):
            if i < nb:
                rec_mm(i)

            if 0 <= i - 1 < nb:
                rec_evac(i - 1)
            if 0 <= i - 2 < nb:
                rec_rope(i - 2)
        for tg in range(4):
            rec_v(tg)
        steps = [(qt, kc) for qt in range(NTB if DBG >= 1 else 0) for kc in range(16)]

        def rec_scores(i):
            qt, kc = steps[i]
            st = (estep0 + i) % 2
            k0, q0 = kc * 128, qt * 512

            def sc(pe, st=st, k0=k0, q0=q0):
                ins = None
                for c in range(2):
                    ins = pe.matmul(self.bank(st * 2 + c), lhsT=kT.t[c * 64:(c + 1) * 64, k0:k0 + 128], rhs=qT.t[c * 64:(c + 1) * 64, q0:q0 + 512], start=True, stop=True)
                return ins
            P.add("pe", sc, reads=kT.r(k0, k0 + 128) + qT.r(q0, q0 + 512), writes=PSR(st * 2, st * 2 + 1))

        estep0 = estep
        if steps:
            rec_scores(0)
        for i, (qt, kc) in enumerate(steps):
            q0 = qt * 512
            st = (estep0 + i) % 2
            e = E[(estep0 + i) % 3]
            P.add("act", (lambda a, st=st, e=e: a.activation(out=e.t[:, 0:1024], in_=self.ps[:, st * 1024:(st + 1) * 1024], func=AF.Exp, scale=scale)),
                  reads=PSR(st * 2, st * 2 + 1), writes=e.r())
            if i + 1 < len(steps):
                rec_scores(i + 1)

            def av(pe, e=e, kc=kc):
                ins = None
                for c in range(2):
                    ins = pe.matmul(self.bank(4 + c), lhsT=V.t[:, kc, :], rhs=e.t[:, c * 512:(c + 1) * 512], start=(kc == 0), stop=(kc == 15))
                for c in range(2):
                    ins = pe.matmul(self.ps[c * 32:(c + 1) * 32, 6 * 512:7 * 512], lhsT=self.onesb.t[:, 0:32], rhs=e.t[:, c * 512:(c + 1) * 512], start=(kc == 0), stop=(kc == 15))
                return ins
            P.add("pe", av, reads=e.r() + V.r(kc * 128, kc * 128 + 128) + self.onesb.r(), writes=PSR(4, 5, 6))
            if kc != 15:
                continue
            P.add("dve", (lambda v: v.tensor_copy(out=ssb.t[0:64, 0:512], in_=self.ps[0:64, 6 * 512:7 * 512])), reads=PSR(6), writes=ssb.r())
            P.add("dve", (lambda v: v.tensor_copy(out=a1.t[:, 0:512], in_=self.bank(4))), reads=PSR(4), writes=a1.r())
            P.add("dve", (lambda v: v.tensor_copy(out=a2.t[:, 0:512], in_=self.bank(5))), reads=PSR(5), writes=a2.r())
            P.add("dve", (lambda v: v.reciprocal(out=ssb.t[0:64, 0:512], in_=ssb.t[0:64, 0:512])), reads=ssb.r(), writes=ssb.r())
            for c, rc in ((0, r1), (1, r2)):
                row = (qt % 2) * 2 + c
                P.add("sp", (lambda e_, c=c, row=row: e_.dma_start(out=dr["bscr"][row:row + 1, :], in_=ssb.t[c * 32:c * 32 + 1, 0:512])),
                      reads=ssb.r(), writes=[("dram", row)], dma="bw%d" % c)
                P.add("sp", (lambda e_, rc=rc, row=row: e_.dma_start(out=rc.t[:, 0:512], in_=dr["bscr"][row].partition_broadcast(128))),
                      reads=[("dram", row)], writes=rc.r(), dma="bc%d" % c)
            P.add("pool", (lambda v: v.tensor_tensor(out=a1.t[:, 0:512], in0=a1.t[:, 0:512], in1=r1.t[:, 0:512], op=ALU.mult)), reads=a1.r() + r1.r(), writes=a1.r())
            P.add("pool", (lambda v: v.tensor_tensor(out=a2.t[:, 0:512], in0=a2.t[:, 0:512], in1=r2.t[:, 0:512], op=ALU.mult)), reads=a2.r() + r2.r(), writes=a2.r())
            P.add("dve", (lambda g, q0=q0: g.scalar_tensor_tensor(out=oh.t[:, q0:q0 + 512], in0=a2.t[:, 0:512], scalar=sm.t[:, 5:6], in1=a1.t[:, 0:512], op0=ALU.mult, op1=ALU.add)),
                  reads=a1.r() + a2.r() + smr, writes=oh.r(q0, q0 + 512))
            P.add("pool", (lambda g, q0=q0: g.tensor_tensor(out=osq.t[:, q0:q0 + 512], in0=oh.t[:, q0:q0 + 512], in1=oh.t[:, q0:q0 + 512], op=ALU.mult)),
                  reads=oh.r(q0, q0 + 512), writes=osq.r(q0, q0 + 512))
        estep += len(steps)
        def subln(h=h):
            for qt in range(NTB):
                q0 = qt * 512
                sb_ = qt % 4
                P.add("pe", (lambda pe, q0=q0, sb_=sb_: pe.matmul(self.bank(sb_), lhsT=self.onesb.t[:], rhs=osq.t[:, q0:q0 + 512], start=True, stop=True)),
                      reads=osq.r(q0, q0 + 512) + self.onesb.r(), writes=PSR(sb_))
                rsb = sl1[qt % 2]
                rrb = sl2[qt % 2]
                self.rstd_from_sums(sb_, 512, (rsb, 0), (rrb, 0), 128.0, 1e-5)
                P.add("dve", (lambda v, q0=q0, rrb=rrb, h=h: v.scalar_tensor_tensor(out=oT.t[:, h, q0:q0 + 512], in0=oh.t[:, q0:q0 + 512], scalar=gsub.t[:, 0:1], in1=rrb.t[:, 0:512], op0=ALU.mult, op1=ALU.mult)),
                      reads=oh.r(q0, q0 + 512) + rrb.r() + gsub.r(), writes=oT.r(h * S + q0, h * S + q0 + 512))
        subln()

    for half in range(2):
        self.out_proj_postnorm(n + "w_o", None, n + "g_mix_post", oT, S, half, mT, sq, tmp, rstd2, tscr)


Builder.attn = _attn
```
